# Optimizing a Trainium2 kernel written in Bass

```python
import math
import jax, jax.numpy as jnp
from jax import lax
import numpy as np

D_MODEL = 1024
BATCH = 8
SEQ = 8192
DEPTH = 1
DEC_BATCH = 16
DEC_SEQ = 32
PAST_LEN = 4096

CHUNK = 64
N_META = 16
CONV_DIM = D_MODEL // 2
CONV_W = 3
N_HEADS = 8
N_KV_HEADS = 2
HEAD_DIM = 64
GROUP = N_HEADS // N_KV_HEADS
Q_DIM = N_HEADS * HEAD_DIM
KV_DIM = N_KV_HEADS * HEAD_DIM
MIX_DIM = CONV_DIM + Q_DIM
IN_DIM = 3 * CONV_DIM + Q_DIM + 2 * KV_DIM
WINDOW = 128
WIN_CHUNKS = WINDOW // CHUNK
N_BUCKETS = 32
MAX_DISTANCE = 128
D_FF = 4 * D_MODEL
EPS = 1e-6
SPLITS = [CONV_DIM, 2 * CONV_DIM, 3 * CONV_DIM, 3 * CONV_DIM + Q_DIM, 3 * CONV_DIM + Q_DIM + KV_DIM]

kernel_name = "hybrid_conv_swa_sink_stream_step"


def rms_norm(x, g):
    xf = x.astype(jnp.float32)
    y = xf * lax.rsqrt(jnp.mean(xf * xf, axis=-1, keepdims=True) + EPS)
    return (y * g.astype(jnp.float32)).astype(x.dtype)


def t5_bucket(rp):
    nb = N_BUCKETS // 2
    max_exact = nb // 2
    ret = jnp.where(rp > 0, nb, 0)
    n = jnp.abs(rp)
    nf = jnp.maximum(n, 1).astype(jnp.float32)
    large = max_exact + (jnp.log(nf / max_exact) / math.log(MAX_DISTANCE / max_exact) * (nb - max_exact)).astype(jnp.int32)
    large = jnp.minimum(large, nb - 1)
    return ret + jnp.where(n < max_exact, n, large)


def rel_bias(table, q_pos, k_pos):
    b = table[t5_bucket(k_pos - q_pos)]
    b = jnp.moveaxis(b, -1, -3)
    return b.reshape(b.shape[:-3] + (N_KV_HEADS, GROUP) + b.shape[-2:]).astype(jnp.float32)


def sink_attention(q, k, v, bias, sinks, mask=None):
    s = jnp.einsum("...qhgd,...khd->...hgqk", q.astype(jnp.float32), k.astype(jnp.float32)) * (HEAD_DIM ** -0.5) + bias
    if mask is not None:
        s = jnp.where(mask, s, -jnp.inf)
    sink = sinks.astype(jnp.float32).reshape(N_KV_HEADS, GROUP, 1, 1)
    m = jnp.maximum(jnp.max(s, axis=-1, keepdims=True), sink)
    p = jnp.exp(s - m)
    denom = jnp.sum(p, axis=-1, keepdims=True) + jnp.exp(sink - m)
    o = jnp.einsum("...hgqk,...khd->...qhgd", p / denom, v.astype(jnp.float32))
    return o.astype(q.dtype)


def causal_conv(up, w, length):
    y = up[:, 0:length] * w[0]
    for j in range(1, CONV_W):
        y = y + up[:, j:j + length] * w[j]
    return y


def in_projection(xn, w_in):
    return jnp.split(xn @ w_in, SPLITS, axis=-1)


def prompt_mixer(xn, w_in, conv_w, sinks, table):
    bsz, L, _ = xn.shape
    b, c, u, q, k, v = in_projection(xn, w_in)
    uc = c * u
    up = jnp.pad(uc, ((0, 0), (CONV_W - 1, 0), (0, 0)))
    y_conv = b * causal_conv(up, conv_w, L)
    conv_state = uc[:, L - (CONV_W - 1):]
    q = q.reshape(bsz, L, N_KV_HEADS, GROUP, HEAD_DIM)
    k = k.reshape(bsz, L, N_KV_HEADS, HEAD_DIM)
    v = v.reshape(bsz, L, N_KV_HEADS, HEAD_DIM)
    km, vm = k[:, :N_META], v[:, :N_META]
    pm = jnp.arange(N_META, dtype=jnp.int32)
    o_meta = sink_attention(q[:, :N_META], km, vm, rel_bias(table, pm[:, None], pm[None, :]), sinks)
    S = L - N_META
    nc = S // CHUNK
    qf = q[:, N_META:].reshape(bsz, nc, CHUNK, N_KV_HEADS, GROUP, HEAD_DIM)

    def band(t):
        tp = jnp.pad(t, ((0, 0), (WIN_CHUNKS * CHUNK, 0), (0, 0), (0, 0)))
        tp = tp.reshape(bsz, nc + WIN_CHUNKS, CHUNK, N_KV_HEADS, HEAD_DIM)
        return jnp.concatenate([tp[:, j:j + nc] for j in range(WIN_CHUNKS + 1)], axis=2)

    def with_meta(tm, t):
        tmb = jnp.broadcast_to(tm[:, None], (bsz, nc) + tm.shape[1:])
        return jnp.concatenate([tmb, band(t)], axis=2)

    kf = with_meta(km, k[:, N_META:])
    vf = with_meta(vm, v[:, N_META:])
    ci = jnp.arange(nc, dtype=jnp.int32)[:, None, None]
    qi = jnp.arange(CHUNK, dtype=jnp.int32)[None, :, None]
    r = jnp.arange((WIN_CHUNKS + 1) * CHUNK, dtype=jnp.int32)[None, None, :]
    frame_k = ci * CHUNK - WIN_CHUNKS * CHUNK + r
    q_pos = N_META + ci * CHUNK + qi
    k_pos = jnp.concatenate([jnp.broadcast_to(pm[None, None, :], (nc, 1, N_META)), N_META + frame_k], axis=-1)
    valid = jnp.concatenate([jnp.ones((nc, 1, N_META), dtype=bool), frame_k >= 0], axis=-1)
    bias = rel_bias(table, q_pos, k_pos)
    o_f = sink_attention(qf, kf, vf, bias, sinks, valid[:, None, None])
    y_attn = jnp.concatenate([o_meta.reshape(bsz, N_META, Q_DIM), o_f.reshape(bsz, S, Q_DIM)], axis=1)
    n_keep = min(WINDOW, S)
    return y_conv, y_attn, k[:, L - n_keep:], v[:, L - n_keep:], km, vm, conv_state


def sample_mixer(xn, w_in, conv_w, sinks, table, ck, cv, cmk, cmv, conv_state):
    bsz, S, _ = xn.shape
    b, c, u, q, k, v = in_projection(xn, w_in)
    uc = c * u
    up = jnp.concatenate([conv_state.astype(uc.dtype), uc], axis=1)
    y_conv = b * causal_conv(up, conv_w, S)
    new_conv = up[:, S:]
    q = q.reshape(bsz, S, N_KV_HEADS, GROUP, HEAD_DIM)
    k = k.reshape(bsz, S, N_KV_HEADS, HEAD_DIM)
    v = v.reshape(bsz, S, N_KV_HEADS, HEAD_DIM)
    n_win = ck.shape[1]
    k_all = jnp.concatenate([cmk.astype(k.dtype), ck.astype(k.dtype), k], axis=1)
    v_all = jnp.concatenate([cmv.astype(v.dtype), cv.astype(v.dtype), v], axis=1)
    q_pos = N_META + PAST_LEN + jnp.arange(S, dtype=jnp.int32)[:, None]
    k_pos = jnp.concatenate([
        jnp.arange(N_META, dtype=jnp.int32),
        N_META + PAST_LEN - n_win + jnp.arange(n_win, dtype=jnp.int32),
        N_META + PAST_LEN + jnp.arange(S, dtype=jnp.int32)])[None, :]
    o = sink_attention(q, k_all, v_all, rel_bias(table, q_pos, k_pos), sinks)
    return y_conv, o.reshape(bsz, S, Q_DIM), k, v, new_conv


def merge(y_conv, y_attn, g_conv, g_attn, w_out):
    return jnp.concatenate([rms_norm(y_conv, g_conv), rms_norm(y_attn, g_attn)], axis=-1) @ w_out


def sq_relu_mlp(x, w_up, w_down):
    h = jax.nn.relu(x @ w_up)
    return (h * h) @ w_down


def setup_inputs(seed: int = 0) -> dict:
    key = jax.random.key(seed)
    ks = jax.random.split(key, 20)
    n_win = min(WINDOW, PAST_LEN)
    f32 = jnp.float32
    nrm = lambda k, s, sc: jax.random.normal(k, s, f32) * sc
    gain = lambda k, s: 1.0 + 0.02 * jax.random.normal(k, s, f32)
    return {
        "x_prompt": nrm(ks[0], (BATCH, SEQ, D_MODEL), 1.0),
        "x_sample": nrm(ks[1], (DEC_BATCH, DEC_SEQ, D_MODEL), 1.0),
        "cache_k": nrm(ks[2], (DEPTH, DEC_BATCH, n_win, N_KV_HEADS, HEAD_DIM), 1.0),
        "cache_v": nrm(ks[3], (DEPTH, DEC_BATCH, n_win, N_KV_HEADS, HEAD_DIM), 1.0),
        "cache_meta_k": nrm(ks[4], (DEPTH, DEC_BATCH, N_META, N_KV_HEADS, HEAD_DIM), 1.0),
        "cache_meta_v": nrm(ks[5], (DEPTH, DEC_BATCH, N_META, N_KV_HEADS, HEAD_DIM), 1.0),
        "state_conv": nrm(ks[6], (DEPTH, DEC_BATCH, CONV_W - 1, CONV_DIM), 1.0),
        "meta_tokens": nrm(ks[7], (N_META, D_MODEL), 1.0),
        "norm_mix": gain(ks[8], (DEPTH, D_MODEL)),
        "w_in": nrm(ks[9], (DEPTH, D_MODEL, IN_DIM), D_MODEL ** -0.5),
        "conv_w": nrm(ks[10], (DEPTH, CONV_W, CONV_DIM), CONV_W ** -0.5),
        "attn_sinks": nrm(ks[11], (DEPTH, N_HEADS), 0.5),
        "rel_bias_table": nrm(ks[12], (N_BUCKETS, N_HEADS), 0.2),
        "norm_conv_out": gain(ks[13], (DEPTH, CONV_DIM)),
        "norm_attn_out": gain(ks[14], (DEPTH, Q_DIM)),
        "w_out": nrm(ks[15], (DEPTH, MIX_DIM, D_MODEL), MIX_DIM ** -0.5),
        "norm_mlp": gain(ks[16], (DEPTH, D_MODEL)),
        "w_up": nrm(ks[17], (DEPTH, D_MODEL, D_FF), D_MODEL ** -0.5),
        "w_down": nrm(ks[18], (DEPTH, D_FF, D_MODEL), D_FF ** -0.5),
        "norm_final": gain(ks[19], (D_MODEL,)),
    }


def reference(x_prompt, x_sample, cache_k, cache_v, cache_meta_k, cache_meta_v, state_conv, meta_tokens,
              norm_mix, w_in, conv_w, attn_sinks, rel_bias_table, norm_conv_out, norm_attn_out, w_out,
              norm_mlp, w_up, w_down, norm_final):
    bsz = x_prompt.shape[0]
    hp = jnp.concatenate([jnp.broadcast_to(meta_tokens.astype(x_prompt.dtype)[None], (bsz, N_META, D_MODEL)), x_prompt], axis=1)
    hs = x_sample
    pk, pv, pmk, pmv, pc, sk, sv, sc = [], [], [], [], [], [], [], []
    for l in range(DEPTH):
        yc, ya, kw, vw, mk, mv, cs = prompt_mixer(rms_norm(hp, norm_mix[l]), w_in[l], conv_w[l], attn_sinks[l], rel_bias_table)
        hp = hp + merge(yc, ya, norm_conv_out[l], norm_attn_out[l], w_out[l])
        hp = hp + sq_relu_mlp(rms_norm(hp, norm_mlp[l]), w_up[l], w_down[l])
        pk.append(kw); pv.append(vw); pmk.append(mk); pmv.append(mv); pc.append(cs)
        yc, ya, kn, vn, cn = sample_mixer(rms_norm(hs, norm_mix[l]), w_in[l], conv_w[l], attn_sinks[l], rel_bias_table,
                                          cache_k[l], cache_v[l], cache_meta_k[l], cache_meta_v[l], state_conv[l])
        hs = hs + merge(yc, ya, norm_conv_out[l], norm_attn_out[l], w_out[l])
        hs = hs + sq_relu_mlp(rms_norm(hs, norm_mlp[l]), w_up[l], w_down[l])
        sk.append(kn); sv.append(vn); sc.append(cn)
    y_prompt = rms_norm(hp, norm_final)[:, N_META:]
    y_sample = rms_norm(hs, norm_final)
    return (y_prompt, y_sample, jnp.stack(pk), jnp.stack(pv), jnp.stack(pmk), jnp.stack(pmv), jnp.stack(pc),
            jnp.stack(sk), jnp.stack(sv), jnp.stack(sc))
```

```python
import contextlib
import math
import types
import numpy as np
import concourse.bass as bass
import concourse.mybir as mybir
from concourse.bass_utils import run_bass_kernel_spmd

F32 = mybir.dt.float32
BF16 = mybir.dt.bfloat16
AF = mybir.ActivationFunctionType
ALU = mybir.AluOpType

D = 1024
SEQ = 8192
N_META = 16
CONV_DIM = 512
Q_DIM = 512
IN_DIM = 2304
IN_X = 2560
D_FF = 4096
EPS = 1e-6
MT = 256
NEG = -30000.0
PAST_LEN = 4096
DBG_STOP = False

HSEG = [("prev", 128, 128, -128), ("cur", 128, 128, 0), ("meta0", 16, 128, -16),
        ("sw", 128, 32, -128), ("sn", 32, 32, 0)]
HOFF = {}
_o = 0
for _n, _nk, _T, _db in HSEG:
    HOFF[_n] = (_o, _nk, _T, _db)
    _o += _nk + _T - 1
HLEN = _o


def _t5_bucket_np(rp):
    rp = np.asarray(rp, dtype=np.int32)
    nb = 16
    max_exact = 8
    ret = np.where(rp > 0, nb, 0)
    n = np.abs(rp)
    nf = np.maximum(n, 1).astype(np.float32)
    large = max_exact + (np.log(nf / np.float32(max_exact)) / np.float32(math.log(128 / max_exact))
                         * np.float32(nb - max_exact)).astype(np.int32)
    large = np.minimum(large, nb - 1)
    return ret + np.where(n < max_exact, n, large)


def _bucket(rp):
    try:
        import jax
        import jax.numpy as jnp
        cpu = jax.devices("cpu")[0]
        with jax.default_device(cpu):
            rp = jnp.asarray(np.asarray(rp, dtype=np.int32))
            nb = 16
            max_exact = 8
            ret = jnp.where(rp > 0, nb, 0)
            n = jnp.abs(rp)
            nf = jnp.maximum(n, 1).astype(jnp.float32)
            large = max_exact + (jnp.log(nf / max_exact) / math.log(128 / max_exact) * (nb - max_exact)).astype(jnp.int32)
            large = jnp.minimum(large, nb - 1)
            return np.asarray(ret + jnp.where(n < max_exact, n, large))
    except Exception:
        return _t5_bucket_np(rp)


def _onehot_const():
    oh = np.zeros((32, HLEN), np.float32)
    for name, (off, nk, T, db) in HOFF.items():
        j = np.arange(nk + T - 1)
        d = db + (nk - 1) - j
        b = _bucket(d)
        oh[b, off + j] = 1.0
    return oh


def _freeze(fn):
    if fn.__closure__ is None:
        return fn
    cells = tuple(types.CellType(c.cell_contents) for c in fn.__closure__)
    return types.FunctionType(fn.__code__, fn.__globals__, fn.__name__, fn.__defaults__, cells)


class Res:
    __slots__ = ("name", "lw", "rd")

    def __init__(self, name):
        self.name = name
        self.lw = None
        self.rd = []


class Op:
    __slots__ = ("eng", "fn", "deps", "dma", "chan", "sig", "hasdep", "idx")


class Prog:
    ENGS = ("pe", "act", "dve", "pool", "sp")

    def __init__(self):
        self.ops = []
        self.chan_cnt = {}
        self.eng_cnt = {e: 0 for e in self.ENGS}
        self.final = []

    def add(self, eng, fn, reads=(), writes=(), dma=False, chan=None, final=False):
        op = Op()
        op.eng = eng
        op.fn = _freeze(fn)
        op.dma = dma
        op.hasdep = False
        op.sig = None
        op.idx = len(self.ops)
        deps = []
        for r in reads:
            if r.lw is not None:
                deps.append(r.lw)
        for w in writes:
            if w.lw is not None:
                deps.append(w.lw)
            deps.extend(w.rd)
        seen = set()
        od = []
        for d in deps:
            if d is op or id(d) in seen:
                continue
            seen.add(id(d))
            if eng == "pe" and d.eng == "pe" and not d.dma and not dma:
                continue
            d.hasdep = True
            od.append(d)
        op.deps = od
        for r in reads:
            r.rd.append(op)
        for w in writes:
            w.lw = op
            w.rd = []
        if dma:
            op.chan = chan if chan is not None else (writes[0] if writes else ("dma", eng))
        else:
            op.chan = None
        if dma:
            op.hasdep = True
        if final:
            op.hasdep = True
            self.final.append(op)
        self.ops.append(op)
        return op

    def emit(self, nc):
        chan_keys = []
        for op in self.ops:
            if not op.hasdep:
                continue
            if op.dma:
                k = op.chan if isinstance(op.chan, (str, tuple)) else id(op.chan)
                if k not in self.chan_cnt:
                    self.chan_cnt[k] = 0
                    chan_keys.append(k)
                self.chan_cnt[k] += 16
                op.sig = (("c", k), self.chan_cnt[k])
            else:
                self.eng_cnt[op.eng] += 1
                op.sig = (("e", op.eng), self.eng_cnt[op.eng])
        with contextlib.ExitStack() as st:
            sems = {}
            for e in self.ENGS:
                sems[("e", e)] = st.enter_context(nc.semaphore("s_" + e))
            for i, k in enumerate(chan_keys):
                sems[("c", k)] = st.enter_context(nc.semaphore("c_%d" % i))
            self.nsem = len(sems)
            block = st.enter_context(nc.Block())

            def run(engname, e):
                known = {}
                for op in self.ops:
                    if op.eng != engname:
                        continue
                    for d in op.deps:
                        sk, v = d.sig
                        if known.get(sk, 0) < v:
                            e.wait_ge(sems[sk], v)
                            known[sk] = v
                    ins = op.fn(e)
                    if op.sig is not None:
                        sk, v = op.sig
                        ins.then_inc(sems[sk], 16 if op.dma else 1)
                if engname == "sp":
                    fin = {}
                    for op in self.final:
                        sk, v = op.sig
                        fin[sk] = max(fin.get(sk, 0), v)
                    for sk, v in fin.items():
                        if known.get(sk, 0) < v:
                            e.wait_ge(sems[sk], v)
                            known[sk] = v

            @block.tensor
            def _(e):
                run("pe", e)

            @block.scalar
            def _(e):
                run("act", e)

            @block.vector
            def _(e):
                run("dve", e)

            @block.gpsimd
            def _(e):
                run("pool", e)

            @block.sync
            def _(e):
                run("sp", e)


class Rot:
    def __init__(self, alloc, name, n, shape, dt):
        self.t = [alloc("%s%d" % (name, i), shape, dt) for i in range(n)]
        self.r = [Res("%s%d" % (name, i)) for i in range(n)]
        self.i = 0

    def get(self):
        i = self.i
        self.i = (i + 1) % len(self.t)
        return self.t[i], self.r[i]


def build(S_TOK):
    assert S_TOK % MT == 0
    NTP = S_TOK // MT
    nc = bass.Bass("TRN2", target_bir_lowering=False)

    def din(name, shape, dt=F32):
        return nc.dram_tensor(name, list(shape), dt, kind="ExternalInput").ap()

    def dout(name, shape, dt=F32):
        return nc.dram_tensor(name, list(shape), dt, kind="ExternalOutput").ap()

    xp_d = din("xp", [S_TOK, D])
    xs_d = din("xs", [64, D])
    ck_d = din("ck", [2, 128, 128])
    cv_d = din("cv", [2, 128, 128])
    cmk_d = din("cmk", [2, 16, 128])
    cmv_d = din("cmv", [2, 16, 128])
    sconv_d = din("sconv", [128, 4, 2, 2])
    meta_d = din("meta", [16, D])
    gmix_d = din("gmix", [1, D])
    gmlp_d = din("gmlp", [1, D])
    gcv_d = din("gcv", [128, 4])
    gatt_d = din("gatt", [1, Q_DIM])
    gfin_d = din("gfin", [1, D])
    convw_d = din("convw", [128, 4, 3])
    sinks_d = din("sinks", [1, 8])
    table_d = din("table", [32, 8])
    oh_d = din("oh", [32, HLEN])
    win_d = din("win", [D, IN_X])
    wout_d = din("wout", [D, D])
    wup_d = din("wup", [D, D_FF])
    wdn_d = din("wdn", [D_FF, D])

    yp_d = dout("yp", [S_TOK, D])
    ys_d = dout("ys", [64, D])
    pk_d = dout("pk", [128, 128])
    pv_d = dout("pv", [128, 128])
    pmk_d = dout("pmk", [16, 128])
    pmv_d = dout("pmv", [16, 128])
    pconv_d = dout("pconv", [128, 4, 2])
    sk_d = dout("sk", [64, 128])
    sv_d = dout("sv", [64, 128])
    sconvo_d = dout("sconvo", [128, 4, 2, 2])

    gscr = nc.dram_tensor("gscr", [8, HLEN], BF16, kind="Internal").ap()
    wup_s = nc.dram_tensor("wup_s", [8, 128, 8, 512], BF16, kind="Internal").ap()
    wdn_s = nc.dram_tensor("wdn_s", [8, 128, 4, 1024], BF16, kind="Internal").ap()
    r_gscr = Res("gscr")
    r_wup_s = [Res("wup_s%d" % i) for i in range(8)]
    r_wdn_s = [Res("wdn_s%d" % i) for i in range(8)]

    P = Prog()
    with contextlib.ExitStack() as st:
        def sb(name, shape, dt):
            return st.enter_context(nc.sbuf_tensor("s_" + name, list(shape), dt))

        def ps(name, shape, dt):
            return st.enter_context(nc.psum_tensor("p_" + name, list(shape), dt))

        bank = [ps("bank%d" % i, [128, 512], F32) for i in range(8) if i != 2]
        bank.insert(2, None)
        TRb = ps("TRb", [128, 1024], BF16)
        r_bank = [Res("bank%d" % i) for i in range(8)]
        r_TR = r_bank[2]
        ssqc_ps = bank[3][:, 0:4]
        r_ssqc_ps = r_bank[3]
        gv_ps = bank[4]
        fmb_i = [0]

        def fmb_get():
            i = fmb_i[0]
            fmb_i[0] = 1 - i
            return bank[i], r_bank[i]

        ident = sb("ident", [128, 128], BF16)
        anti128 = sb("anti128", [128, 128], BF16)
        anti32 = sb("anti32", [32, 32], BF16)
        anti17 = sb("anti17", [17, 17], BF16)
        ones_col = sb("ones_col", [128, 1], BF16)
        iot = sb("iot", [128, 128], F32)
        r_const = Res("const")
        r_iot = Res("iot")
        P.add("pool", lambda e: e.iota(iot[:], pattern=[[1, 128]], base=0, channel_multiplier=-1,
                                       allow_small_or_imprecise_dtypes=True), writes=[r_iot])
        P.add("dve", lambda e: e.tensor_scalar(out=ident[:], in0=iot[:], scalar1=0.0, scalar2=None,
                                               op0=ALU.is_equal), reads=[r_iot], writes=[r_const])
        iot2 = sb("iot2", [128, 128], F32)
        r_iot2 = Res("iot2")
        P.add("pool", lambda e: e.iota(iot2[:], pattern=[[1, 128]], base=0, channel_multiplier=1,
                                       allow_small_or_imprecise_dtypes=True), writes=[r_iot2])
        P.add("dve", lambda e: e.tensor_scalar(out=anti128[:], in0=iot2[:], scalar1=127.0, scalar2=None,
                                               op0=ALU.is_equal), reads=[r_iot2], writes=[r_const])
        P.add("dve", lambda e: e.tensor_scalar(out=anti32[:], in0=iot2[0:32, 0:32], scalar1=31.0, scalar2=None,
                                               op0=ALU.is_equal), reads=[r_iot2], writes=[r_const])
        P.add("dve", lambda e: e.tensor_scalar(out=anti17[:], in0=iot2[0:17, 0:17], scalar1=15.0, scalar2=None,
                                               op0=ALU.is_equal), reads=[r_iot2], writes=[r_const])
        P.add("dve", lambda e: e.memset(ones_col[:], 1.0), writes=[r_const])

        gmix = sb("gmix", [128, D], F32)
        gmlp = sb("gmlp", [128, D], F32)
        gatt = sb("gatt", [128, Q_DIM], F32)
        gcv = sb("gcv", [128, 4], F32)
        convw = sb("convw", [128, 4, 3], F32)
        gfin = sb("gfin", [128, D], F32)
        table = sb("table", [32, 8], F32)
        ohs = sb("ohs", [32, HLEN], F32)
        CBm0 = sb("CBm0", [17, 8], F32)
        CBm = sb("CBm", [17, 8], F32)
        r_small = Res("small")
        r_cb = Res("cb")
        for tl, src in ((gmix, gmix_d.partition_broadcast(128)), (gmlp, gmlp_d.partition_broadcast(128)),
                        (gatt, gatt_d.partition_broadcast(128)), (gcv, gcv_d), (convw, convw_d), (table, table_d),
                        (ohs, oh_d)):
            P.add("sp", lambda e, tl=tl, src=src: e.dma_start(out=tl[:], in_=src), writes=[r_small], dma=True,
                  chan="small")
        P.add("sp", lambda e: e.dma_start(out=gfin[:], in_=gfin_d.partition_broadcast(128)), writes=[r_small],
              dma=True, chan="small")
        P.add("dve", lambda e: e.memset(CBm0[:], 0.0), writes=[r_cb])
        P.add("sp", lambda e: e.dma_start(out=CBm0[16:17, :], in_=sinks_d), writes=[r_cb], dma=True, chan="cb")
        P.add("sp", lambda e: e.dma_start(out=CBm[16:17, :], in_=sinks_d), writes=[r_cb], dma=True, chan="cb")
        P.add("sp", lambda e: e.dma_start(out=CBm[0:16, :], in_=table_d[15:16, :].partition_broadcast(16)),
              writes=[r_cb], dma=True, chan="cb")

        gvb = sb("gvb", [8, HLEN], BF16)
        r_gvb = Res("gvb")
        Hprev = sb("Hprev", [128, 8, 128], BF16)
        Hcur = sb("Hcur", [128, 8, 128], BF16)
        Hm0 = sb("Hm0", [17, 8, 128], BF16)
        Hsw = sb("Hsw", [128, 8, 32], BF16)
        Hsn = sb("Hsn", [32, 8, 32], BF16)
        r_H = Res("H")
        for c0 in range(0, HLEN, 512):
            c1 = min(HLEN, c0 + 512)
            P.add("pe", lambda e, c0=c0, c1=c1: e.matmul(gv_ps[0:8, 0:c1 - c0], lhsT=table[:], rhs=ohs[:, c0:c1],
                                                         start=True, stop=True),
                  reads=[r_small], writes=[r_bank[4]])
            P.add("dve", lambda e, c0=c0, c1=c1: e.tensor_copy(out=gvb[:, c0:c1], in_=gv_ps[0:8, 0:c1 - c0]),
                  writes=[r_bank[4], r_gvb])
        P.add("sp", lambda e: e.dma_start(out=gscr, in_=gvb[:]), reads=[r_gvb], writes=[r_gscr], dma=True)
        P.add("dve", lambda e: e.memset(Hm0[:], 0.0), writes=[r_H])
        for tl, name in ((Hprev, "prev"), (Hcur, "cur"), (Hm0, "meta0"), (Hsw, "sw"), (Hsn, "sn")):
            off, nk, T, db = HOFF[name]
            src = bass.AP(gscr.tensor, off, [[1, nk], [HLEN, 8], [1, T]])
            P.add("sp", lambda e, tl=tl, src=src, nk=nk: e.dma_start(out=tl[0:nk, :, :], in_=src),
                  reads=[r_gscr], writes=[r_H], dma=True, chan="H")
        P.add("dve", lambda e: e.memset(Hprev[64:128, :, 64:128], NEG), writes=[r_H])
        P.add("dve", lambda e: e.memset(Hcur[0:64, :, 0:64], NEG), writes=[r_H])

        WIN = sb("WIN", [128, 8, IN_X], BF16)
        WOUT = sb("WOUT", [128, 8, D], BF16)
        r_WIN = [[Res("WIN%d_%d" % (k, c)) for c in range(2)] for k in range(8)]
        r_WOUT = [Res("WOUT%d" % c) for c in range(2)]

        def prep_weights():
            for k in range(8):
                for c in range(2):
                    P.add("pool", lambda e: e.dma_start(out=WIN[:, k, c * 1280:(c + 1) * 1280],
                                                        in_=win_d[k * 128:(k + 1) * 128, c * 1280:(c + 1) * 1280]),
                          writes=[r_WIN[k][c]], dma=True)
            for c in range(2):
                src = wout_d[:, c * 512:(c + 1) * 512].rearrange("(k p) c -> p k c", p=128)
                P.add("pool", lambda e: e.dma_start(out=WOUT[:, :, c * 512:(c + 1) * 512], in_=src),
                      writes=[r_WOUT[c]], dma=True)
            for i in range(8):
                src = wup_d[:, i * 512:(i + 1) * 512].rearrange("(k p) c -> p k c", p=128)
                P.add("pool", lambda e: e.dma_start(out=wup_s[i], in_=src), writes=[r_wup_s[i]], dma=True)
            for i in range(8):
                src = wdn_d[i * 512:(i + 1) * 512, :].rearrange("(k p) c -> p k c", p=128)
                P.add("pool", lambda e: e.dma_start(out=wdn_s[i], in_=src), writes=[r_wdn_s[i]], dma=True)


        xh = [[sb("xh%d%d" % (p, s), [128, D], F32) for s in range(2)] for p in range(2)]
        r_xh = [[Res("xh%d%d" % (p, s)) for s in range(2)] for p in range(2)]
        xsb = Rot(sb, "xsb", 2, [128, D], BF16)
        junk = Rot(sb, "junk", 1, [128, D], BF16)
        _xsT = sb("xsT", [128, 8, MT], BF16)
        xsT = [_xsT, _xsT]
        _r_xsT = [Res("xsT%d" % s) for s in range(2)]
        r_xsT = [_r_xsT, _r_xsT]
        h1nT = sb("h1nT", [128, 8, MT], BF16)
        r_h1nT = [Res("h1nT%d" % s) for s in range(2)]
        qT = sb("qT", [128, 4, MT], BF16)
        r_qT = [Res("qT%d" % j) for j in range(4)]
        NRING = 4
        KKr = sb("KKr", [128, NRING, 2, 128], BF16)
        r_KKr = [Res("KKr%d" % i) for i in range(NRING)]
        Vr = sb("Vr", [128, NRING, 2, 65], BF16)
        r_Vr = [Res("Vr%d" % i) for i in range(NRING)]
        KKm = sb("KKm", [128, 2, 17], BF16)
        Vm = sb("Vm", [17, 2, 65], BF16)
        r_KKm = Res("KKm")
        r_Vm = Res("Vm")
        KKsm = sb("KKsm", [128, 2, 2, 17], BF16)
        Vsm = sb("Vsm", [17, 2, 2, 65], BF16)
        KKsw = sb("KKsw", [128, 2, 2, 128], BF16)
        Vsw = sb("Vsw", [128, 2, 2, 65], BF16)
        KKsn = sb("KKsn", [128, 2, 64], BF16)
        Vsn = sb("Vsn", [32, 2, 2, 65], BF16)
        r_scache = Res("scache")
        r_KKsn = Res("KKsn")
        r_Vsn = [Res("Vsn0"), Res("Vsn1")]
        ucx = [sb("ucx%d" % p, [128, 4, MT + 4], F32) for p in range(2)]
        r_ucx = [[Res("ucx%d%d" % (p, j)) for j in range(4)] for p in range(2)]
        r_ucctx = [Res("ucctx%d" % p) for p in range(2)]
        csb = Rot(sb, "csb", 2, [128, MT], F32)
        tmpy = Rot(sb, "tmpy", 2, [128, MT], F32)
        bsb = Rot(sb, "bsb", 2, [128, MT], F32)
        ycsq = sb("ycsq", [128, 4, MT], BF16)
        r_ycsq = [Res("ycsq%d" % j) for j in range(4)]
        ycT = sb("ycT", [128, 4, MT], BF16)
        r_ycT = [Res("ycT%d" % j) for j in range(4)]
        yaT = sb("yaT", [128, 4, MT], BF16)
        r_yaT = [Res("yaT%d" % s) for s in range(2)]
        yaraw = Rot(sb, "yaraw", 2, [128, 512], F32)
        yas = Rot(sb, "yas", 2, [128, 512], BF16)
        stat = Rot(sb, "stat", 48, [128, 1], F32)
        rec8 = Rot(sb, "rec8", 2, [128, 8], F32)
        hidT = sb("hidT", [128, 32, MT], BF16)
        r_hidT = [Res("hidT%d" % f) for f in range(32)]
        rtmp = Rot(sb, "rtmp", 2, [128, 2 * MT], F32)
        ring = Rot(sb, "ring", 4, [128, 4096], BF16)
        kvst = Rot(sb, "kvst", 1, [128, 256], F32)
        kdup = sb("kdup", [128, 2, 2, 2, 64], BF16)
        kmdup = sb("kmdup", [16, 2, 2, 2, 64], BF16)
        cstv = [rtmp.t[q][:].rearrange("p (a c) -> p a c", a=4) for q in range(2)]
        r_cstv = [rtmp.r[q] for q in range(2)]
        r_kdup = Res("kdup")

        P.add("pool", lambda e: e.memset(Vr[:], 1.0), writes=r_Vr)
        P.add("pool", lambda e: e.memset(Vm[:], 1.0), writes=[r_Vm])
        P.add("pool", lambda e: e.memset(Vm[:, :, 0:64], 0.0), writes=[r_Vm])
        P.add("pool", lambda e: e.memset(Vsm[:], 1.0), writes=[r_scache])
        P.add("pool", lambda e: e.memset(Vsm[:, :, :, 0:64], 0.0), writes=[r_scache])
        P.add("pool", lambda e: e.memset(Vsw[:], 1.0), writes=[r_scache])
        P.add("pool", lambda e: e.memset(Vsn[:], 1.0), writes=r_Vsn)
        P.add("pool", lambda e: e.memset(KKm[:], 0.0), writes=[r_KKm])
        P.add("pool", lambda e: e.memset(KKsm[:], 0.0), writes=[r_scache])
        P.add("pool", lambda e: e.memset(KKr[:], 0.0), writes=r_KKr)

        out_ops = []

        def rms_stats(src_ap, n_tok, nfeat, reads):
            jt, jr = junk.get()
            s1, r1 = stat.get()
            P.add("act", lambda e: e.activation(out=jt[0:n_tok, 0:nfeat], in_=src_ap, func=AF.Square,
                                                accum_out=s1[0:n_tok, :]), reads=reads, writes=[jr, r1])
            s2, r2 = stat.get()
            P.add("act", lambda e: e.activation(out=s2[0:n_tok, :], in_=s1[0:n_tok, :], func=AF.Ln,
                                                scale=1.0 / nfeat, bias=EPS), reads=[r1], writes=[r2])
            return s2, r2

        def exp_of(ln_t, ln_r, n_tok, scale):
            s3, r3 = stat.get()
            P.add("act", lambda e: e.activation(out=s3[0:n_tok, :], in_=ln_t[0:n_tok, :], func=AF.Exp, scale=scale),
                  reads=[ln_r], writes=[r3])
            return s3, r3

        def norm_part1(src_t, src_r, n_tok, gain):
            ln_t, ln_r = rms_stats(src_t[0:n_tok, :], n_tok, D, [src_r])
            rs_t, rs_r = exp_of(ln_t, ln_r, n_tok, -0.5)
            xb, xbr = xsb.get()
            P.add("dve", lambda e: e.scalar_tensor_tensor(out=xb[0:n_tok, :], in0=src_t[0:n_tok, :],
                                                          scalar=rs_t[0:n_tok, :], in1=gain[0:n_tok, :],
                                                          op0=ALU.mult, op1=ALU.mult),
                  reads=[src_r, rs_r, r_small], writes=[xbr])
            return xb, xbr

        def norm_part2(xb, xbr, n_tok, dstT, dst_r, col0):
            for k in range(8):
                P.add("pe", lambda e, k=k: e.transpose(out=TRb[:, k * 128:k * 128 + n_tok],
                                                       in_=xb[0:n_tok, k * 128:(k + 1) * 128],
                                                       identity=ident[0:n_tok, 0:n_tok]),
                      reads=[xbr, r_const], writes=[r_TR])
            src = TRb[:].rearrange("p (k t) -> p k t", k=8)[:, :, 0:n_tok]
            P.add("dve", lambda e: e.tensor_copy(out=dstT[:, :, col0:col0 + n_tok], in_=src),
                  writes=[r_TR, dst_r])

        def norm_transpose(src_t, src_r, n_tok, dstT, dst_r, col0, gain):
            xb, xbr = norm_part1(src_t, src_r, n_tok, gain)
            norm_part2(xb, xbr, n_tok, dstT, dst_r, col0)


        def inproj_cols(xT, xT_rs, T, cols):
            bt, br = fmb_get()
            for i, col in enumerate(cols):
                for k in range(8):
                    P.add("pe", lambda e, k=k, i=i, col=col: e.matmul(
                        bt[:, i * 256:i * 256 + T], lhsT=WIN[:, k, col:col + 128], rhs=xT[:, k, 0:T],
                        start=(k == 0), stop=(k == 7)), reads=r_WIN[k] + xT_rs, writes=[br])
            return bt, br

        def kv_tokmajor(xT, xT_r, col0, n_tok, bi):
            for k in range(8):
                P.add("pe", lambda e, k=k: e.matmul(bank[bi][0:n_tok, 0:256], lhsT=xT[:, k, col0:col0 + n_tok],
                                                    rhs=WIN[:, k, 2048:2304], start=(k == 0), stop=(k == 7)),
                      reads=r_WIN[k] + [xT_r], writes=[r_bank[bi]])

        PTh = Rot(sb, "PTh", 3, [128, 3, 128], BF16)

        def attention(T, qcol, groups_fn):
            pend = None
            for h in range(8):
                gl = groups_fn(h)
                hp = h % 2
                sbk = bank[h % 2]
                sbr = r_bank[h % 2]
                for g in gl:
                    nk = g["nk"]
                    sl = g["slot"]
                    P.add("pe", lambda e, g=g, nk=nk, sl=sl, hp=hp, h=h: e.matmul(
                        sbk[0:nk, sl * 128:sl * 128 + T], lhsT=g["kk"],
                        rhs=qT[hp * 64:(hp + 1) * 64, h // 2, qcol:qcol + T], start=True, stop=(g["hank"] is None)),
                        reads=g["kk_rs"] + [r_qT[h // 2]], writes=[sbr])
                    if g["hank"] is not None:
                        P.add("pe", lambda e, g=g, nk=nk, sl=sl: e.matmul(
                            sbk[0:nk, sl * 128:sl * 128 + T], lhsT=g["anti"], rhs=g["hank"], start=False, stop=True),
                            reads=[r_H, r_const], writes=[sbr])
                pt, pr = PTh.get()
                full = [g for g in gl if g["nk"] == 128 and g["cb"] is None]
                rest = [g for g in gl if not (g["nk"] == 128 and g["cb"] is None)]
                if full:
                    s0 = min(g["slot"] for g in full)
                    s1 = max(g["slot"] for g in full) + 1
                    assert s1 - s0 == len(full)
                    if T == 128:
                        P.add("act", lambda e, pt=pt, s0=s0, s1=s1: e.activation(
                            out=pt[:, s0:s1, :], in_=sbk[:, s0 * 128:s1 * 128].rearrange("p (a c) -> p a c", c=128),
                            func=AF.Exp), reads=[], writes=[sbr, pr])
                    else:
                        for g in full:
                            sl = g["slot"]
                            P.add("act", lambda e, pt=pt, sl=sl: e.activation(
                                out=pt[:, sl, 0:T], in_=sbk[:, sl * 128:sl * 128 + T], func=AF.Exp),
                                writes=[sbr, pr])
                for g in rest:
                    nk = g["nk"]
                    sl = g["slot"]
                    if g["cb"] is None:
                        P.add("act", lambda e, pt=pt, sl=sl, nk=nk: e.activation(
                            out=pt[0:nk, sl, 0:T], in_=sbk[0:nk, sl * 128:sl * 128 + T], func=AF.Exp),
                            writes=[sbr, pr])
                    else:
                        P.add("act", lambda e, pt=pt, sl=sl, nk=nk, g=g: e.activation(
                            out=pt[0:nk, sl, 0:T], in_=sbk[0:nk, sl * 128:sl * 128 + T], func=AF.Exp, bias=g["cb"]),
                            reads=[r_cb], writes=[sbr, pr])
                if pend is not None:
                    emit_pv(*pend)
                pend = (h, T, gl, pt, pr)
            emit_pv(*pend)

        def emit_pv(h, T, gl, pt, pr):
            ob = bank[4 + h // 4]
            obr = r_bank[4 + h // 4]
            hh = h % 4
            n = len(gl)
            for i, g in enumerate(gl):
                nk = g["nk"]
                sl = g["slot"]
                P.add("pe", lambda e, g=g, nk=nk, i=i, sl=sl: e.matmul(
                    ob[0:T, hh * 65:(hh + 1) * 65], lhsT=pt[0:nk, sl, 0:T], rhs=g["v"], start=(i == 0), stop=(i == n - 1)),
                    reads=[pr] + g["v_rs"], writes=[obr])


        def attn_E1(T):
            rc, rcr = rec8.get()
            yr, yrr = yaraw.get()
            for b in range(2):
                ob = bank[4 + b]
                o3 = ob[0:T, 0:260].rearrange("p (h c) -> p h c", h=4)
                P.add("dve", lambda e, o3=o3, b=b: e.reciprocal(out=rc[0:T, 4 * b:4 * b + 4], in_=o3[:, :, 64]),
                      writes=[r_bank[4 + b], rcr])
                P.add("dve", lambda e, o3=o3, b=b: e.tensor_tensor(
                    out=yr[0:T, 256 * b:256 * (b + 1)].rearrange("p (h c) -> p h c", h=4),
                    in0=o3[:, :, 0:64],
                    in1=rc[0:T, 4 * b:4 * b + 4].unsqueeze(2).to_broadcast([T, 4, 64]),
                    op=ALU.mult), reads=[rcr], writes=[r_bank[4 + b], yrr])
            return yr, yrr

        def attn_E2a(T, yr, yrr, lnc_t, lnc_r):
            lna_t, lna_r = rms_stats(yr[0:T, :], T, Q_DIM, [yrr])
            d_t, d_r = stat.get()
            P.add("dve", lambda e: e.tensor_tensor(out=d_t[0:T, :], in0=lnc_t[0:T, :], in1=lna_t[0:T, :],
                                                   op=ALU.subtract), reads=[lnc_r, lna_r], writes=[d_r])
            ratio_t, ratio_r = exp_of(d_t, d_r, T, 0.5)
            rstdc_t, rstdc_r = exp_of(lnc_t, lnc_r, T, -0.5)
            ys, ysr = yas.get()
            P.add("dve", lambda e: e.scalar_tensor_tensor(out=ys[0:T, :], in0=yr[0:T, :], scalar=ratio_t[0:T, :],
                                                          in1=gatt[0:T, :], op0=ALU.mult, op1=ALU.mult),
                  reads=[yrr, ratio_r, r_small], writes=[ysr])
            return ys, ysr, rstdc_t, rstdc_r

        def attn_E2b(T, ys, ysr, yaT_col0, r_yaT_s):
            for j in range(4):
                P.add("pe", lambda e, j=j: e.transpose(out=TRb[:, j * 128:j * 128 + T], in_=ys[0:T, j * 128:(j + 1) * 128],
                                                       identity=ident[0:T, 0:T]),
                      reads=[ysr, r_const], writes=[r_TR])
            src = TRb[:, 0:512].rearrange("p (k t) -> p k t", k=4)[:, :, 0:T]
            P.add("dve", lambda e: e.tensor_copy(out=yaT[:, :, yaT_col0:yaT_col0 + T], in_=src),
                  writes=[r_TR, r_yaT_s])


        ring_state = {"pending": [], "left": 16 * (NTP + 1)}

        def mlp_weights_iter():
            while True:
                for i in range(8):
                    yield ("up", i)
                for i in range(8):
                    yield ("dn", i)

        wgen = mlp_weights_iter()

        def issue_wload():
            if ring_state["left"] <= 0:
                return
            ring_state["left"] -= 1
            kind, i = next(wgen)
            t, r = ring.get()
            src = (wup_s if kind == "up" else wdn_s)[i]
            rs = (r_wup_s if kind == "up" else r_wdn_s)[i]
            if kind == "up":
                dstv = t[:].rearrange("p (k c) -> p k c", k=8)
            else:
                dstv = t[:].rearrange("p (k c) -> p k c", k=4)
            P.add("sp", lambda e, dstv=dstv, src=src: e.dma_start(out=dstv, in_=src), reads=[rs], writes=[r], dma=True)
            ring_state["pending"].append((kind, i, t, r))

        def take_wload(kind, i):
            k2, i2, t, r = ring_state["pending"].pop(0)
            assert (k2, i2) == (kind, i)
            return t, r

        class Tile:
            pass

        def stage_A1(tl, s):
            p = tl.par
            c0, n = tl.subs[s]
            src = tl.xsrc(s)
            P.add("sp", lambda e: e.dma_start(out=xh[p][s][0:n, :], in_=src), writes=[r_xh[p][s]], dma=True)
            tl.a1[s] = norm_part1(xh[p][s], r_xh[p][s], n, gmix)

        def stage_A2(tl, s):
            p = tl.par
            c0, n = tl.subs[s]
            xb, xbr = tl.a1[s]
            norm_part2(xb, xbr, n, xsT[p], r_xsT[p][s], c0)

        def stage_A(tl):
            tl.a1 = [None] * len(tl.subs)
            for s in range(len(tl.subs)):
                stage_A1(tl, s)
                stage_A2(tl, s)


        def stage_B(tl):
            for _ in stage_B_gen(tl):
                pass

        def stage_B_gen(tl):
            if tl.kind == "sample":
                sample_ctx_load()
            p = tl.par
            T = tl.T
            xT = xsT[p]
            xT_rs = [r_xsT[p][s] for s in range(len(tl.subs))]
            bt, br = inproj_cols(xT, xT_rs, T, [2304, 2432])
            if tl.kind == "prompt":
                s0 = (2 * tl.t) % NRING
                for g in range(2):
                    P.add("dve", lambda e, bt=bt, g=g, s0=s0: e.tensor_copy(
                        out=KKr[:, s0:s0 + 2, g, :], in_=bt[:, g * 256:(g + 1) * 256].rearrange("p (a c) -> p a c", a=2)),
                        writes=[br, r_KKr[s0], r_KKr[s0 + 1]])
            elif tl.kind == "meta":
                for g in range(2):
                    P.add("dve", lambda e, bt=bt, g=g: e.tensor_copy(out=KKm[:, g, 0:16], in_=bt[:, g * 256:g * 256 + 16]),
                          writes=[br, r_KKm])
            else:
                for g in range(2):
                    P.add("dve", lambda e, bt=bt, g=g: e.tensor_copy(out=KKsn[:, g, :], in_=bt[:, g * 256:g * 256 + 64]),
                          writes=[br, r_KKsn])
            yield
            for s, (c0, n) in enumerate(tl.subs):
                if s > 0:
                    yield
                bi = 6 + s
                kv_tokmajor(xT, r_xsT[p][s], c0, n, bi)
                src_v = bank[bi][0:n, 128:256].rearrange("p (g c) -> p g c", g=2)
                if tl.kind == "prompt":
                    sl = (2 * tl.t + s) % NRING
                    P.add("act", lambda e, sl=sl, src_v=src_v: e.activation(out=Vr[:, sl, :, 0:64], in_=src_v, func=AF.Copy),
                          writes=[r_bank[bi], r_Vr[sl]])
                elif tl.kind == "meta":
                    P.add("act", lambda e, src_v=src_v: e.activation(out=Vm[0:16, :, 0:64], in_=src_v, func=AF.Copy),
                          writes=[r_bank[bi], r_Vm])
                else:
                    P.add("act", lambda e, s=s, src_v=src_v: e.activation(out=Vsn[0:32, s, :, 0:64], in_=src_v, func=AF.Copy),
                          writes=[r_bank[bi], r_Vsn[s]])
                outs = tl.kv_out(s)
                if outs is not None:
                    kd, vd = outs
                    kt, kr = kvst.get()
                    P.add("dve", lambda e, kt=kt, n=n, bi=bi: e.tensor_copy(out=kt[0:n, :], in_=bank[bi][0:n, 0:256]),
                          writes=[r_bank[bi], kr])
                    out_ops.append(P.add("sp", lambda e, kt=kt, n=n, kd=kd: e.dma_start(out=kd, in_=kt[0:n, 0:128]),
                                         reads=[kr], dma=True, chan=("kvo", id(kr), 0), final=True))
                    out_ops.append(P.add("sp", lambda e, kt=kt, n=n, vd=vd: e.dma_start(out=vd, in_=kt[0:n, 128:256]),
                                         reads=[kr], dma=True, chan=("kvo", id(kr), 1), final=True))
            L_ = tl.L
            nseg = T // L_
            W = L_ + 2
            for j in range(4):
                yield
                ba, bar = inproj_cols(xT, xT_rs, T, [512 + 128 * j, 1024 + 128 * j])
                ct, cr = csb.get()
                P.add("act", lambda e, ct=ct, ba=ba: e.activation(out=ct[:, 0:T], in_=ba[:, 0:T], func=AF.Copy),
                      writes=[bar, cr])
                ucv = ucx[p][:, j, 0:nseg * W].rearrange("p (a w) -> p a w", a=nseg)
                P.add("dve", lambda e, ct=ct, ba=ba, ucv=ucv: e.tensor_tensor(
                    out=ucv[:, :, 2:2 + L_], in0=ba[:, 256:256 + T].rearrange("p (a w) -> p a w", a=nseg),
                    in1=ct[:, 0:T].rearrange("p (a w) -> p a w", a=nseg), op=ALU.mult),
                    reads=[cr], writes=[bar, r_ucx[p][j]])
                if tl.kind == "meta":
                    continue
                yield
                bb, bbr = inproj_cols(xT, xT_rs, T, [0 + 128 * j, 1536 + 128 * j])
                bs, bsr = bsb.get()
                P.add("act", lambda e, bs=bs, bb=bb: e.activation(out=bs[:, 0:T], in_=bb[:, 0:T], func=AF.Copy),
                      writes=[bbr, bsr])
                P.add("act", lambda e, bb=bb, j=j: e.activation(out=qT[:, j, 0:T], in_=bb[:, 256:256 + T], func=AF.Copy,
                                                               scale=0.125), writes=[bbr, r_qT[j]])
                ty, tyr = tmpy.get()
                tyv = ty[:, 0:T].rearrange("p (a w) -> p a w", a=nseg)
                P.add("pool", lambda e, tyv=tyv, ucv=ucv, j=j: e.tensor_scalar(
                    out=tyv, in0=ucv[:, :, 0:L_], scalar1=convw[:, j, 0:1], scalar2=1.0, op0=ALU.mult, op1=ALU.mult),
                    reads=[r_ucx[p][j], r_ucctx[p], r_small], writes=[tyr])
                for tap in (1, 2):
                    P.add("dve", lambda e, tyv=tyv, ucv=ucv, j=j, tap=tap: e.scalar_tensor_tensor(
                        out=tyv, in0=ucv[:, :, tap:tap + L_], scalar=convw[:, j, tap:tap + 1], in1=tyv,
                        op0=ALU.mult, op1=ALU.add), reads=[r_ucx[p][j], r_ucctx[p], r_small], writes=[tyr])
                P.add("dve", lambda e, ty=ty, bs=bs, j=j: e.tensor_tensor(out=ty[:, 0:T], in0=ty[:, 0:T],
                                                                          in1=bs[:, 0:T], op=ALU.mult),
                      reads=[bsr], writes=[tyr])
                P.add("pool", lambda e, ty=ty, j=j: e.tensor_scalar(out=ycT[:, j, 0:T], in0=ty[:, 0:T],
                                                                    scalar1=gcv[:, j:j + 1], scalar2=1.0,
                                                                    op0=ALU.mult, op1=ALU.mult),
                      reads=[tyr, r_small], writes=[r_ycT[j]])
                P.add("act", lambda e, ty=ty, j=j: e.activation(out=ycsq[:, j, 0:T], in_=ty[:, 0:T], func=AF.Square),
                      reads=[tyr], writes=[r_ycsq[j]])
            yield
            tl.after_conv()
            tl.lnc = []
            if tl.kind != "meta":
                for s, (c0, n) in enumerate(tl.subs):
                    for j in range(4):
                        P.add("pe", lambda e, j=j, c0=c0, n=n, s=s: e.matmul(
                            ssqc_ps[0:n, s:s + 1], lhsT=ycsq[:, j, c0:c0 + n], rhs=ones_col[:, 0:1],
                            start=(j == 0), stop=(j == 3)), reads=[r_ycsq[j], r_const], writes=[r_ssqc_ps])
                    sc, scr = stat.get()
                    P.add("dve", lambda e, sc=sc, n=n, s=s: e.tensor_copy(out=sc[0:n, :], in_=ssqc_ps[0:n, s:s + 1]),
                          writes=[r_ssqc_ps, scr])
                    l2, l2r = stat.get()
                    P.add("act", lambda e, sc=sc, l2=l2, n=n: e.activation(out=l2[0:n, :], in_=sc[0:n, :], func=AF.Ln,
                                                                         scale=1.0 / CONV_DIM, bias=EPS),
                          reads=[scr], writes=[l2r])
                    tl.lnc.append((l2, l2r))

        def stage_D(tl, s):
            c0, n = tl.subs[s]
            attention(n, c0, lambda h: tl.groups(s, h))
            tl.e1[s] = attn_E1(n)

        def stage_E2a(tl, s):
            c0, n = tl.subs[s]
            yr, yrr = tl.e1[s]
            tl.e2[s] = attn_E2a(n, yr, yrr, tl.lnc[s][0], tl.lnc[s][1])

        def stage_E2b(tl, s):
            c0, n = tl.subs[s]
            ys, ysr, _, _ = tl.e2[s]
            attn_E2b(n, ys, ysr, c0, r_yaT[s])

        def stage_F(tl, s):
            p = tl.par
            c0, n = tl.subs[s]
            _, _, rstdc_t, rstdc_r = tl.e2[s]
            for half in range(2):
                ob = bank[half]
                for k in range(8):
                    if k < 4:
                        lhsT = ycT[:, k, c0:c0 + n]
                        rr = r_ycT[k]
                    else:
                        lhsT = yaT[:, k - 4, c0:c0 + n]
                        rr = r_yaT[s]
                    P.add("pe", lambda e, k=k: e.matmul(
                        ob[0:n, :], lhsT=lhsT, rhs=WOUT[:, k, half * 512:(half + 1) * 512],
                        start=(k == 0), stop=(k == 7)), reads=[rr, r_WOUT[half]], writes=[r_bank[half]])
                P.add("dve", lambda e: e.scalar_tensor_tensor(
                    out=xh[p][s][0:n, half * 512:(half + 1) * 512], in0=ob[0:n, :], scalar=rstdc_t[0:n, :],
                    in1=xh[p][s][0:n, half * 512:(half + 1) * 512], op0=ALU.mult, op1=ALU.add),
                    reads=[rstdc_r], writes=[r_bank[half], r_xh[p][s]])

        def stage_G1(tl, s):
            p = tl.par
            c0, n = tl.subs[s]
            tl.g1[s] = norm_part1(xh[p][s], r_xh[p][s], n, gmlp)

        def stage_G2(tl, s):
            c0, n = tl.subs[s]
            xb, xbr = tl.g1[s]
            norm_part2(xb, xbr, n, h1nT, r_h1nT[s], c0)

        sq_i = [0]


        def stage_H(tl, hooks=None):
            T = tl.T
            hr = [r_h1nT[s] for s in range(len(tl.subs))]
            for piece in range(8):
                wt_, wr = take_wload("up", piece)
                wt = wt_[:].rearrange("p (k c) -> p k c", k=8)
                for pair in range(2):
                    f0 = piece * 4 + pair * 2
                    bt, br = fmb_get()
                    for i in range(2):
                        fi = pair * 2 + i
                        for k in range(8):
                            P.add("pe", lambda e, bt=bt, wt=wt, k=k, fi=fi, i=i: e.matmul(
                                bt[:, i * 256:i * 256 + T], lhsT=wt[:, k, fi * 128:(fi + 1) * 128], rhs=h1nT[:, k, 0:T],
                                start=(k == 0), stop=(k == 7)), reads=[wr] + hr, writes=[br])
                    rt, rr = rtmp.get()
                    rtv = rt[:].rearrange("p (a c) -> p a c", a=2)
                    P.add("act", lambda e, rtv=rtv, bt=bt: e.activation(
                        out=rtv[:, :, 0:T], in_=bt[:].rearrange("p (a c) -> p a c", a=2)[:, :, 0:T], func=AF.Relu),
                        writes=[br, rr])
                    eng = "dve" if sq_i[0] % 2 == 0 else "pool"
                    sq_i[0] += 1
                    P.add(eng, lambda e, rtv=rtv, f0=f0: e.tensor_tensor(
                        out=hidT[:, f0:f0 + 2, 0:T], in0=rtv[:, :, 0:T], in1=rtv[:, :, 0:T], op=ALU.mult),
                        reads=[rr], writes=[r_hidT[f0], r_hidT[f0 + 1]])
                issue_wload()
                if hooks and piece in hooks:
                    for fn in hooks[piece]:
                        fn()


        def stage_I_piece(tl, piece):
            p = tl.par
            wt_, wr = take_wload("dn", piece)
            wt = wt_[:].rearrange("p (k c) -> p k c", k=4)
            for s, (c0, n) in enumerate(tl.subs):
                for half in range(2):
                    ob = bank[4 + 2 * s + half]
                    for kc in range(4):
                        f = piece * 4 + kc
                        P.add("pe", lambda e, kc=kc, f=f: e.matmul(
                            ob[0:n, :], lhsT=hidT[:, f, c0:c0 + n], rhs=wt[:, kc, half * 512:(half + 1) * 512],
                            start=(piece == 0 and kc == 0), stop=(piece == 7 and kc == 3)),
                            reads=[wr, r_hidT[f]], writes=[r_bank[4 + 2 * s + half]])
            issue_wload()
            if piece == 7:
                for s, (c0, n) in enumerate(tl.subs):
                    for half in range(2):
                        ob = bank[4 + 2 * s + half]
                        P.add("dve", lambda e: e.tensor_tensor(
                            out=xh[p][s][0:n, half * 512:(half + 1) * 512], in0=ob[0:n, :],
                            in1=xh[p][s][0:n, half * 512:(half + 1) * 512], op=ALU.add),
                            writes=[r_bank[4 + 2 * s + half], r_xh[p][s]])


        def stage_J(tl):
            p = tl.par
            for s, (c0, n) in enumerate(tl.subs):
                ln_t, ln_r = rms_stats(xh[p][s][0:n, :], n, D, [r_xh[p][s]])
                rs_t, rs_r = exp_of(ln_t, ln_r, n, -0.5)
                yt, yr = xh[p][s], r_xh[p][s]
                P.add("dve", lambda e, yt=yt, n=n, s=s, rs_t=rs_t: e.scalar_tensor_tensor(
                    out=yt[0:n, :], in0=xh[p][s][0:n, :], scalar=rs_t[0:n, :], in1=gfin[0:n, :], op0=ALU.mult,
                    op1=ALU.mult), reads=[r_xh[p][s], rs_r, r_small], writes=[yr])
                dst = tl.ydst(s)
                out_ops.append(P.add("sp", lambda e, yt=yt, n=n, dst=dst: e.dma_start(out=dst, in_=yt[0:n, :]),
                                     reads=[yr], dma=True, chan=("yo", id(yr)), final=True))

        tiles = []
        mt = Tile()
        mt.kind = "meta"
        mt.par = 1
        mt.T = 16
        mt.L = 16
        mt.subs = [(0, 16)]
        mt.xsrc = lambda s: meta_d
        mt.kv_out = lambda s: (pmk_d, pmv_d)

        def meta_after():
            P.add("dve", lambda e: e.tensor_copy(out=ucx[0][:, :, 0:2], in_=ucx[1][:, :, 16:18]),
                  reads=r_ucx[1], writes=[r_ucctx[0]])
        mt.after_conv = meta_after
        tiles.append(mt)

        for t in range(NTP):
            tl = Tile()
            tl.kind = "prompt"
            tl.t = t
            tl.par = t % 2
            tl.T = MT
            tl.L = MT
            tl.subs = [(0, 128), (128, 128)]
            tl.xsrc = lambda s, t=t: xp_d[t * MT + s * 128: t * MT + (s + 1) * 128, :]
            tl.ydst = lambda s, t=t: yp_d[t * MT + s * 128: t * MT + (s + 1) * 128, :]
            if t == NTP - 1:
                tl.kv_out = lambda s: (pk_d, pv_d) if s == 1 else None
            else:
                tl.kv_out = lambda s: None

            def after(t=t, tl=tl):
                p = tl.par
                if t == NTP - 1:
                    out_ops.append(P.add("sp", lambda e: e.dma_start(out=pconv_d, in_=ucx[p][:, :, MT:MT + 2]),
                                         reads=r_ucx[p], dma=True, chan="pconv", final=True))
                else:
                    P.add("dve", lambda e: e.tensor_copy(out=ucx[1 - p][:, :, 0:2], in_=ucx[p][:, :, MT:MT + 2]),
                          reads=r_ucx[p], writes=[r_ucctx[1 - p]])
            tl.after_conv = after

            def groups(s, h, t=t):
                bi = 2 * t + s
                hp = h % 2
                g = h // 4
                gl = []
                gl.append(dict(kk=KKm[hp * 64:(hp + 1) * 64, g, 0:17], kk_rs=[r_KKm], v=Vm[0:17, g, :], v_rs=[r_Vm], nk=17,
                               hank=(Hm0[0:17, h, :] if bi == 0 else None), anti=anti17[:],
                               cb=(CBm0[0:17, h:h + 1] if bi == 0 else CBm[0:17, h:h + 1]), slot=2))
                if bi >= 1:
                    sl = (bi - 1) % NRING
                    gl.append(dict(kk=KKr[hp * 64:(hp + 1) * 64, sl, g, :], kk_rs=[r_KKr[sl]], v=Vr[:, sl, g, :],
                                   v_rs=[r_Vr[sl]], nk=128, hank=Hprev[:, h, :], anti=anti128[:], cb=None, slot=0))
                sl = bi % NRING
                gl.append(dict(kk=KKr[hp * 64:(hp + 1) * 64, sl, g, :], kk_rs=[r_KKr[sl]], v=Vr[:, sl, g, :],
                               v_rs=[r_Vr[sl]], nk=128, hank=Hcur[:, h, :], anti=anti128[:], cb=None, slot=1))
                return gl
            tl.groups = groups
            tiles.append(tl)

        stl = Tile()
        stl.kind = "sample"
        stl.par = NTP % 2
        stl.T = 64
        stl.L = 32
        stl.subs = [(0, 32), (32, 32)]
        stl.xsrc = lambda s: xs_d[32 * s:32 * (s + 1), :]
        stl.ydst = lambda s: ys_d[32 * s:32 * (s + 1), :]
        stl.kv_out = lambda s: (sk_d[32 * s:32 * (s + 1), :], sv_d[32 * s:32 * (s + 1), :])

        def sample_after():
            p = stl.par
            v = ucx[p][:, :, 0:68].rearrange("p j (a w) -> p j a w", a=2)
            for j in range(4):
                out_ops.append(P.add("sp", lambda e, j=j: e.dma_start(out=sconvo_d[:, j, :, :], in_=v[:, j, :, 32:34]),
                                     reads=r_ucx[p], dma=True, chan="sconvo", final=True))
        stl.after_conv = sample_after

        def sgroups(s, h):
            hp = h % 2
            g = h // 4
            return [
                dict(kk=KKsm[hp * 64:(hp + 1) * 64, s, g, 0:17], kk_rs=[r_scache], v=Vsm[0:17, s, g, :], v_rs=[r_scache],
                     nk=17, hank=None, anti=None, cb=CBm[0:17, h:h + 1], slot=2),
                dict(kk=KKsw[hp * 64:(hp + 1) * 64, s, g, :], kk_rs=[r_scache], v=Vsw[:, s, g, :], v_rs=[r_scache],
                     nk=128, hank=Hsw[:, h, :], anti=anti128[:], cb=None, slot=0),
                dict(kk=KKsn[hp * 64:(hp + 1) * 64, g, 32 * s:32 * (s + 1)], kk_rs=[r_KKsn], v=Vsn[0:32, s, g, :],
                     v_rs=[r_Vsn[s]], nk=32, hank=Hsn[0:32, h, :], anti=anti32[:], cb=None, slot=1),
            ]
        stl.groups = sgroups
        tiles.append(stl)

        def sample_ctx_load():
            p = stl.par
            v = ucx[p][:, :, 0:68].rearrange("p j (a w) -> p j a w", a=2)
            for j in range(4):
                P.add("sp", lambda e, j=j: e.dma_start(out=v[:, j, :, 0:2], in_=sconv_d[:, j, :, :]),
                      writes=[r_ucctx[p]] + r_ucx[p], dma=True, chan="sctx")

        def sample_cache_prep():
            p = stl.par
            for i, src in enumerate((ck_d, cv_d)):
                for sq in range(2):
                    P.add("sp", lambda e, i=i, src=src, sq=sq: e.dma_start(out=cstv[sq][:, i, :], in_=src[sq]),
                          writes=[r_cstv[sq]], dma=True, chan=("cst", sq))
            for i, src in enumerate((cmk_d, cmv_d)):
                for sq in range(2):
                    P.add("sp", lambda e, i=i, src=src, sq=sq: e.dma_start(out=cstv[sq][0:16, 2 + i, :], in_=src[sq]),
                          writes=[r_cstv[sq]], dma=True, chan=("cst", sq))
            for sq in range(2):
                P.add("dve", lambda e, sq=sq: e.tensor_copy(out=Vsw[:, sq, :, 0:64],
                                                            in_=cstv[sq][:, 1, :].rearrange("k (g c) -> k g c", g=2)),
                      reads=[r_cstv[sq]], writes=[r_scache])
                P.add("dve", lambda e, sq=sq: e.tensor_copy(out=Vsm[0:16, sq, :, 0:64],
                                                            in_=cstv[sq][0:16, 3, :].rearrange("k (g c) -> k g c", g=2)),
                      reads=[r_cstv[sq]], writes=[r_scache])
                for cp in range(2):
                    P.add("dve", lambda e, cp=cp, sq=sq: e.tensor_copy(
                        out=kdup[:, sq, :, cp, :], in_=cstv[sq][:, 0, :].rearrange("k (g c) -> k g c", g=2)),
                        reads=[r_cstv[sq]], writes=[r_kdup])
                    P.add("dve", lambda e, cp=cp, sq=sq: e.tensor_copy(
                        out=kmdup[:, sq, :, cp, :], in_=cstv[sq][0:16, 2, :].rearrange("k (g c) -> k g c", g=2)),
                        reads=[r_cstv[sq]], writes=[r_kdup])
            for s in range(2):
                for g in range(2):
                    P.add("pe", lambda e, s=s, g=g: e.transpose(
                        out=TRb[:, 0:128], in_=kdup[:, s, g, :, :].rearrange("k a c -> k (a c)"), identity=ident[:]),
                        reads=[r_kdup, r_const], writes=[r_TR])
                    P.add("dve", lambda e, s=s, g=g: e.tensor_copy(out=KKsw[:, s, g, :], in_=TRb[:, 0:128]),
                          writes=[r_TR, r_scache])
                    P.add("pe", lambda e, s=s, g=g: e.transpose(
                        out=TRb[:, 0:16], in_=kmdup[:, s, g, :, :].rearrange("k a c -> k (a c)"), identity=ident[0:16, 0:16]),
                        reads=[r_kdup, r_const], writes=[r_TR])
                    P.add("dve", lambda e, s=s, g=g: e.tensor_copy(out=KKsm[:, s, g, 0:16], in_=TRb[:, 0:16]),
                          writes=[r_TR, r_scache])

        prep_weights()
        stage_A(tiles[0])
        stage_B(tiles[0])
        stage_A(tiles[1])
        sample_cache_prep()
        for _ in range(4):
            issue_wload()
        full = tiles[1:]
        prev = None
        stage_B(full[0])
        for i, tl in enumerate(full):
            nsub = len(tl.subs)
            tl.e1 = [None] * nsub
            tl.e2 = [None] * nsub
            tl.g1 = [None] * nsub
            for s in range(nsub):
                stage_D(tl, s)
            seq = [("m", stage_E2a, 0), ("m", stage_E2a, 1), ("i",), ("m", stage_E2b, 0), ("i",), ("m", stage_F, 0),
                   ("m", stage_E2b, 1), ("i",), ("m", stage_G1, 0), ("m", stage_F, 1), ("i",), ("m", stage_G2, 0),
                   ("m", stage_G1, 1), ("i",), ("m", stage_G2, 1), ("i",), ("i",), ("i",)]
            piece = 0
            for it in seq:
                if it[0] == "m":
                    it[1](tl, it[2])
                elif prev is not None:
                    stage_I_piece(prev, piece)
                    piece += 1
            if prev is not None:
                assert piece == 8
                stage_J(prev)
            hooks = None
            if i + 1 < len(full):
                nx = full[i + 1]
                nx.a1 = [None] * len(nx.subs)
                for s in range(len(nx.subs)):
                    stage_A1(nx, s)
                bgen = stage_B_gen(nx)

                def step(n, bgen=bgen):
                    def f():
                        for _ in range(n):
                            try:
                                next(bgen)
                            except StopIteration:
                                pass
                    return f

                def a2(nx=nx):
                    for s in range(len(nx.subs)):
                        stage_A2(nx, s)
                hooks = {1: [a2], 2: [step(2)], 3: [step(2)], 4: [step(2)], 5: [step(2)], 6: [step(2)], 7: [step(20)]}
            stage_H(tl, hooks)
            prev = tl
        for piece in range(8):
            stage_I_piece(prev, piece)
        stage_J(prev)

        P.emit(nc)
    return nc, P


_CACHE = {}


def _get_nc(S_TOK):
    if S_TOK not in _CACHE:
        _CACHE[S_TOK] = build(S_TOK)
    return _CACHE[S_TOK][0]


def kernel(x_prompt, x_sample, cache_k, cache_v, cache_meta_k, cache_meta_v, state_conv, meta_tokens,
           norm_mix, w_in, conv_w, attn_sinks, rel_bias_table, norm_conv_out, norm_attn_out, w_out,
           norm_mlp, w_up, w_down, norm_final):
    f = lambda a: np.ascontiguousarray(np.asarray(a, dtype=np.float32))
    x_prompt = f(x_prompt)
    x_sample = f(x_sample)
    B, S_TOK, _ = x_prompt.shape
    ncores = 8
    assert B == ncores
    nc = _get_nc(S_TOK)
    w_in0 = f(w_in)[0]
    k0 = w_in0[:, 2048:2112]
    k1 = w_in0[:, 2112:2176]
    win_x = np.ascontiguousarray(np.concatenate([w_in0, k0, k0, k1, k1], axis=1))
    pk = lambda v: np.ascontiguousarray(v.reshape(-1, 128).T)
    gout = np.concatenate([f(norm_conv_out)[0], f(norm_attn_out)[0]])
    convw = np.ascontiguousarray(f(conv_w)[0].reshape(3, 4, 128).transpose(2, 1, 0))
    common = {
        "meta": f(meta_tokens), "gmix": f(norm_mix).reshape(1, D), "gmlp": f(norm_mlp).reshape(1, D),
        "gcv": pk(f(norm_conv_out)[0]), "gatt": f(norm_attn_out).reshape(1, Q_DIM),
        "gfin": f(norm_final).reshape(1, D), "convw": convw, "sinks": f(attn_sinks).reshape(1, 8),
        "table": f(rel_bias_table), "oh": _onehot_const(), "win": win_x, "wout": f(w_out)[0],
        "wup": f(w_up)[0], "wdn": f(w_down)[0],
    }
    ck = f(cache_k)[0].reshape(16, 128, 128)
    cv = f(cache_v)[0].reshape(16, 128, 128)
    cmk = f(cache_meta_k)[0].reshape(16, 16, 128)
    cmv = f(cache_meta_v)[0].reshape(16, 16, 128)
    sc = f(state_conv)[0]
    in_maps = []
    for c in range(ncores):
        m = dict(common)
        m["xp"] = x_prompt[c]
        m["xs"] = np.ascontiguousarray(x_sample[2 * c:2 * c + 2].reshape(64, D))
        m["ck"] = np.ascontiguousarray(ck[2 * c:2 * c + 2])
        m["cv"] = np.ascontiguousarray(cv[2 * c:2 * c + 2])
        m["cmk"] = np.ascontiguousarray(cmk[2 * c:2 * c + 2])
        m["cmv"] = np.ascontiguousarray(cmv[2 * c:2 * c + 2])
        m["sconv"] = np.ascontiguousarray(sc[2 * c:2 * c + 2].reshape(2, 2, 4, 128).transpose(3, 2, 0, 1))
        in_maps.append(m)
    res = run_bass_kernel_spmd(nc, in_maps, core_ids=list(range(ncores)))
    R = res.results
    y_prompt = np.stack([R[c]["yp"] for c in range(ncores)]).astype(np.float32)
    y_sample = np.concatenate([R[c]["ys"].reshape(2, 32, D) for c in range(ncores)]).astype(np.float32)
    p_k = np.stack([R[c]["pk"].reshape(128, 2, 64) for c in range(ncores)])[None].astype(np.float32)
    p_v = np.stack([R[c]["pv"].reshape(128, 2, 64) for c in range(ncores)])[None].astype(np.float32)
    p_mk = np.stack([R[c]["pmk"].reshape(16, 2, 64) for c in range(ncores)])[None].astype(np.float32)
    p_mv = np.stack([R[c]["pmv"].reshape(16, 2, 64) for c in range(ncores)])[None].astype(np.float32)
    p_conv = np.stack([R[c]["pconv"].transpose(2, 1, 0).reshape(2, 512) for c in range(ncores)])[None].astype(np.float32)
    s_k = np.concatenate([R[c]["sk"].reshape(2, 32, 2, 64) for c in range(ncores)])[None].astype(np.float32)
    s_v = np.concatenate([R[c]["sv"].reshape(2, 32, 2, 64) for c in range(ncores)])[None].astype(np.float32)
    s_conv = np.concatenate([R[c]["sconvo"].transpose(2, 3, 1, 0).reshape(2, 2, 512) for c in range(ncores)])[None].astype(np.float32)
    return (y_prompt, y_sample, p_k, p_v, p_mk, p_mv, p_conv, s_k, s_v, s_conv)
```

```python
import contextlib
import math
import types
import numpy as np
import concourse.bass as bass
import concourse.mybir as mybir
from concourse.bass_utils import run_bass_kernel_spmd

F32 = mybir.dt.float32
BF16 = mybir.dt.bfloat16
AF = mybir.ActivationFunctionType
ALU = mybir.AluOpType

D = 1024
SEQ = 8192
N_META = 16
CONV_DIM = 512
Q_DIM = 512
IN_DIM = 2304
IN_X = 2560
D_FF = 4096
EPS = 1e-6
MT = 256
NEG = -30000.0
PAST_LEN = 4096
DBG_STOP = False

HSEG = [("prev", 128, 128, -128), ("cur", 128, 128, 0), ("meta0", 16, 128, -16),
        ("sw", 128, 32, -128), ("sn", 32, 32, 0)]
HOFF = {}
_o = 0
for _n, _nk, _T, _db in HSEG:
    HOFF[_n] = (_o, _nk, _T, _db)
    _o += _nk + _T - 1
HLEN = _o


def _t5_bucket_np(rp):
    rp = np.asarray(rp, dtype=np.int32)
    nb = 16
    max_exact = 8
    ret = np.where(rp > 0, nb, 0)
    n = np.abs(rp)
    nf = np.maximum(n, 1).astype(np.float32)
    large = max_exact + (np.log(nf / np.float32(max_exact)) / np.float32(math.log(128 / max_exact))
                         * np.float32(nb - max_exact)).astype(np.int32)
    large = np.minimum(large, nb - 1)
    return ret + np.where(n < max_exact, n, large)


def _bucket(rp):
    try:
        import jax
        import jax.numpy as jnp
        cpu = jax.devices("cpu")[0]
        with jax.default_device(cpu):
            rp = jnp.asarray(np.asarray(rp, dtype=np.int32))
            nb = 16
            max_exact = 8
            ret = jnp.where(rp > 0, nb, 0)
            n = jnp.abs(rp)
            nf = jnp.maximum(n, 1).astype(jnp.float32)
            large = max_exact + (jnp.log(nf / max_exact) / math.log(128 / max_exact) * (nb - max_exact)).astype(jnp.int32)
            large = jnp.minimum(large, nb - 1)
            return np.asarray(ret + jnp.where(n < max_exact, n, large))
    except Exception:
        return _t5_bucket_np(rp)


def _onehot_const():
    oh = np.zeros((32, HLEN), np.float32)
    for name, (off, nk, T, db) in HOFF.items():
        j = np.arange(nk + T - 1)
        d = db + (nk - 1) - j
        b = _bucket(d)
        oh[b, off + j] = 1.0
    return oh


def _freeze(fn):
    if fn.__closure__ is None:
        return fn
    cells = tuple(types.CellType(c.cell_contents) for c in fn.__closure__)
    return types.FunctionType(fn.__code__, fn.__globals__, fn.__name__, fn.__defaults__, cells)


class Res:
    __slots__ = ("name", "lw", "rd")

    def __init__(self, name):
        self.name = name
        self.lw = None
        self.rd = []


class Op:
    __slots__ = ("eng", "fn", "deps", "dma", "chan", "sig", "hasdep", "idx")


class Prog:
    ENGS = ("pe", "act", "dve", "pool", "sp")

    def __init__(self):
        self.ops = []
        self.chan_cnt = {}
        self.eng_cnt = {e: 0 for e in self.ENGS}
        self.final = []

    def add(self, eng, fn, reads=(), writes=(), dma=False, chan=None, final=False):
        op = Op()
        op.eng = eng
        op.fn = _freeze(fn)
        op.dma = dma
        op.hasdep = False
        op.sig = None
        op.idx = len(self.ops)
        deps = []
        for r in reads:
            if r.lw is not None:
                deps.append(r.lw)
        for w in writes:
            if w.lw is not None:
                deps.append(w.lw)
            deps.extend(w.rd)
        seen = set()
        od = []
        for d in deps:
            if d is op or id(d) in seen:
                continue
            seen.add(id(d))
            if eng == "pe" and d.eng == "pe" and not d.dma and not dma:
                continue
            d.hasdep = True
            od.append(d)
        op.deps = od
        for r in reads:
            r.rd.append(op)
        for w in writes:
            w.lw = op
            w.rd = []
        if dma:
            op.chan = chan if chan is not None else (writes[0] if writes else ("dma", eng))
        else:
            op.chan = None
        if dma:
            op.hasdep = True
        if final:
            op.hasdep = True
            self.final.append(op)
        self.ops.append(op)
        return op

    def emit(self, nc):
        chan_keys = []
        for op in self.ops:
            if not op.hasdep:
                continue
            if op.dma:
                k = op.chan if isinstance(op.chan, (str, tuple)) else id(op.chan)
                if k not in self.chan_cnt:
                    self.chan_cnt[k] = 0
                    chan_keys.append(k)
                self.chan_cnt[k] += 16
                op.sig = (("c", k), self.chan_cnt[k])
            else:
                self.eng_cnt[op.eng] += 1
                op.sig = (("e", op.eng), self.eng_cnt[op.eng])
        with contextlib.ExitStack() as st:
            sems = {}
            for e in self.ENGS:
                sems[("e", e)] = st.enter_context(nc.semaphore("s_" + e))
            for i, k in enumerate(chan_keys):
                sems[("c", k)] = st.enter_context(nc.semaphore("c_%d" % i))
            self.nsem = len(sems)
            block = st.enter_context(nc.Block())

            def run(engname, e):
                known = {}
                for op in self.ops:
                    if op.eng != engname:
                        continue
                    for d in op.deps:
                        sk, v = d.sig
                        if known.get(sk, 0) < v:
                            e.wait_ge(sems[sk], v)
                            known[sk] = v
                    ins = op.fn(e)
                    if op.sig is not None:
                        sk, v = op.sig
                        ins.then_inc(sems[sk], 16 if op.dma else 1)
                if engname == "sp":
                    fin = {}
                    for op in self.final:
                        sk, v = op.sig
                        fin[sk] = max(fin.get(sk, 0), v)
                    for sk, v in fin.items():
                        if known.get(sk, 0) < v:
                            e.wait_ge(sems[sk], v)
                            known[sk] = v

            @block.tensor
            def _(e):
                run("pe", e)

            @block.scalar
            def _(e):
                run("act", e)

            @block.vector
            def _(e):
                run("dve", e)

            @block.gpsimd
            def _(e):
                run("pool", e)

            @block.sync
            def _(e):
                run("sp", e)


class Rot:
    def __init__(self, alloc, name, n, shape, dt):
        self.t = [alloc("%s%d" % (name, i), shape, dt) for i in range(n)]
        self.r = [Res("%s%d" % (name, i)) for i in range(n)]
        self.i = 0

    def get(self):
        i = self.i
        self.i = (i + 1) % len(self.t)
        return self.t[i], self.r[i]


def build(S_TOK):
    assert S_TOK % MT == 0
    NTP = S_TOK // MT
    nc = bass.Bass("TRN2", target_bir_lowering=False)

    def din(name, shape, dt=F32):
        return nc.dram_tensor(name, list(shape), dt, kind="ExternalInput").ap()

    def dout(name, shape, dt=F32):
        return nc.dram_tensor(name, list(shape), dt, kind="ExternalOutput").ap()

    xp_d = din("xp", [S_TOK, D])
    xs_d = din("xs", [64, D])
    ck_d = din("ck", [2, 128, 128])
    cv_d = din("cv", [2, 128, 128])
    cmk_d = din("cmk", [2, 16, 128])
    cmv_d = din("cmv", [2, 16, 128])
    sconv_d = din("sconv", [128, 4, 2, 2])
    meta_d = din("meta", [16, D])
    gmix_d = din("gmix", [1, D])
    gmlp_d = din("gmlp", [1, D])
    gcv_d = din("gcv", [128, 4])
    gatt_d = din("gatt", [1, Q_DIM])
    gfin_d = din("gfin", [1, D])
    convw_d = din("convw", [128, 4, 3])
    sinks_d = din("sinks", [1, 8])
    table_d = din("table", [32, 8])
    oh_d = din("oh", [32, HLEN])
    win_d = din("win", [D, IN_X])
    wout_d = din("wout", [D, D])
    wup_d = din("wup", [D, D_FF])
    wdn_d = din("wdn", [D_FF, D])

    yp_d = dout("yp", [S_TOK, D])
    ys_d = dout("ys", [64, D])
    pk_d = dout("pk", [128, 128])
    pv_d = dout("pv", [128, 128])
    pmk_d = dout("pmk", [16, 128])
    pmv_d = dout("pmv", [16, 128])
    pconv_d = dout("pconv", [128, 4, 2])
    sk_d = dout("sk", [64, 128])
    sv_d = dout("sv", [64, 128])
    sconvo_d = dout("sconvo", [128, 4, 2, 2])

    gscr = nc.dram_tensor("gscr", [8, HLEN], BF16, kind="Internal").ap()
    wup_s = nc.dram_tensor("wup_s", [8, 128, 8, 512], BF16, kind="Internal").ap()
    wdn_s = nc.dram_tensor("wdn_s", [8, 128, 4, 1024], BF16, kind="Internal").ap()
    r_gscr = Res("gscr")
    r_wup_s = [Res("wup_s%d" % i) for i in range(8)]
    r_wdn_s = [Res("wdn_s%d" % i) for i in range(8)]

    P = Prog()
    with contextlib.ExitStack() as st:
        def sb(name, shape, dt):
            return st.enter_context(nc.sbuf_tensor("s_" + name, list(shape), dt))

        def ps(name, shape, dt):
            return st.enter_context(nc.psum_tensor("p_" + name, list(shape), dt))

        bank = [ps("bank%d" % i, [128, 512], F32) for i in range(8) if i != 2]
        bank.insert(2, None)
        TRb = ps("TRb", [128, 1024], BF16)
        r_bank = [Res("bank%d" % i) for i in range(8)]
        r_TR = r_bank[2]
        ssqc_ps = bank[3][:, 0:4]
        r_ssqc_ps = r_bank[3]
        gv_ps = bank[4]
        fmb_i = [0]

        def fmb_get():
            i = fmb_i[0]
            fmb_i[0] = 1 - i
            return bank[i], r_bank[i]

        ident = sb("ident", [128, 128], BF16)
        anti128 = sb("anti128", [128, 128], BF16)
        anti32 = sb("anti32", [32, 32], BF16)
        anti17 = sb("anti17", [17, 17], BF16)
        ones_col = sb("ones_col", [128, 1], BF16)
        iot = sb("iot", [128, 128], F32)
        r_const = Res("const")
        r_iot = Res("iot")
        P.add("pool", lambda e: e.iota(iot[:], pattern=[[1, 128]], base=0, channel_multiplier=-1,
                                       allow_small_or_imprecise_dtypes=True), writes=[r_iot])
        P.add("dve", lambda e: e.tensor_scalar(out=ident[:], in0=iot[:], scalar1=0.0, scalar2=None,
                                               op0=ALU.is_equal), reads=[r_iot], writes=[r_const])
        iot2 = sb("iot2", [128, 128], F32)
        r_iot2 = Res("iot2")
        P.add("pool", lambda e: e.iota(iot2[:], pattern=[[1, 128]], base=0, channel_multiplier=1,
                                       allow_small_or_imprecise_dtypes=True), writes=[r_iot2])
        P.add("dve", lambda e: e.tensor_scalar(out=anti128[:], in0=iot2[:], scalar1=127.0, scalar2=None,
                                               op0=ALU.is_equal), reads=[r_iot2], writes=[r_const])
        P.add("dve", lambda e: e.tensor_scalar(out=anti32[:], in0=iot2[0:32, 0:32], scalar1=31.0, scalar2=None,
                                               op0=ALU.is_equal), reads=[r_iot2], writes=[r_const])
        P.add("dve", lambda e: e.tensor_scalar(out=anti17[:], in0=iot2[0:17, 0:17], scalar1=15.0, scalar2=None,
                                               op0=ALU.is_equal), reads=[r_iot2], writes=[r_const])
        P.add("dve", lambda e: e.memset(ones_col[:], 1.0), writes=[r_const])

        gmix = sb("gmix", [128, D], F32)
        gmlp = sb("gmlp", [128, D], F32)
        gatt = sb("gatt", [128, Q_DIM], F32)
        gcv = sb("gcv", [128, 4], F32)
        convw = sb("convw", [128, 4, 3], F32)
        gfin = sb("gfin", [128, D], F32)
        table = sb("table", [32, 8], F32)
        ohs = sb("ohs", [32, HLEN], F32)
        CBm0 = sb("CBm0", [17, 8], F32)
        CBm = sb("CBm", [17, 8], F32)
        r_small = Res("small")
        r_cb = Res("cb")
        for tl, src in ((gmix, gmix_d.partition_broadcast(128)), (gmlp, gmlp_d.partition_broadcast(128)),
                        (gatt, gatt_d.partition_broadcast(128)), (gcv, gcv_d), (convw, convw_d), (table, table_d),
                        (ohs, oh_d)):
            P.add("sp", lambda e, tl=tl, src=src: e.dma_start(out=tl[:], in_=src), writes=[r_small], dma=True,
                  chan="small")
        P.add("sp", lambda e: e.dma_start(out=gfin[:], in_=gfin_d.partition_broadcast(128)), writes=[r_small],
              dma=True, chan="small")
        P.add("dve", lambda e: e.memset(CBm0[:], 0.0), writes=[r_cb])
        P.add("sp", lambda e: e.dma_start(out=CBm0[16:17, :], in_=sinks_d), writes=[r_cb], dma=True, chan="cb")
        P.add("sp", lambda e: e.dma_start(out=CBm[16:17, :], in_=sinks_d), writes=[r_cb], dma=True, chan="cb")
        P.add("sp", lambda e: e.dma_start(out=CBm[0:16, :], in_=table_d[15:16, :].partition_broadcast(16)),
              writes=[r_cb], dma=True, chan="cb")

        gvb = sb("gvb", [8, HLEN], BF16)
        r_gvb = Res("gvb")
        Hprev = sb("Hprev", [128, 8, 128], BF16)
        Hcur = sb("Hcur", [128, 8, 128], BF16)
        Hm0 = sb("Hm0", [17, 8, 128], BF16)
        Hsw = sb("Hsw", [128, 8, 32], BF16)
        Hsn = sb("Hsn", [32, 8, 32], BF16)
        r_H = Res("H")
        for c0 in range(0, HLEN, 512):
            c1 = min(HLEN, c0 + 512)
            P.add("pe", lambda e, c0=c0, c1=c1: e.matmul(gv_ps[0:8, 0:c1 - c0], lhsT=table[:], rhs=ohs[:, c0:c1],
                                                         start=True, stop=True),
                  reads=[r_small], writes=[r_bank[4]])
            P.add("dve", lambda e, c0=c0, c1=c1: e.tensor_copy(out=gvb[:, c0:c1], in_=gv_ps[0:8, 0:c1 - c0]),
                  writes=[r_bank[4], r_gvb])
        P.add("sp", lambda e: e.dma_start(out=gscr, in_=gvb[:]), reads=[r_gvb], writes=[r_gscr], dma=True)
        P.add("dve", lambda e: e.memset(Hm0[:], 0.0), writes=[r_H])
        for tl, name in ((Hprev, "prev"), (Hcur, "cur"), (Hm0, "meta0"), (Hsw, "sw"), (Hsn, "sn")):
            off, nk, T, db = HOFF[name]
            src = bass.AP(gscr.tensor, off, [[1, nk], [HLEN, 8], [1, T]])
            P.add("sp", lambda e, tl=tl, src=src, nk=nk: e.dma_start(out=tl[0:nk, :, :], in_=src),
                  reads=[r_gscr], writes=[r_H], dma=True, chan="H")
        P.add("dve", lambda e: e.memset(Hprev[64:128, :, 64:128], NEG), writes=[r_H])
        P.add("dve", lambda e: e.memset(Hcur[0:64, :, 0:64], NEG), writes=[r_H])

        WIN = sb("WIN", [128, 8, IN_X], BF16)
        WOUT = sb("WOUT", [128, 8, D], BF16)
        r_WIN = [[Res("WIN%d_%d" % (k, c)) for c in range(2)] for k in range(8)]
        r_WOUT = [Res("WOUT%d" % c) for c in range(2)]

        def prep_weights():
            for k in range(8):
                for c in range(2):
                    P.add("pool", lambda e: e.dma_start(out=WIN[:, k, c * 1280:(c + 1) * 1280],
                                                        in_=win_d[k * 128:(k + 1) * 128, c * 1280:(c + 1) * 1280]),
                          writes=[r_WIN[k][c]], dma=True)
            for c in range(2):
                src = wout_d[:, c * 512:(c + 1) * 512].rearrange("(k p) c -> p k c", p=128)
                P.add("pool", lambda e: e.dma_start(out=WOUT[:, :, c * 512:(c + 1) * 512], in_=src),
                      writes=[r_WOUT[c]], dma=True)
            for i in range(8):
                src = wup_d[:, i * 512:(i + 1) * 512].rearrange("(k p) c -> p k c", p=128)
                P.add("pool", lambda e: e.dma_start(out=wup_s[i], in_=src), writes=[r_wup_s[i]], dma=True)
            for i in range(8):
                src = wdn_d[i * 512:(i + 1) * 512, :].rearrange("(k p) c -> p k c", p=128)
                P.add("pool", lambda e: e.dma_start(out=wdn_s[i], in_=src), writes=[r_wdn_s[i]], dma=True)


        xh = [[sb("xh%d%d" % (p, s), [128, D], F32) for s in range(2)] for p in range(2)]
        r_xh = [[Res("xh%d%d" % (p, s)) for s in range(2)] for p in range(2)]
        xsb = Rot(sb, "xsb", 2, [128, D], BF16)
        junk = Rot(sb, "junk", 1, [128, D], BF16)
        _xsT = sb("xsT", [128, 8, MT], BF16)
        xsT = [_xsT, _xsT]
        _r_xsT = [Res("xsT%d" % s) for s in range(2)]
        r_xsT = [_r_xsT, _r_xsT]
        h1nT = sb("h1nT", [128, 8, MT], BF16)
        r_h1nT = [Res("h1nT%d" % s) for s in range(2)]
        qT = sb("qT", [128, 4, MT], BF16)
        r_qT = [Res("qT%d" % j) for j in range(4)]
        NRING = 4
        KKr = sb("KKr", [128, NRING, 2, 128], BF16)
        r_KKr = [Res("KKr%d" % i) for i in range(NRING)]
        Vr = sb("Vr", [128, NRING, 2, 65], BF16)
        r_Vr = [Res("Vr%d" % i) for i in range(NRING)]
        KKm = sb("KKm", [128, 2, 17], BF16)
        Vm = sb("Vm", [17, 2, 65], BF16)
        r_KKm = Res("KKm")
        r_Vm = Res("Vm")
        KKsm = sb("KKsm", [128, 2, 2, 17], BF16)
        Vsm = sb("Vsm", [17, 2, 2, 65], BF16)
        KKsw = sb("KKsw", [128, 2, 2, 128], BF16)
        Vsw = sb("Vsw", [128, 2, 2, 65], BF16)
        KKsn = sb("KKsn", [128, 2, 64], BF16)
        Vsn = sb("Vsn", [32, 2, 2, 65], BF16)
        r_scache = Res("scache")
        r_KKsn = Res("KKsn")
        r_Vsn = [Res("Vsn0"), Res("Vsn1")]
        ucx = [sb("ucx%d" % p, [128, 4, MT + 4], F32) for p in range(2)]
        r_ucx = [[Res("ucx%d%d" % (p, j)) for j in range(4)] for p in range(2)]
        r_ucctx = [Res("ucctx%d" % p) for p in range(2)]
        csb = Rot(sb, "csb", 2, [128, MT], F32)
        tmpy = Rot(sb, "tmpy", 2, [128, MT], F32)
        bsb = Rot(sb, "bsb", 2, [128, MT], F32)
        ycsq = sb("ycsq", [128, 4, MT], BF16)
        r_ycsq = [Res("ycsq%d" % j) for j in range(4)]
        ycT = sb("ycT", [128, 4, MT], BF16)
        r_ycT = [Res("ycT%d" % j) for j in range(4)]
        yaT = sb("yaT", [128, 4, MT], BF16)
        r_yaT = [Res("yaT%d" % s) for s in range(2)]
        yaraw = Rot(sb, "yaraw", 2, [128, 512], F32)
        yas = Rot(sb, "yas", 2, [128, 512], BF16)
        stat = Rot(sb, "stat", 48, [128, 1], F32)
        rec8 = Rot(sb, "rec8", 2, [128, 8], F32)
        hidT = sb("hidT", [128, 32, MT], BF16)
        r_hidT = [Res("hidT%d" % f) for f in range(32)]
        rtmp = Rot(sb, "rtmp", 2, [128, 2 * MT], F32)
        ring = Rot(sb, "ring", 4, [128, 4096], BF16)
        kvst = Rot(sb, "kvst", 1, [128, 256], F32)
        kdup = sb("kdup", [128, 2, 2, 2, 64], BF16)
        kmdup = sb("kmdup", [16, 2, 2, 2, 64], BF16)
        cstv = [rtmp.t[q][:].rearrange("p (a c) -> p a c", a=4) for q in range(2)]
        r_cstv = [rtmp.r[q] for q in range(2)]
        r_kdup = Res("kdup")

        P.add("pool", lambda e: e.memset(Vr[:], 1.0), writes=r_Vr)
        P.add("pool", lambda e: e.memset(Vm[:], 1.0), writes=[r_Vm])
        P.add("pool", lambda e: e.memset(Vm[:, :, 0:64], 0.0), writes=[r_Vm])
        P.add("pool", lambda e: e.memset(Vsm[:], 1.0), writes=[r_scache])
        P.add("pool", lambda e: e.memset(Vsm[:, :, :, 0:64], 0.0), writes=[r_scache])
        P.add("pool", lambda e: e.memset(Vsw[:], 1.0), writes=[r_scache])
        P.add("pool", lambda e: e.memset(Vsn[:], 1.0), writes=r_Vsn)
        P.add("pool", lambda e: e.memset(KKm[:], 0.0), writes=[r_KKm])
        P.add("pool", lambda e: e.memset(KKsm[:], 0.0), writes=[r_scache])
        P.add("pool", lambda e: e.memset(KKr[:], 0.0), writes=r_KKr)

        out_ops = []

        def rms_stats(src_ap, n_tok, nfeat, reads):
            jt, jr = junk.get()
            s1, r1 = stat.get()
            P.add("act", lambda e: e.activation(out=jt[0:n_tok, 0:nfeat], in_=src_ap, func=AF.Square,
                                                accum_out=s1[0:n_tok, :]), reads=reads, writes=[jr, r1])
            s2, r2 = stat.get()
            P.add("act", lambda e: e.activation(out=s2[0:n_tok, :], in_=s1[0:n_tok, :], func=AF.Ln,
                                                scale=1.0 / nfeat, bias=EPS), reads=[r1], writes=[r2])
            return s2, r2

        def exp_of(ln_t, ln_r, n_tok, scale):
            s3, r3 = stat.get()
            P.add("act", lambda e: e.activation(out=s3[0:n_tok, :], in_=ln_t[0:n_tok, :], func=AF.Exp, scale=scale),
                  reads=[ln_r], writes=[r3])
            return s3, r3

        def norm_part1(src_t, src_r, n_tok, gain):
            ln_t, ln_r = rms_stats(src_t[0:n_tok, :], n_tok, D, [src_r])
            rs_t, rs_r = exp_of(ln_t, ln_r, n_tok, -0.5)
            xb, xbr = xsb.get()
            P.add("dve", lambda e: e.scalar_tensor_tensor(out=xb[0:n_tok, :], in0=src_t[0:n_tok, :],
                                                          scalar=rs_t[0:n_tok, :], in1=gain[0:n_tok, :],
                                                          op0=ALU.mult, op1=ALU.mult),
                  reads=[src_r, rs_r, r_small], writes=[xbr])
            return xb, xbr

        def norm_part2(xb, xbr, n_tok, dstT, dst_r, col0):
            for k in range(8):
                P.add("pe", lambda e, k=k: e.transpose(out=TRb[:, k * 128:k * 128 + n_tok],
                                                       in_=xb[0:n_tok, k * 128:(k + 1) * 128],
                                                       identity=ident[0:n_tok, 0:n_tok]),
                      reads=[xbr, r_const], writes=[r_TR])
            src = TRb[:].rearrange("p (k t) -> p k t", k=8)[:, :, 0:n_tok]
            P.add("dve", lambda e: e.tensor_copy(out=dstT[:, :, col0:col0 + n_tok], in_=src),
                  writes=[r_TR, dst_r])

        def norm_transpose(src_t, src_r, n_tok, dstT, dst_r, col0, gain):
            xb, xbr = norm_part1(src_t, src_r, n_tok, gain)
            norm_part2(xb, xbr, n_tok, dstT, dst_r, col0)


        def inproj_cols(xT, xT_rs, T, cols):
            bt, br = fmb_get()
            for i, col in enumerate(cols):
                for k in range(8):
                    P.add("pe", lambda e, k=k, i=i, col=col: e.matmul(
                        bt[:, i * 256:i * 256 + T], lhsT=WIN[:, k, col:col + 128], rhs=xT[:, k, 0:T],
                        start=(k == 0), stop=(k == 7)), reads=r_WIN[k] + xT_rs, writes=[br])
            return bt, br

        def kv_tokmajor(xT, xT_r, col0, n_tok, bi):
            for k in range(8):
                P.add("pe", lambda e, k=k: e.matmul(bank[bi][0:n_tok, 0:256], lhsT=xT[:, k, col0:col0 + n_tok],
                                                    rhs=WIN[:, k, 2048:2304], start=(k == 0), stop=(k == 7)),
                      reads=r_WIN[k] + [xT_r], writes=[r_bank[bi]])

        PTh = Rot(sb, "PTh", 3, [128, 3, 128], BF16)

        def attention(T, qcol, groups_fn):
            pend = None
            for h in range(8):
                gl = groups_fn(h)
                hp = h % 2
                sbk = bank[h % 2]
                sbr = r_bank[h % 2]
                for g in gl:
                    nk = g["nk"]
                    sl = g["slot"]
                    P.add("pe", lambda e, g=g, nk=nk, sl=sl, hp=hp, h=h: e.matmul(
                        sbk[0:nk, sl * 128:sl * 128 + T], lhsT=g["kk"],
                        rhs=qT[hp * 64:(hp + 1) * 64, h // 2, qcol:qcol + T], start=True, stop=(g["hank"] is None)),
                        reads=g["kk_rs"] + [r_qT[h // 2]], writes=[sbr])
                    if g["hank"] is not None:
                        P.add("pe", lambda e, g=g, nk=nk, sl=sl: e.matmul(
                            sbk[0:nk, sl * 128:sl * 128 + T], lhsT=g["anti"], rhs=g["hank"], start=False, stop=True),
                            reads=[r_H, r_const], writes=[sbr])
                pt, pr = PTh.get()
                full = [g for g in gl if g["nk"] == 128 and g["cb"] is None]
                rest = [g for g in gl if not (g["nk"] == 128 and g["cb"] is None)]
                if full:
                    s0 = min(g["slot"] for g in full)
                    s1 = max(g["slot"] for g in full) + 1
                    assert s1 - s0 == len(full)
                    if T == 128:
                        P.add("act", lambda e, pt=pt, s0=s0, s1=s1: e.activation(
                            out=pt[:, s0:s1, :], in_=sbk[:, s0 * 128:s1 * 128].rearrange("p (a c) -> p a c", c=128),
                            func=AF.Exp), reads=[], writes=[sbr, pr])
                    else:
                        for g in full:
                            sl = g["slot"]
                            P.add("act", lambda e, pt=pt, sl=sl: e.activation(
                                out=pt[:, sl, 0:T], in_=sbk[:, sl * 128:sl * 128 + T], func=AF.Exp),
                                writes=[sbr, pr])
                for g in rest:
                    nk = g["nk"]
                    sl = g["slot"]
                    if g["cb"] is None:
                        P.add("act", lambda e, pt=pt, sl=sl, nk=nk: e.activation(
                            out=pt[0:nk, sl, 0:T], in_=sbk[0:nk, sl * 128:sl * 128 + T], func=AF.Exp),
                            writes=[sbr, pr])
                    else:
                        P.add("act", lambda e, pt=pt, sl=sl, nk=nk, g=g: e.activation(
                            out=pt[0:nk, sl, 0:T], in_=sbk[0:nk, sl * 128:sl * 128 + T], func=AF.Exp, bias=g["cb"]),
                            reads=[r_cb], writes=[sbr, pr])
                if pend is not None:
                    emit_pv(*pend)
                pend = (h, T, gl, pt, pr)
            emit_pv(*pend)

        def emit_pv(h, T, gl, pt, pr):
            ob = bank[4 + h // 4]
            obr = r_bank[4 + h // 4]
            hh = h % 4
            n = len(gl)
            for i, g in enumerate(gl):
                nk = g["nk"]
                sl = g["slot"]
                P.add("pe", lambda e, g=g, nk=nk, i=i, sl=sl: e.matmul(
                    ob[0:T, hh * 65:(hh + 1) * 65], lhsT=pt[0:nk, sl, 0:T], rhs=g["v"], start=(i == 0), stop=(i == n - 1)),
                    reads=[pr] + g["v_rs"], writes=[obr])


        def attn_E1(T):
            rc, rcr = rec8.get()
            yr, yrr = yaraw.get()
            for b in range(2):
                ob = bank[4 + b]
                o3 = ob[0:T, 0:260].rearrange("p (h c) -> p h c", h=4)
                P.add("dve", lambda e, o3=o3, b=b: e.reciprocal(out=rc[0:T, 4 * b:4 * b + 4], in_=o3[:, :, 64]),
                      writes=[r_bank[4 + b], rcr])
                P.add("dve", lambda e, o3=o3, b=b: e.tensor_tensor(
                    out=yr[0:T, 256 * b:256 * (b + 1)].rearrange("p (h c) -> p h c", h=4),
                    in0=o3[:, :, 0:64],
                    in1=rc[0:T, 4 * b:4 * b + 4].unsqueeze(2).to_broadcast([T, 4, 64]),
                    op=ALU.mult), reads=[rcr], writes=[r_bank[4 + b], yrr])
            return yr, yrr

        def attn_E2a(T, yr, yrr, lnc_t, lnc_r):
            lna_t, lna_r = rms_stats(yr[0:T, :], T, Q_DIM, [yrr])
            d_t, d_r = stat.get()
            P.add("dve", lambda e: e.tensor_tensor(out=d_t[0:T, :], in0=lnc_t[0:T, :], in1=lna_t[0:T, :],
                                                   op=ALU.subtract), reads=[lnc_r, lna_r], writes=[d_r])
            ratio_t, ratio_r = exp_of(d_t, d_r, T, 0.5)
            rstdc_t, rstdc_r = exp_of(lnc_t, lnc_r, T, -0.5)
            ys, ysr = yas.get()
            P.add("dve", lambda e: e.scalar_tensor_tensor(out=ys[0:T, :], in0=yr[0:T, :], scalar=ratio_t[0:T, :],
                                                          in1=gatt[0:T, :], op0=ALU.mult, op1=ALU.mult),
                  reads=[yrr, ratio_r, r_small], writes=[ysr])
            return ys, ysr, rstdc_t, rstdc_r

        def attn_E2b(T, ys, ysr, yaT_col0, r_yaT_s):
            for j in range(4):
                P.add("pe", lambda e, j=j: e.transpose(out=TRb[:, j * 128:j * 128 + T], in_=ys[0:T, j * 128:(j + 1) * 128],
                                                       identity=ident[0:T, 0:T]),
                      reads=[ysr, r_const], writes=[r_TR])
            src = TRb[:, 0:512].rearrange("p (k t) -> p k t", k=4)[:, :, 0:T]
            P.add("dve", lambda e: e.tensor_copy(out=yaT[:, :, yaT_col0:yaT_col0 + T], in_=src),
                  writes=[r_TR, r_yaT_s])


        ring_state = {"pending": [], "left": 16 * (NTP + 1)}

        def mlp_weights_iter():
            while True:
                for i in range(8):
                    yield ("up", i)
                for i in range(8):
                    yield ("dn", i)

        wgen = mlp_weights_iter()

        def issue_wload():
            if ring_state["left"] <= 0:
                return
            ring_state["left"] -= 1
            kind, i = next(wgen)
            t, r = ring.get()
            src = (wup_s if kind == "up" else wdn_s)[i]
            rs = (r_wup_s if kind == "up" else r_wdn_s)[i]
            if kind == "up":
                dstv = t[:].rearrange("p (k c) -> p k c", k=8)
            else:
                dstv = t[:].rearrange("p (k c) -> p k c", k=4)
            P.add("pool", lambda e, dstv=dstv, src=src: e.dma_start(out=dstv, in_=src), reads=[rs], writes=[r], dma=True)
            ring_state["pending"].append((kind, i, t, r))

        def take_wload(kind, i):
            k2, i2, t, r = ring_state["pending"].pop(0)
            assert (k2, i2) == (kind, i)
            return t, r

        class Tile:
            pass

        def stage_A1(tl, s):
            p = tl.par
            c0, n = tl.subs[s]
            src = tl.xsrc(s)
            P.add("sp", lambda e: e.dma_start(out=xh[p][s][0:n, :], in_=src), writes=[r_xh[p][s]], dma=True)
            tl.a1[s] = norm_part1(xh[p][s], r_xh[p][s], n, gmix)

        def stage_A2(tl, s):
            p = tl.par
            c0, n = tl.subs[s]
            xb, xbr = tl.a1[s]
            norm_part2(xb, xbr, n, xsT[p], r_xsT[p][s], c0)

        def stage_A(tl):
            tl.a1 = [None] * len(tl.subs)
            for s in range(len(tl.subs)):
                stage_A1(tl, s)
                stage_A2(tl, s)


        def stage_B(tl):
            for _ in stage_B_gen(tl):
                pass

        def stage_B_gen(tl):
            if tl.kind == "sample":
                sample_ctx_load()
            p = tl.par
            T = tl.T
            xT = xsT[p]
            xT_rs = [r_xsT[p][s] for s in range(len(tl.subs))]
            bt, br = inproj_cols(xT, xT_rs, T, [2304, 2432])
            if tl.kind == "prompt":
                s0 = (2 * tl.t) % NRING
                for g in range(2):
                    P.add("dve", lambda e, bt=bt, g=g, s0=s0: e.tensor_copy(
                        out=KKr[:, s0:s0 + 2, g, :], in_=bt[:, g * 256:(g + 1) * 256].rearrange("p (a c) -> p a c", a=2)),
                        writes=[br, r_KKr[s0], r_KKr[s0 + 1]])
            elif tl.kind == "meta":
                for g in range(2):
                    P.add("dve", lambda e, bt=bt, g=g: e.tensor_copy(out=KKm[:, g, 0:16], in_=bt[:, g * 256:g * 256 + 16]),
                          writes=[br, r_KKm])
            else:
                for g in range(2):
                    P.add("dve", lambda e, bt=bt, g=g: e.tensor_copy(out=KKsn[:, g, :], in_=bt[:, g * 256:g * 256 + 64]),
                          writes=[br, r_KKsn])
            yield
            for s, (c0, n) in enumerate(tl.subs):
                if s > 0:
                    yield
                bi = 6 + s
                kv_tokmajor(xT, r_xsT[p][s], c0, n, bi)
                src_v = bank[bi][0:n, 128:256].rearrange("p (g c) -> p g c", g=2)
                if tl.kind == "prompt":
                    sl = (2 * tl.t + s) % NRING
                    P.add("act", lambda e, sl=sl, src_v=src_v: e.activation(out=Vr[:, sl, :, 0:64], in_=src_v, func=AF.Copy),
                          writes=[r_bank[bi], r_Vr[sl]])
                elif tl.kind == "meta":
                    P.add("act", lambda e, src_v=src_v: e.activation(out=Vm[0:16, :, 0:64], in_=src_v, func=AF.Copy),
                          writes=[r_bank[bi], r_Vm])
                else:
                    P.add("act", lambda e, s=s, src_v=src_v: e.activation(out=Vsn[0:32, s, :, 0:64], in_=src_v, func=AF.Copy),
                          writes=[r_bank[bi], r_Vsn[s]])
                outs = tl.kv_out(s)
                if outs is not None:
                    kd, vd = outs
                    kt, kr = kvst.get()
                    P.add("dve", lambda e, kt=kt, n=n, bi=bi: e.tensor_copy(out=kt[0:n, :], in_=bank[bi][0:n, 0:256]),
                          writes=[r_bank[bi], kr])
                    out_ops.append(P.add("sp", lambda e, kt=kt, n=n, kd=kd: e.dma_start(out=kd, in_=kt[0:n, 0:128]),
                                         reads=[kr], dma=True, chan=("kvo", id(kr), 0), final=True))
                    out_ops.append(P.add("sp", lambda e, kt=kt, n=n, vd=vd: e.dma_start(out=vd, in_=kt[0:n, 128:256]),
                                         reads=[kr], dma=True, chan=("kvo", id(kr), 1), final=True))
            L_ = tl.L
            nseg = T // L_
            W = L_ + 2
            for j in range(4):
                yield
                ba, bar = inproj_cols(xT, xT_rs, T, [512 + 128 * j, 1024 + 128 * j])
                ct, cr = csb.get()
                P.add("act", lambda e, ct=ct, ba=ba: e.activation(out=ct[:, 0:T], in_=ba[:, 0:T], func=AF.Copy),
                      writes=[bar, cr])
                ucv = ucx[p][:, j, 0:nseg * W].rearrange("p (a w) -> p a w", a=nseg)
                P.add("dve", lambda e, ct=ct, ba=ba, ucv=ucv: e.tensor_tensor(
                    out=ucv[:, :, 2:2 + L_], in0=ba[:, 256:256 + T].rearrange("p (a w) -> p a w", a=nseg),
                    in1=ct[:, 0:T].rearrange("p (a w) -> p a w", a=nseg), op=ALU.mult),
                    reads=[cr], writes=[bar, r_ucx[p][j]])
                if tl.kind == "meta":
                    continue
                yield
                bb, bbr = inproj_cols(xT, xT_rs, T, [0 + 128 * j, 1536 + 128 * j])
                bs, bsr = bsb.get()
                P.add("act", lambda e, bs=bs, bb=bb: e.activation(out=bs[:, 0:T], in_=bb[:, 0:T], func=AF.Copy),
                      writes=[bbr, bsr])
                P.add("act", lambda e, bb=bb, j=j: e.activation(out=qT[:, j, 0:T], in_=bb[:, 256:256 + T], func=AF.Copy,
                                                               scale=0.125), writes=[bbr, r_qT[j]])
                ty, tyr = tmpy.get()
                tyv = ty[:, 0:T].rearrange("p (a w) -> p a w", a=nseg)
                P.add("dve", lambda e, tyv=tyv, ucv=ucv, j=j: e.tensor_scalar(
                    out=tyv, in0=ucv[:, :, 0:L_], scalar1=convw[:, j, 0:1], scalar2=None, op0=ALU.mult),
                    reads=[r_ucx[p][j], r_ucctx[p], r_small], writes=[tyr])
                for tap in (1, 2):
                    P.add("dve", lambda e, tyv=tyv, ucv=ucv, j=j, tap=tap: e.scalar_tensor_tensor(
                        out=tyv, in0=ucv[:, :, tap:tap + L_], scalar=convw[:, j, tap:tap + 1], in1=tyv,
                        op0=ALU.mult, op1=ALU.add), reads=[r_ucx[p][j], r_ucctx[p], r_small], writes=[tyr])
                P.add("dve", lambda e, ty=ty, bs=bs, j=j: e.tensor_tensor(out=ty[:, 0:T], in0=ty[:, 0:T],
                                                                          in1=bs[:, 0:T], op=ALU.mult),
                      reads=[bsr], writes=[tyr])
                P.add("dve", lambda e, ty=ty, j=j: e.tensor_scalar(out=ycT[:, j, 0:T], in0=ty[:, 0:T],
                                                                   scalar1=gcv[:, j:j + 1], scalar2=None,
                                                                   op0=ALU.mult),
                      reads=[tyr, r_small], writes=[r_ycT[j]])
                P.add("act", lambda e, ty=ty, j=j: e.activation(out=ycsq[:, j, 0:T], in_=ty[:, 0:T], func=AF.Square),
                      reads=[tyr], writes=[r_ycsq[j]])
            yield
            tl.after_conv()
            tl.lnc = []
            if tl.kind != "meta":
                for s, (c0, n) in enumerate(tl.subs):
                    for j in range(4):
                        P.add("pe", lambda e, j=j, c0=c0, n=n, s=s: e.matmul(
                            ssqc_ps[0:n, s:s + 1], lhsT=ycsq[:, j, c0:c0 + n], rhs=ones_col[:, 0:1],
                            start=(j == 0), stop=(j == 3)), reads=[r_ycsq[j], r_const], writes=[r_ssqc_ps])
                    sc, scr = stat.get()
                    P.add("dve", lambda e, sc=sc, n=n, s=s: e.tensor_copy(out=sc[0:n, :], in_=ssqc_ps[0:n, s:s + 1]),
                          writes=[r_ssqc_ps, scr])
                    l2, l2r = stat.get()
                    P.add("act", lambda e, sc=sc, l2=l2, n=n: e.activation(out=l2[0:n, :], in_=sc[0:n, :], func=AF.Ln,
                                                                         scale=1.0 / CONV_DIM, bias=EPS),
                          reads=[scr], writes=[l2r])
                    tl.lnc.append((l2, l2r))

        def stage_D(tl, s):
            c0, n = tl.subs[s]
            attention(n, c0, lambda h: tl.groups(s, h))
            tl.e1[s] = attn_E1(n)

        def stage_E2a(tl, s):
            c0, n = tl.subs[s]
            yr, yrr = tl.e1[s]
            tl.e2[s] = attn_E2a(n, yr, yrr, tl.lnc[s][0], tl.lnc[s][1])

        def stage_E2b(tl, s):
            c0, n = tl.subs[s]
            ys, ysr, _, _ = tl.e2[s]
            attn_E2b(n, ys, ysr, c0, r_yaT[s])

        def stage_F(tl, s):
            p = tl.par
            c0, n = tl.subs[s]
            _, _, rstdc_t, rstdc_r = tl.e2[s]
            for half in range(2):
                ob = bank[half]
                for k in range(8):
                    if k < 4:
                        lhsT = ycT[:, k, c0:c0 + n]
                        rr = r_ycT[k]
                    else:
                        lhsT = yaT[:, k - 4, c0:c0 + n]
                        rr = r_yaT[s]
                    P.add("pe", lambda e, k=k: e.matmul(
                        ob[0:n, :], lhsT=lhsT, rhs=WOUT[:, k, half * 512:(half + 1) * 512],
                        start=(k == 0), stop=(k == 7)), reads=[rr, r_WOUT[half]], writes=[r_bank[half]])
                P.add("dve", lambda e: e.scalar_tensor_tensor(
                    out=xh[p][s][0:n, half * 512:(half + 1) * 512], in0=ob[0:n, :], scalar=rstdc_t[0:n, :],
                    in1=xh[p][s][0:n, half * 512:(half + 1) * 512], op0=ALU.mult, op1=ALU.add),
                    reads=[rstdc_r], writes=[r_bank[half], r_xh[p][s]])

        def stage_G1(tl, s):
            p = tl.par
            c0, n = tl.subs[s]
            tl.g1[s] = norm_part1(xh[p][s], r_xh[p][s], n, gmlp)

        def stage_G2(tl, s):
            c0, n = tl.subs[s]
            xb, xbr = tl.g1[s]
            norm_part2(xb, xbr, n, h1nT, r_h1nT[s], c0)

        sq_i = [0]


        def stage_H(tl, hooks=None):
            T = tl.T
            hr = [r_h1nT[s] for s in range(len(tl.subs))]
            for piece in range(8):
                wt_, wr = take_wload("up", piece)
                wt = wt_[:].rearrange("p (k c) -> p k c", k=8)
                for pair in range(2):
                    f0 = piece * 4 + pair * 2
                    bt, br = fmb_get()
                    for i in range(2):
                        fi = pair * 2 + i
                        for k in range(8):
                            P.add("pe", lambda e, bt=bt, wt=wt, k=k, fi=fi, i=i: e.matmul(
                                bt[:, i * 256:i * 256 + T], lhsT=wt[:, k, fi * 128:(fi + 1) * 128], rhs=h1nT[:, k, 0:T],
                                start=(k == 0), stop=(k == 7)), reads=[wr] + hr, writes=[br])
                    rt, rr = rtmp.get()
                    rtv = rt[:].rearrange("p (a c) -> p a c", a=2)
                    P.add("act", lambda e, rtv=rtv, bt=bt: e.activation(
                        out=rtv[:, :, 0:T], in_=bt[:].rearrange("p (a c) -> p a c", a=2)[:, :, 0:T], func=AF.Relu),
                        writes=[br, rr])
                    eng = "dve"
                    sq_i[0] += 1
                    P.add(eng, lambda e, rtv=rtv, f0=f0: e.tensor_tensor(
                        out=hidT[:, f0:f0 + 2, 0:T], in0=rtv[:, :, 0:T], in1=rtv[:, :, 0:T], op=ALU.mult),
                        reads=[rr], writes=[r_hidT[f0], r_hidT[f0 + 1]])
                issue_wload()
                if hooks and piece in hooks:
                    for fn in hooks[piece]:
                        fn()


        def stage_I_piece(tl, piece):
            p = tl.par
            wt_, wr = take_wload("dn", piece)
            wt = wt_[:].rearrange("p (k c) -> p k c", k=4)
            for s, (c0, n) in enumerate(tl.subs):
                for half in range(2):
                    ob = bank[4 + 2 * s + half]
                    for kc in range(4):
                        f = piece * 4 + kc
                        P.add("pe", lambda e, kc=kc, f=f: e.matmul(
                            ob[0:n, :], lhsT=hidT[:, f, c0:c0 + n], rhs=wt[:, kc, half * 512:(half + 1) * 512],
                            start=(piece == 0 and kc == 0), stop=(piece == 7 and kc == 3)),
                            reads=[wr, r_hidT[f]], writes=[r_bank[4 + 2 * s + half]])
            issue_wload()
            if piece == 7:
                for s, (c0, n) in enumerate(tl.subs):
                    for half in range(2):
                        ob = bank[4 + 2 * s + half]
                        P.add("dve", lambda e: e.tensor_tensor(
                            out=xh[p][s][0:n, half * 512:(half + 1) * 512], in0=ob[0:n, :],
                            in1=xh[p][s][0:n, half * 512:(half + 1) * 512], op=ALU.add),
                            writes=[r_bank[4 + 2 * s + half], r_xh[p][s]])


        def stage_J(tl):
            p = tl.par
            for s, (c0, n) in enumerate(tl.subs):
                ln_t, ln_r = rms_stats(xh[p][s][0:n, :], n, D, [r_xh[p][s]])
                rs_t, rs_r = exp_of(ln_t, ln_r, n, -0.5)
                yt, yr = xh[p][s], r_xh[p][s]
                P.add("dve", lambda e, yt=yt, n=n, s=s, rs_t=rs_t: e.scalar_tensor_tensor(
                    out=yt[0:n, :], in0=xh[p][s][0:n, :], scalar=rs_t[0:n, :], in1=gfin[0:n, :], op0=ALU.mult,
                    op1=ALU.mult), reads=[r_xh[p][s], rs_r, r_small], writes=[yr])
                dst = tl.ydst(s)
                out_ops.append(P.add("sp", lambda e, yt=yt, n=n, dst=dst: e.dma_start(out=dst, in_=yt[0:n, :]),
                                     reads=[yr], dma=True, chan=("yo", id(yr)), final=True))

        tiles = []
        mt = Tile()
        mt.kind = "meta"
        mt.par = 1
        mt.T = 16
        mt.L = 16
        mt.subs = [(0, 16)]
        mt.xsrc = lambda s: meta_d
        mt.kv_out = lambda s: (pmk_d, pmv_d)

        def meta_after():
            P.add("dve", lambda e: e.tensor_copy(out=ucx[0][:, :, 0:2], in_=ucx[1][:, :, 16:18]),
                  reads=r_ucx[1], writes=[r_ucctx[0]])
        mt.after_conv = meta_after
        tiles.append(mt)

        for t in range(NTP):
            tl = Tile()
            tl.kind = "prompt"
            tl.t = t
            tl.par = t % 2
            tl.T = MT
            tl.L = MT
            tl.subs = [(0, 128), (128, 128)]
            tl.xsrc = lambda s, t=t: xp_d[t * MT + s * 128: t * MT + (s + 1) * 128, :]
            tl.ydst = lambda s, t=t: yp_d[t * MT + s * 128: t * MT + (s + 1) * 128, :]
            if t == NTP - 1:
                tl.kv_out = lambda s: (pk_d, pv_d) if s == 1 else None
            else:
                tl.kv_out = lambda s: None

            def after(t=t, tl=tl):
                p = tl.par
                if t == NTP - 1:
                    out_ops.append(P.add("sp", lambda e: e.dma_start(out=pconv_d, in_=ucx[p][:, :, MT:MT + 2]),
                                         reads=r_ucx[p], dma=True, chan="pconv", final=True))
                else:
                    P.add("dve", lambda e: e.tensor_copy(out=ucx[1 - p][:, :, 0:2], in_=ucx[p][:, :, MT:MT + 2]),
                          reads=r_ucx[p], writes=[r_ucctx[1 - p]])
            tl.after_conv = after

            def groups(s, h, t=t):
                bi = 2 * t + s
                hp = h % 2
                g = h // 4
                gl = []
                gl.append(dict(kk=KKm[hp * 64:(hp + 1) * 64, g, 0:17], kk_rs=[r_KKm], v=Vm[0:17, g, :], v_rs=[r_Vm], nk=17,
                               hank=(Hm0[0:17, h, :] if bi == 0 else None), anti=anti17[:],
                               cb=(CBm0[0:17, h:h + 1] if bi == 0 else CBm[0:17, h:h + 1]), slot=2))
                if bi >= 1:
                    sl = (bi - 1) % NRING
                    gl.append(dict(kk=KKr[hp * 64:(hp + 1) * 64, sl, g, :], kk_rs=[r_KKr[sl]], v=Vr[:, sl, g, :],
                                   v_rs=[r_Vr[sl]], nk=128, hank=Hprev[:, h, :], anti=anti128[:], cb=None, slot=0))
                sl = bi % NRING
                gl.append(dict(kk=KKr[hp * 64:(hp + 1) * 64, sl, g, :], kk_rs=[r_KKr[sl]], v=Vr[:, sl, g, :],
                               v_rs=[r_Vr[sl]], nk=128, hank=Hcur[:, h, :], anti=anti128[:], cb=None, slot=1))
                return gl
            tl.groups = groups
            tiles.append(tl)

        stl = Tile()
        stl.kind = "sample"
        stl.par = NTP % 2
        stl.T = 64
        stl.L = 32
        stl.subs = [(0, 32), (32, 32)]
        stl.xsrc = lambda s: xs_d[32 * s:32 * (s + 1), :]
        stl.ydst = lambda s: ys_d[32 * s:32 * (s + 1), :]
        stl.kv_out = lambda s: (sk_d[32 * s:32 * (s + 1), :], sv_d[32 * s:32 * (s + 1), :])

        def sample_after():
            p = stl.par
            v = ucx[p][:, :, 0:68].rearrange("p j (a w) -> p j a w", a=2)
            for j in range(4):
                out_ops.append(P.add("sp", lambda e, j=j: e.dma_start(out=sconvo_d[:, j, :, :], in_=v[:, j, :, 32:34]),
                                     reads=r_ucx[p], dma=True, chan="sconvo", final=True))
        stl.after_conv = sample_after

        def sgroups(s, h):
            hp = h % 2
            g = h // 4
            return [
                dict(kk=KKsm[hp * 64:(hp + 1) * 64, s, g, 0:17], kk_rs=[r_scache], v=Vsm[0:17, s, g, :], v_rs=[r_scache],
                     nk=17, hank=None, anti=None, cb=CBm[0:17, h:h + 1], slot=2),
                dict(kk=KKsw[hp * 64:(hp + 1) * 64, s, g, :], kk_rs=[r_scache], v=Vsw[:, s, g, :], v_rs=[r_scache],
                     nk=128, hank=Hsw[:, h, :], anti=anti128[:], cb=None, slot=0),
                dict(kk=KKsn[hp * 64:(hp + 1) * 64, g, 32 * s:32 * (s + 1)], kk_rs=[r_KKsn], v=Vsn[0:32, s, g, :],
                     v_rs=[r_Vsn[s]], nk=32, hank=Hsn[0:32, h, :], anti=anti32[:], cb=None, slot=1),
            ]
        stl.groups = sgroups
        tiles.append(stl)

        def sample_ctx_load():
            p = stl.par
            v = ucx[p][:, :, 0:68].rearrange("p j (a w) -> p j a w", a=2)
            for j in range(4):
                P.add("sp", lambda e, j=j: e.dma_start(out=v[:, j, :, 0:2], in_=sconv_d[:, j, :, :]),
                      writes=[r_ucctx[p]] + r_ucx[p], dma=True, chan="sctx")

        def sample_cache_prep():
            p = stl.par
            for i, src in enumerate((ck_d, cv_d)):
                for sq in range(2):
                    P.add("sp", lambda e, i=i, src=src, sq=sq: e.dma_start(out=cstv[sq][:, i, :], in_=src[sq]),
                          writes=[r_cstv[sq]], dma=True, chan=("cst", sq))
            for i, src in enumerate((cmk_d, cmv_d)):
                for sq in range(2):
                    P.add("sp", lambda e, i=i, src=src, sq=sq: e.dma_start(out=cstv[sq][0:16, 2 + i, :], in_=src[sq]),
                          writes=[r_cstv[sq]], dma=True, chan=("cst", sq))
            for sq in range(2):
                P.add("dve", lambda e, sq=sq: e.tensor_copy(out=Vsw[:, sq, :, 0:64],
                                                            in_=cstv[sq][:, 1, :].rearrange("k (g c) -> k g c", g=2)),
                      reads=[r_cstv[sq]], writes=[r_scache])
                P.add("dve", lambda e, sq=sq: e.tensor_copy(out=Vsm[0:16, sq, :, 0:64],
                                                            in_=cstv[sq][0:16, 3, :].rearrange("k (g c) -> k g c", g=2)),
                      reads=[r_cstv[sq]], writes=[r_scache])
                for cp in range(2):
                    P.add("dve", lambda e, cp=cp, sq=sq: e.tensor_copy(
                        out=kdup[:, sq, :, cp, :], in_=cstv[sq][:, 0, :].rearrange("k (g c) -> k g c", g=2)),
                        reads=[r_cstv[sq]], writes=[r_kdup])
                    P.add("dve", lambda e, cp=cp, sq=sq: e.tensor_copy(
                        out=kmdup[:, sq, :, cp, :], in_=cstv[sq][0:16, 2, :].rearrange("k (g c) -> k g c", g=2)),
                        reads=[r_cstv[sq]], writes=[r_kdup])
            for s in range(2):
                for g in range(2):
                    P.add("pe", lambda e, s=s, g=g: e.transpose(
                        out=TRb[:, 0:128], in_=kdup[:, s, g, :, :].rearrange("k a c -> k (a c)"), identity=ident[:]),
                        reads=[r_kdup, r_const], writes=[r_TR])
                    P.add("dve", lambda e, s=s, g=g: e.tensor_copy(out=KKsw[:, s, g, :], in_=TRb[:, 0:128]),
                          writes=[r_TR, r_scache])
                    P.add("pe", lambda e, s=s, g=g: e.transpose(
                        out=TRb[:, 0:16], in_=kmdup[:, s, g, :, :].rearrange("k a c -> k (a c)"), identity=ident[0:16, 0:16]),
                        reads=[r_kdup, r_const], writes=[r_TR])
                    P.add("dve", lambda e, s=s, g=g: e.tensor_copy(out=KKsm[:, s, g, 0:16], in_=TRb[:, 0:16]),
                          writes=[r_TR, r_scache])

        prep_weights()
        stage_A(tiles[0])
        stage_B(tiles[0])
        stage_A(tiles[1])
        sample_cache_prep()
        for _ in range(4):
            issue_wload()
        full = tiles[1:]
        prev = None
        stage_B(full[0])
        for i, tl in enumerate(full):
            nsub = len(tl.subs)
            tl.e1 = [None] * nsub
            tl.e2 = [None] * nsub
            tl.g1 = [None] * nsub
            for s in range(nsub):
                stage_D(tl, s)
            seq = [("m", stage_E2a, 0), ("m", stage_E2a, 1), ("i",), ("m", stage_E2b, 0), ("i",), ("m", stage_F, 0),
                   ("m", stage_E2b, 1), ("i",), ("m", stage_G1, 0), ("m", stage_F, 1), ("i",), ("m", stage_G2, 0),
                   ("m", stage_G1, 1), ("i",), ("m", stage_G2, 1), ("i",), ("i",), ("i",)]
            piece = 0
            for it in seq:
                if it[0] == "m":
                    it[1](tl, it[2])
                elif prev is not None:
                    stage_I_piece(prev, piece)
                    piece += 1
            if prev is not None:
                assert piece == 8
                stage_J(prev)
            hooks = None
            if i + 1 < len(full):
                nx = full[i + 1]
                nx.a1 = [None] * len(nx.subs)
                for s in range(len(nx.subs)):
                    stage_A1(nx, s)
                bgen = stage_B_gen(nx)

                def step(n, bgen=bgen):
                    def f():
                        for _ in range(n):
                            try:
                                next(bgen)
                            except StopIteration:
                                pass
                    return f

                def a2(nx=nx):
                    for s in range(len(nx.subs)):
                        stage_A2(nx, s)
                hooks = {2: [a2], 3: [step(2)], 4: [step(2)], 5: [step(2)], 6: [step(3)], 7: [step(20)]}
            stage_H(tl, hooks)
            prev = tl
        for piece in range(8):
            stage_I_piece(prev, piece)
        stage_J(prev)

        P.emit(nc)
    return nc, P


_CACHE = {}


def _get_nc(S_TOK):
    if S_TOK not in _CACHE:
        _CACHE[S_TOK] = build(S_TOK)
    return _CACHE[S_TOK][0]


def kernel(x_prompt, x_sample, cache_k, cache_v, cache_meta_k, cache_meta_v, state_conv, meta_tokens,
           norm_mix, w_in, conv_w, attn_sinks, rel_bias_table, norm_conv_out, norm_attn_out, w_out,
           norm_mlp, w_up, w_down, norm_final):
    f = lambda a: np.ascontiguousarray(np.asarray(a, dtype=np.float32))
    x_prompt = f(x_prompt)
    x_sample = f(x_sample)
    B, S_TOK, _ = x_prompt.shape
    ncores = 8
    assert B == ncores
    nc = _get_nc(S_TOK)
    w_in0 = f(w_in)[0]
    k0 = w_in0[:, 2048:2112]
    k1 = w_in0[:, 2112:2176]
    win_x = np.ascontiguousarray(np.concatenate([w_in0, k0, k0, k1, k1], axis=1))
    pk = lambda v: np.ascontiguousarray(v.reshape(-1, 128).T)
    gout = np.concatenate([f(norm_conv_out)[0], f(norm_attn_out)[0]])
    convw = np.ascontiguousarray(f(conv_w)[0].reshape(3, 4, 128).transpose(2, 1, 0))
    common = {
        "meta": f(meta_tokens), "gmix": f(norm_mix).reshape(1, D), "gmlp": f(norm_mlp).reshape(1, D),
        "gcv": pk(f(norm_conv_out)[0]), "gatt": f(norm_attn_out).reshape(1, Q_DIM),
        "gfin": f(norm_final).reshape(1, D), "convw": convw, "sinks": f(attn_sinks).reshape(1, 8),
        "table": f(rel_bias_table), "oh": _onehot_const(), "win": win_x, "wout": f(w_out)[0],
        "wup": f(w_up)[0], "wdn": f(w_down)[0],
    }
    ck = f(cache_k)[0].reshape(16, 128, 128)
    cv = f(cache_v)[0].reshape(16, 128, 128)
    cmk = f(cache_meta_k)[0].reshape(16, 16, 128)
    cmv = f(cache_meta_v)[0].reshape(16, 16, 128)
    sc = f(state_conv)[0]
    in_maps = []
    for c in range(ncores):
        m = dict(common)
        m["xp"] = x_prompt[c]
        m["xs"] = np.ascontiguousarray(x_sample[2 * c:2 * c + 2].reshape(64, D))
        m["ck"] = np.ascontiguousarray(ck[2 * c:2 * c + 2])
        m["cv"] = np.ascontiguousarray(cv[2 * c:2 * c + 2])
        m["cmk"] = np.ascontiguousarray(cmk[2 * c:2 * c + 2])
        m["cmv"] = np.ascontiguousarray(cmv[2 * c:2 * c + 2])
        m["sconv"] = np.ascontiguousarray(sc[2 * c:2 * c + 2].reshape(2, 2, 4, 128).transpose(3, 2, 0, 1))
        in_maps.append(m)
    res = run_bass_kernel_spmd(nc, in_maps, core_ids=list(range(ncores)))
    R = res.results
    y_prompt = np.stack([R[c]["yp"] for c in range(ncores)]).astype(np.float32)
    y_sample = np.concatenate([R[c]["ys"].reshape(2, 32, D) for c in range(ncores)]).astype(np.float32)
    p_k = np.stack([R[c]["pk"].reshape(128, 2, 64) for c in range(ncores)])[None].astype(np.float32)
    p_v = np.stack([R[c]["pv"].reshape(128, 2, 64) for c in range(ncores)])[None].astype(np.float32)
    p_mk = np.stack([R[c]["pmk"].reshape(16, 2, 64) for c in range(ncores)])[None].astype(np.float32)
    p_mv = np.stack([R[c]["pmv"].reshape(16, 2, 64) for c in range(ncores)])[None].astype(np.float32)
    p_conv = np.stack([R[c]["pconv"].transpose(2, 1, 0).reshape(2, 512) for c in range(ncores)])[None].astype(np.float32)
    s_k = np.concatenate([R[c]["sk"].reshape(2, 32, 2, 64) for c in range(ncores)])[None].astype(np.float32)
    s_v = np.concatenate([R[c]["sv"].reshape(2, 32, 2, 64) for c in range(ncores)])[None].astype(np.float32)
    s_conv = np.concatenate([R[c]["sconvo"].transpose(2, 3, 1, 0).reshape(2, 2, 512) for c in range(ncores)])[None].astype(np.float32)
    return (y_prompt, y_sample, p_k, p_v, p_mk, p_mv, p_conv, s_k, s_v, s_conv)
```

```python
import contextlib
import math
import types
import numpy as np
import concourse.bass as bass
import concourse.mybir as mybir
from concourse.bass_utils import run_bass_kernel_spmd

F32 = mybir.dt.float32
BF16 = mybir.dt.bfloat16
AF = mybir.ActivationFunctionType
ALU = mybir.AluOpType

D = 1024
SEQ = 8192
N_META = 16
CONV_DIM = 512
Q_DIM = 512
IN_DIM = 2304
IN_X = 2560
D_FF = 4096
EPS = 1e-6
MT = 256
NEG = -30000.0
PAST_LEN = 4096
DBG_STOP = False

HSEG = [("prev", 128, 128, -128), ("cur", 128, 128, 0), ("meta0", 16, 128, -16),
        ("sw", 128, 32, -128), ("sn", 32, 32, 0)]
HOFF = {}
_o = 0
for _n, _nk, _T, _db in HSEG:
    HOFF[_n] = (_o, _nk, _T, _db)
    _o += _nk + _T - 1
HLEN = _o


def _t5_bucket_np(rp):
    rp = np.asarray(rp, dtype=np.int32)
    nb = 16
    max_exact = 8
    ret = np.where(rp > 0, nb, 0)
    n = np.abs(rp)
    nf = np.maximum(n, 1).astype(np.float32)
    large = max_exact + (np.log(nf / np.float32(max_exact)) / np.float32(math.log(128 / max_exact))
                         * np.float32(nb - max_exact)).astype(np.int32)
    large = np.minimum(large, nb - 1)
    return ret + np.where(n < max_exact, n, large)


def _bucket(rp):
    try:
        import jax
        import jax.numpy as jnp
        cpu = jax.devices("cpu")[0]
        with jax.default_device(cpu):
            rp = jnp.asarray(np.asarray(rp, dtype=np.int32))
            nb = 16
            max_exact = 8
            ret = jnp.where(rp > 0, nb, 0)
            n = jnp.abs(rp)
            nf = jnp.maximum(n, 1).astype(jnp.float32)
            large = max_exact + (jnp.log(nf / max_exact) / math.log(128 / max_exact) * (nb - max_exact)).astype(jnp.int32)
            large = jnp.minimum(large, nb - 1)
            return np.asarray(ret + jnp.where(n < max_exact, n, large))
    except Exception:
        return _t5_bucket_np(rp)


def _onehot_const():
    oh = np.zeros((32, HLEN), np.float32)
    for name, (off, nk, T, db) in HOFF.items():
        j = np.arange(nk + T - 1)
        d = db + (nk - 1) - j
        b = _bucket(d)
        oh[b, off + j] = 1.0
    return oh


def _freeze(fn):
    if fn.__closure__ is None:
        return fn
    cells = tuple(types.CellType(c.cell_contents) for c in fn.__closure__)
    return types.FunctionType(fn.__code__, fn.__globals__, fn.__name__, fn.__defaults__, cells)


class Res:
    __slots__ = ("name", "lw", "rd")

    def __init__(self, name):
        self.name = name
        self.lw = None
        self.rd = []


class Op:
    __slots__ = ("eng", "fn", "deps", "dma", "chan", "sig", "hasdep", "idx")


class Prog:
    ENGS = ("pe", "act", "dve", "pool", "sp")

    def __init__(self):
        self.ops = []
        self.chan_cnt = {}
        self.eng_cnt = {e: 0 for e in self.ENGS}
        self.final = []

    def add(self, eng, fn, reads=(), writes=(), dma=False, chan=None, final=False):
        op = Op()
        op.eng = eng
        op.fn = _freeze(fn)
        op.dma = dma
        op.hasdep = False
        op.sig = None
        op.idx = len(self.ops)
        deps = []
        for r in reads:
            if r.lw is not None:
                deps.append(r.lw)
        for w in writes:
            if w.lw is not None:
                deps.append(w.lw)
            deps.extend(w.rd)
        seen = set()
        od = []
        for d in deps:
            if d is op or id(d) in seen:
                continue
            seen.add(id(d))
            if eng == "pe" and d.eng == "pe" and not d.dma and not dma:
                continue
            d.hasdep = True
            od.append(d)
        op.deps = od
        for r in reads:
            r.rd.append(op)
        for w in writes:
            w.lw = op
            w.rd = []
        if dma:
            op.chan = chan if chan is not None else (writes[0] if writes else ("dma", eng))
        else:
            op.chan = None
        if dma:
            op.hasdep = True
        if final:
            op.hasdep = True
            self.final.append(op)
        self.ops.append(op)
        return op

    def emit(self, nc):
        chan_keys = []
        for op in self.ops:
            if not op.hasdep:
                continue
            if op.dma:
                k = op.chan if isinstance(op.chan, (str, tuple)) else id(op.chan)
                if k not in self.chan_cnt:
                    self.chan_cnt[k] = 0
                    chan_keys.append(k)
                self.chan_cnt[k] += 16
                op.sig = (("c", k), self.chan_cnt[k])
            else:
                self.eng_cnt[op.eng] += 1
                op.sig = (("e", op.eng), self.eng_cnt[op.eng])
        with contextlib.ExitStack() as st:
            sems = {}
            for e in self.ENGS:
                sems[("e", e)] = st.enter_context(nc.semaphore("s_" + e))
            for i, k in enumerate(chan_keys):
                sems[("c", k)] = st.enter_context(nc.semaphore("c_%d" % i))
            self.nsem = len(sems)
            block = st.enter_context(nc.Block())

            def run(engname, e):
                known = {}
                for op in self.ops:
                    if op.eng != engname:
                        continue
                    for d in op.deps:
                        sk, v = d.sig
                        if known.get(sk, 0) < v:
                            e.wait_ge(sems[sk], v)
                            known[sk] = v
                    ins = op.fn(e)
                    if op.sig is not None:
                        sk, v = op.sig
                        ins.then_inc(sems[sk], 16 if op.dma else 1)
                if engname == "sp":
                    fin = {}
                    for op in self.final:
                        sk, v = op.sig
                        fin[sk] = max(fin.get(sk, 0), v)
                    for sk, v in fin.items():
                        if known.get(sk, 0) < v:
                            e.wait_ge(sems[sk], v)
                            known[sk] = v

            @block.tensor
            def _(e):
                run("pe", e)

            @block.scalar
            def _(e):
                run("act", e)

            @block.vector
            def _(e):
                run("dve", e)

            @block.gpsimd
            def _(e):
                run("pool", e)

            @block.sync
            def _(e):
                run("sp", e)


class Rot:
    def __init__(self, alloc, name, n, shape, dt):
        self.t = [alloc("%s%d" % (name, i), shape, dt) for i in range(n)]
        self.r = [Res("%s%d" % (name, i)) for i in range(n)]
        self.i = 0

    def get(self):
        i = self.i
        self.i = (i + 1) % len(self.t)
        return self.t[i], self.r[i]


def build(S_TOK):
    assert S_TOK % MT == 0
    NTP = S_TOK // MT
    nc = bass.Bass("TRN2", target_bir_lowering=False)

    def din(name, shape, dt=F32):
        return nc.dram_tensor(name, list(shape), dt, kind="ExternalInput").ap()

    def dout(name, shape, dt=F32):
        return nc.dram_tensor(name, list(shape), dt, kind="ExternalOutput").ap()

    xp_d = din("xp", [S_TOK, D])
    xs_d = din("xs", [64, D])
    ck_d = din("ck", [2, 128, 128])
    cv_d = din("cv", [2, 128, 128])
    cmk_d = din("cmk", [2, 16, 128])
    cmv_d = din("cmv", [2, 16, 128])
    sconv_d = din("sconv", [128, 4, 2, 2])
    meta_d = din("meta", [16, D])
    gmix_d = din("gmix", [1, D])
    gmlp_d = din("gmlp", [1, D])
    gcv_d = din("gcv", [128, 4])
    gatt_d = din("gatt", [1, Q_DIM])
    gfin_d = din("gfin", [1, D])
    convw_d = din("convw", [128, 4, 3])
    sinks_d = din("sinks", [1, 8])
    table_d = din("table", [32, 8])
    oh_d = din("oh", [32, HLEN])
    win_d = din("win", [D, IN_X])
    wout_d = din("wout", [D, D])
    wup_d = din("wup", [D, D_FF])
    wdn_d = din("wdn", [D_FF, D])

    yp_d = dout("yp", [S_TOK, D])
    ys_d = dout("ys", [64, D])
    pk_d = dout("pk", [128, 128])
    pv_d = dout("pv", [128, 128])
    pmk_d = dout("pmk", [16, 128])
    pmv_d = dout("pmv", [16, 128])
    pconv_d = dout("pconv", [128, 4, 2])
    sk_d = dout("sk", [64, 128])
    sv_d = dout("sv", [64, 128])
    sconvo_d = dout("sconvo", [128, 4, 2, 2])

    gscr = nc.dram_tensor("gscr", [8, HLEN], BF16, kind="Internal").ap()
    wup_s = nc.dram_tensor("wup_s", [8, 128, 8, 512], BF16, kind="Internal").ap()
    wdn_s = nc.dram_tensor("wdn_s", [8, 128, 4, 1024], BF16, kind="Internal").ap()
    r_gscr = Res("gscr")
    r_wup_s = [Res("wup_s%d" % i) for i in range(8)]
    r_wdn_s = [Res("wdn_s%d" % i) for i in range(8)]

    P = Prog()
    with contextlib.ExitStack() as st:
        def sb(name, shape, dt):
            return st.enter_context(nc.sbuf_tensor("s_" + name, list(shape), dt))

        def ps(name, shape, dt):
            return st.enter_context(nc.psum_tensor("p_" + name, list(shape), dt))

        bank = [ps("bank%d" % i, [128, 512], F32) for i in range(8) if i != 2]
        bank.insert(2, None)
        TRb = ps("TRb", [128, 1024], BF16)
        r_bank = [Res("bank%d" % i) for i in range(8)]
        r_TR = r_bank[2]
        ssqc_ps = bank[3][:, 0:4]
        r_ssqc_ps = r_bank[3]
        gv_ps = bank[4]
        fmb_i = [0]

        def fmb_get():
            i = fmb_i[0]
            fmb_i[0] = 1 - i
            return bank[i], r_bank[i]

        xh = [[sb("xh%d%d" % (p, s), [128, D], F32) for s in range(2)] for p in range(3)]
        r_xh = [[Res("xh%d%d" % (p, s)) for s in range(2)] for p in range(3)]
        xsb = Rot(sb, "xsb", 2, [128, D], BF16)
        yaraw = Rot(sb, "yaraw", 2, [128, 512], F32)
        yas = Rot(sb, "yas", 2, [128, 512], BF16)

        ident = sb("ident", [128, 128], BF16)
        anti128 = sb("anti128", [128, 128], BF16)
        anti32 = sb("anti32", [32, 32], BF16)
        anti17 = sb("anti17", [17, 17], BF16)
        ones_col = sb("ones_col", [128, 1], BF16)
        iot = yaraw.t[0][:, 0:128]
        r_const = Res("const")
        r_iot = yaraw.r[0]
        P.add("pool", lambda e: e.iota(iot[:], pattern=[[1, 128]], base=0, channel_multiplier=-1,
                                       allow_small_or_imprecise_dtypes=True), writes=[r_iot])
        P.add("dve", lambda e: e.tensor_scalar(out=ident[:], in0=iot[:], scalar1=0.0, scalar2=None,
                                               op0=ALU.is_equal), reads=[r_iot], writes=[r_const])
        iot2 = yaraw.t[1][:, 0:128]
        r_iot2 = yaraw.r[1]
        P.add("pool", lambda e: e.iota(iot2[:], pattern=[[1, 128]], base=0, channel_multiplier=1,
                                       allow_small_or_imprecise_dtypes=True), writes=[r_iot2])
        P.add("dve", lambda e: e.tensor_scalar(out=anti128[:], in0=iot2[:], scalar1=127.0, scalar2=None,
                                               op0=ALU.is_equal), reads=[r_iot2], writes=[r_const])
        P.add("dve", lambda e: e.tensor_scalar(out=anti32[:], in0=iot2[0:32, 0:32], scalar1=31.0, scalar2=None,
                                               op0=ALU.is_equal), reads=[r_iot2], writes=[r_const])
        P.add("dve", lambda e: e.tensor_scalar(out=anti17[:], in0=iot2[0:17, 0:17], scalar1=15.0, scalar2=None,
                                               op0=ALU.is_equal), reads=[r_iot2], writes=[r_const])
        P.add("dve", lambda e: e.memset(ones_col[:], 1.0), writes=[r_const])

        gmix = sb("gmix", [128, D], F32)
        gmlp = sb("gmlp", [128, D], F32)
        gatt = sb("gatt", [128, Q_DIM], F32)
        gcv = sb("gcv", [128, 4], F32)
        convw = sb("convw", [128, 4, 3], F32)
        gfin = sb("gfin", [128, D], F32)
        table = sb("table", [32, 8], F32)
        ohs = xh[2][1][0:32, 0:HLEN]
        CBm0 = sb("CBm0", [17, 8], F32)
        CBm = sb("CBm", [17, 8], F32)
        r_small = Res("small")
        r_cb = Res("cb")
        for tl, src in ((gmix, gmix_d.partition_broadcast(128)), (gmlp, gmlp_d.partition_broadcast(128)),
                        (gatt, gatt_d.partition_broadcast(128)), (gcv, gcv_d), (convw, convw_d), (table, table_d),
                        ):
            P.add("sp", lambda e, tl=tl, src=src: e.dma_start(out=tl[:], in_=src), writes=[r_small], dma=True,
                  chan="small")
        P.add("sp", lambda e: e.dma_start(out=gfin[:], in_=gfin_d.partition_broadcast(128)), writes=[r_small],
              dma=True, chan="small")
        P.add("dve", lambda e: e.memset(CBm0[:], 0.0), writes=[r_cb])
        P.add("sp", lambda e: e.dma_start(out=CBm0[16:17, :], in_=sinks_d), writes=[r_cb], dma=True, chan="cb")
        P.add("sp", lambda e: e.dma_start(out=CBm[16:17, :], in_=sinks_d), writes=[r_cb], dma=True, chan="cb")
        P.add("sp", lambda e: e.dma_start(out=CBm[0:16, :], in_=table_d[15:16, :].partition_broadcast(16)),
              writes=[r_cb], dma=True, chan="cb")

        gvb = xsb.t[1][0:8, 0:HLEN]
        r_gvb = xsb.r[1]
        P.add("sp", lambda e: e.dma_start(out=ohs, in_=oh_d), writes=[r_xh[2][1]], dma=True)
        Hprev = sb("Hprev", [128, 8, 128], BF16)
        Hcur = sb("Hcur", [128, 8, 128], BF16)
        Hm0 = sb("Hm0", [17, 8, 128], BF16)
        Hsw = sb("Hsw", [128, 8, 32], BF16)
        Hsn = sb("Hsn", [32, 8, 32], BF16)
        r_H = Res("H")
        for c0 in range(0, HLEN, 512):
            c1 = min(HLEN, c0 + 512)
            P.add("pe", lambda e, c0=c0, c1=c1: e.matmul(gv_ps[0:8, 0:c1 - c0], lhsT=table[:], rhs=ohs[:, c0:c1],
                                                         start=True, stop=True),
                  reads=[r_small, r_xh[2][1]], writes=[r_bank[4]])
            P.add("dve", lambda e, c0=c0, c1=c1: e.tensor_copy(out=gvb[:, c0:c1], in_=gv_ps[0:8, 0:c1 - c0]),
                  writes=[r_bank[4], r_gvb])
        P.add("sp", lambda e: e.dma_start(out=gscr, in_=gvb), reads=[r_gvb], writes=[r_gscr], dma=True)
        P.add("dve", lambda e: e.memset(Hm0[:], 0.0), writes=[r_H])
        for tl, name in ((Hprev, "prev"), (Hcur, "cur"), (Hm0, "meta0"), (Hsw, "sw"), (Hsn, "sn")):
            off, nk, T, db = HOFF[name]
            src = bass.AP(gscr.tensor, off, [[1, nk], [HLEN, 8], [1, T]])
            P.add("sp", lambda e, tl=tl, src=src, nk=nk: e.dma_start(out=tl[0:nk, :, :], in_=src),
                  reads=[r_gscr], writes=[r_H], dma=True, chan="H")
        P.add("dve", lambda e: e.memset(Hprev[64:128, :, 64:128], NEG), writes=[r_H])
        P.add("dve", lambda e: e.memset(Hcur[0:64, :, 0:64], NEG), writes=[r_H])

        WIN = sb("WIN", [128, 8, IN_X], BF16)
        WOUT = sb("WOUT", [128, 8, D], BF16)
        r_WIN = [[Res("WIN%d_%d" % (k, c)) for c in range(2)] for k in range(8)]
        r_WOUT = [Res("WOUT%d" % c) for c in range(2)]

        def prep_weights():
            for k in range(8):
                for c in range(2):
                    P.add("pool", lambda e: e.dma_start(out=WIN[:, k, c * 1280:(c + 1) * 1280],
                                                        in_=win_d[k * 128:(k + 1) * 128, c * 1280:(c + 1) * 1280]),
                          writes=[r_WIN[k][c]], dma=True)
            for c in range(2):
                src = wout_d[:, c * 512:(c + 1) * 512].rearrange("(k p) c -> p k c", p=128)
                P.add("pool", lambda e: e.dma_start(out=WOUT[:, :, c * 512:(c + 1) * 512], in_=src),
                      writes=[r_WOUT[c]], dma=True)
            for i in range(8):
                src = wup_d[:, i * 512:(i + 1) * 512].rearrange("(k p) c -> p k c", p=128)
                P.add("pool", lambda e: e.dma_start(out=wup_s[i], in_=src), writes=[r_wup_s[i]], dma=True)
            for i in range(8):
                src = wdn_d[i * 512:(i + 1) * 512, :].rearrange("(k p) c -> p k c", p=128)
                P.add("pool", lambda e: e.dma_start(out=wdn_s[i], in_=src), writes=[r_wdn_s[i]], dma=True)


        junk = Rot(sb, "junk", 1, [128, D], BF16)
        _xsT = sb("xsT", [128, 8, MT], BF16)
        xsT = [_xsT, _xsT]
        _r_xsT = [Res("xsT%d" % s) for s in range(2)]
        r_xsT = [_r_xsT, _r_xsT]
        h1nT = sb("h1nT", [128, 8, MT], BF16)
        r_h1nT = [Res("h1nT%d" % s) for s in range(2)]
        qT = sb("qT", [128, 4, MT], BF16)
        r_qT = [Res("qT%d" % j) for j in range(4)]
        NRING = 4
        KKr = sb("KKr", [128, NRING, 2, 128], BF16)
        r_KKr = [Res("KKr%d" % i) for i in range(NRING)]
        Vr = sb("Vr", [128, NRING, 2, 65], BF16)
        r_Vr = [Res("Vr%d" % i) for i in range(NRING)]
        KKm = sb("KKm", [128, 2, 17], BF16)
        Vm = sb("Vm", [17, 2, 65], BF16)
        r_KKm = Res("KKm")
        r_Vm = Res("Vm")
        KKsm = sb("KKsm", [128, 2, 2, 17], BF16)
        Vsm = sb("Vsm", [17, 2, 2, 65], BF16)
        KKsw = sb("KKsw", [128, 2, 2, 128], BF16)
        Vsw = sb("Vsw", [128, 2, 2, 65], BF16)
        KKsn = sb("KKsn", [128, 2, 64], BF16)
        Vsn = sb("Vsn", [32, 2, 2, 65], BF16)
        r_scache = Res("scache")
        r_KKsn = Res("KKsn")
        r_Vsn = [Res("Vsn0"), Res("Vsn1")]
        ucx = [sb("ucx%d" % p, [128, 4, MT + 4], F32) for p in range(2)]
        r_ucx = [[Res("ucx%d%d" % (p, j)) for j in range(4)] for p in range(2)]
        r_ucctx = [Res("ucctx%d" % p) for p in range(2)]
        csb = Rot(sb, "csb", 2, [128, MT], F32)
        tmpy = Rot(sb, "tmpy", 2, [128, MT], F32)
        bsb = Rot(sb, "bsb", 2, [128, MT], F32)
        ycsq = sb("ycsq", [128, 4, MT], BF16)
        r_ycsq = [Res("ycsq%d" % j) for j in range(4)]
        ycT = sb("ycT", [128, 4, MT], BF16)
        r_ycT = [Res("ycT%d" % j) for j in range(4)]
        yaT = sb("yaT", [128, 4, MT], BF16)
        r_yaT = [Res("yaT%d" % s) for s in range(2)]
        stat = Rot(sb, "stat", 48, [128, 1], F32)
        rec8 = Rot(sb, "rec8", 2, [128, 8], F32)
        hidT = sb("hidT", [128, 32, MT], BF16)
        r_hidT = [Res("hidT%d" % f) for f in range(32)]
        rtmp = Rot(sb, "rtmp", 2, [128, 2 * MT], F32)
        ring = Rot(sb, "ring", 4, [128, 4096], BF16)
        kvst = Rot(sb, "kvst", 1, [128, 256], F32)
        kdup = yas.t[0][:].rearrange("p (s g a c) -> p s g a c", s=2, g=2, a=2)
        kmdup = yas.t[1][0:16, :].rearrange("p (s g a c) -> p s g a c", s=2, g=2, a=2)
        r_kmdup = yas.r[1]
        cstv = [rtmp.t[q][:].rearrange("p (a c) -> p a c", a=4) for q in range(2)]
        r_cstv = [rtmp.r[q] for q in range(2)]
        r_kdup = yas.r[0]

        P.add("pool", lambda e: e.memset(Vr[:], 1.0), writes=r_Vr)
        P.add("pool", lambda e: e.memset(Vm[:], 1.0), writes=[r_Vm])
        P.add("pool", lambda e: e.memset(Vm[:, :, 0:64], 0.0), writes=[r_Vm])
        P.add("pool", lambda e: e.memset(Vsm[:], 1.0), writes=[r_scache])
        P.add("pool", lambda e: e.memset(Vsm[:, :, :, 0:64], 0.0), writes=[r_scache])
        P.add("pool", lambda e: e.memset(Vsw[:], 1.0), writes=[r_scache])
        P.add("pool", lambda e: e.memset(Vsn[:], 1.0), writes=r_Vsn)
        P.add("pool", lambda e: e.memset(KKm[:], 0.0), writes=[r_KKm])
        P.add("pool", lambda e: e.memset(KKsm[:], 0.0), writes=[r_scache])
        P.add("pool", lambda e: e.memset(KKr[:], 0.0), writes=r_KKr)

        out_ops = []

        def rms_stats(src_ap, n_tok, nfeat, reads):
            jt, jr = junk.get()
            s1, r1 = stat.get()
            P.add("act", lambda e: e.activation(out=jt[0:n_tok, 0:nfeat], in_=src_ap, func=AF.Square,
                                                accum_out=s1[0:n_tok, :]), reads=reads, writes=[jr, r1])
            s2, r2 = stat.get()
            P.add("act", lambda e: e.activation(out=s2[0:n_tok, :], in_=s1[0:n_tok, :], func=AF.Ln,
                                                scale=1.0 / nfeat, bias=EPS), reads=[r1], writes=[r2])
            return s2, r2

        def exp_of(ln_t, ln_r, n_tok, scale):
            s3, r3 = stat.get()
            P.add("act", lambda e: e.activation(out=s3[0:n_tok, :], in_=ln_t[0:n_tok, :], func=AF.Exp, scale=scale),
                  reads=[ln_r], writes=[r3])
            return s3, r3

        def norm_part1(src_t, src_r, n_tok, gain):
            ln_t, ln_r = rms_stats(src_t[0:n_tok, :], n_tok, D, [src_r])
            rs_t, rs_r = exp_of(ln_t, ln_r, n_tok, -0.5)
            xb, xbr = xsb.get()
            P.add("dve", lambda e: e.scalar_tensor_tensor(out=xb[0:n_tok, :], in0=src_t[0:n_tok, :],
                                                          scalar=rs_t[0:n_tok, :], in1=gain[0:n_tok, :],
                                                          op0=ALU.mult, op1=ALU.mult),
                  reads=[src_r, rs_r, r_small], writes=[xbr])
            return xb, xbr

        def norm_part2(xb, xbr, n_tok, dstT, dst_r, col0):
            for k in range(8):
                P.add("pe", lambda e, k=k: e.transpose(out=TRb[:, k * 128:k * 128 + n_tok],
                                                       in_=xb[0:n_tok, k * 128:(k + 1) * 128],
                                                       identity=ident[0:n_tok, 0:n_tok]),
                      reads=[xbr, r_const], writes=[r_TR])
            src = TRb[:].rearrange("p (k t) -> p k t", k=8)[:, :, 0:n_tok]
            P.add("dve", lambda e: e.tensor_copy(out=dstT[:, :, col0:col0 + n_tok], in_=src),
                  writes=[r_TR, dst_r])

        def norm_transpose(src_t, src_r, n_tok, dstT, dst_r, col0, gain):
            xb, xbr = norm_part1(src_t, src_r, n_tok, gain)
            norm_part2(xb, xbr, n_tok, dstT, dst_r, col0)


        def inproj_cols(xT, xT_rs, T, cols):
            bt, br = fmb_get()
            for i, col in enumerate(cols):
                for k in range(8):
                    P.add("pe", lambda e, k=k, i=i, col=col: e.matmul(
                        bt[:, i * 256:i * 256 + T], lhsT=WIN[:, k, col:col + 128], rhs=xT[:, k, 0:T],
                        start=(k == 0), stop=(k == 7)), reads=r_WIN[k] + xT_rs, writes=[br])
            return bt, br

        def kv_tokmajor(xT, xT_r, col0, n_tok, bi):
            for k in range(8):
                P.add("pe", lambda e, k=k: e.matmul(bank[bi][0:n_tok, 0:256], lhsT=xT[:, k, col0:col0 + n_tok],
                                                    rhs=WIN[:, k, 2048:2304], start=(k == 0), stop=(k == 7)),
                      reads=r_WIN[k] + [xT_r], writes=[r_bank[bi]])

        PTh = Rot(sb, "PTh", 3, [128, 3, 128], BF16)

        def attention(T, qcol, groups_fn):
            pend = None
            for h in range(8):
                gl = groups_fn(h)
                hp = h % 2
                sbk = bank[h % 2]
                sbr = r_bank[h % 2]
                for g in gl:
                    nk = g["nk"]
                    sl = g["slot"]
                    P.add("pe", lambda e, g=g, nk=nk, sl=sl, hp=hp, h=h: e.matmul(
                        sbk[0:nk, sl * 128:sl * 128 + T], lhsT=g["kk"],
                        rhs=qT[hp * 64:(hp + 1) * 64, h // 2, qcol:qcol + T], start=True, stop=(g["hank"] is None)),
                        reads=g["kk_rs"] + [r_qT[h // 2]], writes=[sbr])
                    if g["hank"] is not None:
                        P.add("pe", lambda e, g=g, nk=nk, sl=sl: e.matmul(
                            sbk[0:nk, sl * 128:sl * 128 + T], lhsT=g["anti"], rhs=g["hank"], start=False, stop=True),
                            reads=[r_H, r_const], writes=[sbr])
                pt, pr = PTh.get()
                full = [g for g in gl if g["nk"] == 128 and g["cb"] is None]
                rest = [g for g in gl if not (g["nk"] == 128 and g["cb"] is None)]
                if full:
                    s0 = min(g["slot"] for g in full)
                    s1 = max(g["slot"] for g in full) + 1
                    assert s1 - s0 == len(full)
                    if T == 128:
                        P.add("act", lambda e, pt=pt, s0=s0, s1=s1: e.activation(
                            out=pt[:, s0:s1, :], in_=sbk[:, s0 * 128:s1 * 128].rearrange("p (a c) -> p a c", c=128),
                            func=AF.Exp), reads=[], writes=[sbr, pr])
                    else:
                        for g in full:
                            sl = g["slot"]
                            P.add("act", lambda e, pt=pt, sl=sl: e.activation(
                                out=pt[:, sl, 0:T], in_=sbk[:, sl * 128:sl * 128 + T], func=AF.Exp),
                                writes=[sbr, pr])
                for g in rest:
                    nk = g["nk"]
                    sl = g["slot"]
                    if g["cb"] is None:
                        P.add("act", lambda e, pt=pt, sl=sl, nk=nk: e.activation(
                            out=pt[0:nk, sl, 0:T], in_=sbk[0:nk, sl * 128:sl * 128 + T], func=AF.Exp),
                            writes=[sbr, pr])
                    else:
                        P.add("act", lambda e, pt=pt, sl=sl, nk=nk, g=g: e.activation(
                            out=pt[0:nk, sl, 0:T], in_=sbk[0:nk, sl * 128:sl * 128 + T], func=AF.Exp, bias=g["cb"]),
                            reads=[r_cb], writes=[sbr, pr])
                if pend is not None:
                    emit_pv(*pend)
                pend = (h, T, gl, pt, pr)
            emit_pv(*pend)

        def emit_pv(h, T, gl, pt, pr):
            ob = bank[4 + h // 4]
            obr = r_bank[4 + h // 4]
            hh = h % 4
            n = len(gl)
            for i, g in enumerate(gl):
                nk = g["nk"]
                sl = g["slot"]
                P.add("pe", lambda e, g=g, nk=nk, i=i, sl=sl: e.matmul(
                    ob[0:T, hh * 65:(hh + 1) * 65], lhsT=pt[0:nk, sl, 0:T], rhs=g["v"], start=(i == 0), stop=(i == n - 1)),
                    reads=[pr] + g["v_rs"], writes=[obr])


        def attn_E1(T):
            rc, rcr = rec8.get()
            yr, yrr = yaraw.get()
            for b in range(2):
                ob = bank[4 + b]
                o3 = ob[0:T, 0:260].rearrange("p (h c) -> p h c", h=4)
                P.add("dve", lambda e, o3=o3, b=b: e.reciprocal(out=rc[0:T, 4 * b:4 * b + 4], in_=o3[:, :, 64]),
                      writes=[r_bank[4 + b], rcr])
                P.add("dve", lambda e, o3=o3, b=b: e.tensor_tensor(
                    out=yr[0:T, 256 * b:256 * (b + 1)].rearrange("p (h c) -> p h c", h=4),
                    in0=o3[:, :, 0:64],
                    in1=rc[0:T, 4 * b:4 * b + 4].unsqueeze(2).to_broadcast([T, 4, 64]),
                    op=ALU.mult), reads=[rcr], writes=[r_bank[4 + b], yrr])
            return yr, yrr

        def attn_E2a(T, yr, yrr, lnc_t, lnc_r):
            lna_t, lna_r = rms_stats(yr[0:T, :], T, Q_DIM, [yrr])
            d_t, d_r = stat.get()
            P.add("dve", lambda e: e.tensor_tensor(out=d_t[0:T, :], in0=lnc_t[0:T, :], in1=lna_t[0:T, :],
                                                   op=ALU.subtract), reads=[lnc_r, lna_r], writes=[d_r])
            ratio_t, ratio_r = exp_of(d_t, d_r, T, 0.5)
            rstdc_t, rstdc_r = exp_of(lnc_t, lnc_r, T, -0.5)
            ys, ysr = yas.get()
            P.add("dve", lambda e: e.scalar_tensor_tensor(out=ys[0:T, :], in0=yr[0:T, :], scalar=ratio_t[0:T, :],
                                                          in1=gatt[0:T, :], op0=ALU.mult, op1=ALU.mult),
                  reads=[yrr, ratio_r, r_small], writes=[ysr])
            return ys, ysr, rstdc_t, rstdc_r

        def attn_E2b(T, ys, ysr, yaT_col0, r_yaT_s):
            for j in range(4):
                P.add("pe", lambda e, j=j: e.transpose(out=TRb[:, j * 128:j * 128 + T], in_=ys[0:T, j * 128:(j + 1) * 128],
                                                       identity=ident[0:T, 0:T]),
                      reads=[ysr, r_const], writes=[r_TR])
            src = TRb[:, 0:512].rearrange("p (k t) -> p k t", k=4)[:, :, 0:T]
            P.add("dve", lambda e: e.tensor_copy(out=yaT[:, :, yaT_col0:yaT_col0 + T], in_=src),
                  writes=[r_TR, r_yaT_s])


        ring_state = {"pending": [], "left": 16 * (NTP + 1)}

        def mlp_weights_iter():
            while True:
                for i in range(8):
                    yield ("up", i)
                for i in range(8):
                    yield ("dn", i)

        wgen = mlp_weights_iter()

        def issue_wload():
            if ring_state["left"] <= 0:
                return
            ring_state["left"] -= 1
            kind, i = next(wgen)
            t, r = ring.get()
            src = (wup_s if kind == "up" else wdn_s)[i]
            rs = (r_wup_s if kind == "up" else r_wdn_s)[i]
            if kind == "up":
                dstv = t[:].rearrange("p (k c) -> p k c", k=8)
            else:
                dstv = t[:].rearrange("p (k c) -> p k c", k=4)
            P.add("pool", lambda e, dstv=dstv, src=src: e.dma_start(out=dstv, in_=src), reads=[rs], writes=[r], dma=True)
            ring_state["pending"].append((kind, i, t, r))

        def take_wload(kind, i):
            k2, i2, t, r = ring_state["pending"].pop(0)
            assert (k2, i2) == (kind, i)
            return t, r

        class Tile:
            pass

        def stage_A0(tl, s):
            p = tl.xp
            c0, n = tl.subs[s]
            src = tl.xsrc(s)
            P.add("sp", lambda e: e.dma_start(out=xh[p][s][0:n, :], in_=src), writes=[r_xh[p][s]], dma=True)

        def stage_A1(tl, s):
            p = tl.xp
            c0, n = tl.subs[s]
            tl.a1[s] = norm_part1(xh[p][s], r_xh[p][s], n, gmix)

        def stage_A2(tl, s):
            p = tl.par
            c0, n = tl.subs[s]
            xb, xbr = tl.a1[s]
            norm_part2(xb, xbr, n, xsT[p], r_xsT[p][s], c0)

        def stage_A(tl):
            tl.a1 = [None] * len(tl.subs)
            for s in range(len(tl.subs)):
                stage_A0(tl, s)
                stage_A1(tl, s)
                stage_A2(tl, s)


        def stage_B(tl):
            for _ in stage_B_gen(tl):
                pass

        def stage_B_gen(tl):
            if tl.kind == "sample":
                sample_ctx_load()
            p = tl.par
            T = tl.T
            xT = xsT[p]
            xT_rs = [r_xsT[p][s] for s in range(len(tl.subs))]
            bt, br = inproj_cols(xT, xT_rs, T, [2304, 2432])
            if tl.kind == "prompt":
                s0 = (2 * tl.t) % NRING
                for g in range(2):
                    P.add("dve", lambda e, bt=bt, g=g, s0=s0: e.tensor_copy(
                        out=KKr[:, s0:s0 + 2, g, :], in_=bt[:, g * 256:(g + 1) * 256].rearrange("p (a c) -> p a c", a=2)),
                        writes=[br, r_KKr[s0], r_KKr[s0 + 1]])
            elif tl.kind == "meta":
                for g in range(2):
                    P.add("dve", lambda e, bt=bt, g=g: e.tensor_copy(out=KKm[:, g, 0:16], in_=bt[:, g * 256:g * 256 + 16]),
                          writes=[br, r_KKm])
            else:
                for g in range(2):
                    P.add("dve", lambda e, bt=bt, g=g: e.tensor_copy(out=KKsn[:, g, :], in_=bt[:, g * 256:g * 256 + 64]),
                          writes=[br, r_KKsn])
            yield
            for s, (c0, n) in enumerate(tl.subs):
                if s > 0:
                    yield
                bi = 6 + s
                kv_tokmajor(xT, r_xsT[p][s], c0, n, bi)
                src_v = bank[bi][0:n, 128:256].rearrange("p (g c) -> p g c", g=2)
                if tl.kind == "prompt":
                    sl = (2 * tl.t + s) % NRING
                    P.add("act", lambda e, sl=sl, src_v=src_v: e.activation(out=Vr[:, sl, :, 0:64], in_=src_v, func=AF.Copy),
                          writes=[r_bank[bi], r_Vr[sl]])
                elif tl.kind == "meta":
                    P.add("act", lambda e, src_v=src_v: e.activation(out=Vm[0:16, :, 0:64], in_=src_v, func=AF.Copy),
                          writes=[r_bank[bi], r_Vm])
                else:
                    P.add("act", lambda e, s=s, src_v=src_v: e.activation(out=Vsn[0:32, s, :, 0:64], in_=src_v, func=AF.Copy),
                          writes=[r_bank[bi], r_Vsn[s]])
                outs = tl.kv_out(s)
                if outs is not None:
                    kd, vd = outs
                    kt, kr = kvst.get()
                    P.add("dve", lambda e, kt=kt, n=n, bi=bi: e.tensor_copy(out=kt[0:n, :], in_=bank[bi][0:n, 0:256]),
                          writes=[r_bank[bi], kr])
                    out_ops.append(P.add("sp", lambda e, kt=kt, n=n, kd=kd: e.dma_start(out=kd, in_=kt[0:n, 0:128]),
                                         reads=[kr], dma=True, chan=("kvo", id(kr), 0), final=True))
                    out_ops.append(P.add("sp", lambda e, kt=kt, n=n, vd=vd: e.dma_start(out=vd, in_=kt[0:n, 128:256]),
                                         reads=[kr], dma=True, chan=("kvo", id(kr), 1), final=True))
            L_ = tl.L
            nseg = T // L_
            W = L_ + 2
            for j in range(4):
                yield
                ba, bar = inproj_cols(xT, xT_rs, T, [512 + 128 * j, 1024 + 128 * j])
                ct, cr = csb.get()
                P.add("act", lambda e, ct=ct, ba=ba: e.activation(out=ct[:, 0:T], in_=ba[:, 0:T], func=AF.Copy),
                      writes=[bar, cr])
                ucv = ucx[p][:, j, 0:nseg * W].rearrange("p (a w) -> p a w", a=nseg)
                P.add("dve", lambda e, ct=ct, ba=ba, ucv=ucv: e.tensor_tensor(
                    out=ucv[:, :, 2:2 + L_], in0=ba[:, 256:256 + T].rearrange("p (a w) -> p a w", a=nseg),
                    in1=ct[:, 0:T].rearrange("p (a w) -> p a w", a=nseg), op=ALU.mult),
                    reads=[cr], writes=[bar, r_ucx[p][j]])
                if tl.kind == "meta":
                    continue
                yield
                bb, bbr = inproj_cols(xT, xT_rs, T, [0 + 128 * j, 1536 + 128 * j])
                bs, bsr = bsb.get()
                P.add("act", lambda e, bs=bs, bb=bb: e.activation(out=bs[:, 0:T], in_=bb[:, 0:T], func=AF.Copy),
                      writes=[bbr, bsr])
                P.add("act", lambda e, bb=bb, j=j: e.activation(out=qT[:, j, 0:T], in_=bb[:, 256:256 + T], func=AF.Copy,
                                                               scale=0.125), writes=[bbr, r_qT[j]])
                ty, tyr = tmpy.get()
                tyv = ty[:, 0:T].rearrange("p (a w) -> p a w", a=nseg)
                P.add("dve", lambda e, tyv=tyv, ucv=ucv, j=j: e.tensor_scalar(
                    out=tyv, in0=ucv[:, :, 0:L_], scalar1=convw[:, j, 0:1], scalar2=None, op0=ALU.mult),
                    reads=[r_ucx[p][j], r_ucctx[p], r_small], writes=[tyr])
                for tap in (1, 2):
                    P.add("dve", lambda e, tyv=tyv, ucv=ucv, j=j, tap=tap: e.scalar_tensor_tensor(
                        out=tyv, in0=ucv[:, :, tap:tap + L_], scalar=convw[:, j, tap:tap + 1], in1=tyv,
                        op0=ALU.mult, op1=ALU.add), reads=[r_ucx[p][j], r_ucctx[p], r_small], writes=[tyr])
                P.add("dve", lambda e, ty=ty, bs=bs, j=j: e.tensor_tensor(out=ty[:, 0:T], in0=ty[:, 0:T],
                                                                          in1=bs[:, 0:T], op=ALU.mult),
                      reads=[bsr], writes=[tyr])
                P.add("dve", lambda e, ty=ty, j=j: e.tensor_scalar(out=ycT[:, j, 0:T], in0=ty[:, 0:T],
                                                                   scalar1=gcv[:, j:j + 1], scalar2=None,
                                                                   op0=ALU.mult),
                      reads=[tyr, r_small], writes=[r_ycT[j]])
                P.add("act", lambda e, ty=ty, j=j: e.activation(out=ycsq[:, j, 0:T], in_=ty[:, 0:T], func=AF.Square),
                      reads=[tyr], writes=[r_ycsq[j]])
            yield
            tl.after_conv()
            tl.lnc = []
            if tl.kind != "meta":
                for s, (c0, n) in enumerate(tl.subs):
                    for j in range(4):
                        P.add("pe", lambda e, j=j, c0=c0, n=n, s=s: e.matmul(
                            ssqc_ps[0:n, s:s + 1], lhsT=ycsq[:, j, c0:c0 + n], rhs=ones_col[:, 0:1],
                            start=(j == 0), stop=(j == 3)), reads=[r_ycsq[j], r_const], writes=[r_ssqc_ps])
                    sc, scr = stat.get()
                    P.add("dve", lambda e, sc=sc, n=n, s=s: e.tensor_copy(out=sc[0:n, :], in_=ssqc_ps[0:n, s:s + 1]),
                          writes=[r_ssqc_ps, scr])
                    l2, l2r = stat.get()
                    P.add("act", lambda e, sc=sc, l2=l2, n=n: e.activation(out=l2[0:n, :], in_=sc[0:n, :], func=AF.Ln,
                                                                         scale=1.0 / CONV_DIM, bias=EPS),
                          reads=[scr], writes=[l2r])
                    tl.lnc.append((l2, l2r))

        def stage_D(tl, s):
            c0, n = tl.subs[s]
            attention(n, c0, lambda h: tl.groups(s, h))
            tl.e1[s] = attn_E1(n)

        def stage_E2a(tl, s):
            c0, n = tl.subs[s]
            yr, yrr = tl.e1[s]
            tl.e2[s] = attn_E2a(n, yr, yrr, tl.lnc[s][0], tl.lnc[s][1])

        def stage_E2b(tl, s):
            c0, n = tl.subs[s]
            ys, ysr, _, _ = tl.e2[s]
            attn_E2b(n, ys, ysr, c0, r_yaT[s])

        def stage_F(tl, s):
            p = tl.xp
            c0, n = tl.subs[s]
            _, _, rstdc_t, rstdc_r = tl.e2[s]
            for half in range(2):
                ob = bank[half]
                for k in range(8):
                    if k < 4:
                        lhsT = ycT[:, k, c0:c0 + n]
                        rr = r_ycT[k]
                    else:
                        lhsT = yaT[:, k - 4, c0:c0 + n]
                        rr = r_yaT[s]
                    P.add("pe", lambda e, k=k: e.matmul(
                        ob[0:n, :], lhsT=lhsT, rhs=WOUT[:, k, half * 512:(half + 1) * 512],
                        start=(k == 0), stop=(k == 7)), reads=[rr, r_WOUT[half]], writes=[r_bank[half]])
                P.add("dve", lambda e: e.scalar_tensor_tensor(
                    out=xh[p][s][0:n, half * 512:(half + 1) * 512], in0=ob[0:n, :], scalar=rstdc_t[0:n, :],
                    in1=xh[p][s][0:n, half * 512:(half + 1) * 512], op0=ALU.mult, op1=ALU.add),
                    reads=[rstdc_r], writes=[r_bank[half], r_xh[p][s]])

        def stage_G1(tl, s):
            p = tl.xp
            c0, n = tl.subs[s]
            tl.g1[s] = norm_part1(xh[p][s], r_xh[p][s], n, gmlp)

        def stage_G2(tl, s):
            c0, n = tl.subs[s]
            xb, xbr = tl.g1[s]
            norm_part2(xb, xbr, n, h1nT, r_h1nT[s], c0)

        sq_i = [0]


        def stage_H(tl, hooks=None):
            T = tl.T
            hr = [r_h1nT[s] for s in range(len(tl.subs))]
            for piece in range(8):
                wt_, wr = take_wload("up", piece)
                wt = wt_[:].rearrange("p (k c) -> p k c", k=8)
                for pair in range(2):
                    f0 = piece * 4 + pair * 2
                    bt, br = fmb_get()
                    for i in range(2):
                        fi = pair * 2 + i
                        for k in range(8):
                            P.add("pe", lambda e, bt=bt, wt=wt, k=k, fi=fi, i=i: e.matmul(
                                bt[:, i * 256:i * 256 + T], lhsT=wt[:, k, fi * 128:(fi + 1) * 128], rhs=h1nT[:, k, 0:T],
                                start=(k == 0), stop=(k == 7)), reads=[wr] + hr, writes=[br])
                    rt, rr = rtmp.get()
                    rtv = rt[:].rearrange("p (a c) -> p a c", a=2)
                    P.add("act", lambda e, rtv=rtv, bt=bt: e.activation(
                        out=rtv[:, :, 0:T], in_=bt[:].rearrange("p (a c) -> p a c", a=2)[:, :, 0:T], func=AF.Relu),
                        writes=[br, rr])
                    eng = "dve"
                    sq_i[0] += 1
                    P.add(eng, lambda e, rtv=rtv, f0=f0: e.tensor_tensor(
                        out=hidT[:, f0:f0 + 2, 0:T], in0=rtv[:, :, 0:T], in1=rtv[:, :, 0:T], op=ALU.mult),
                        reads=[rr], writes=[r_hidT[f0], r_hidT[f0 + 1]])
                issue_wload()
                if hooks and piece in hooks:
                    for fn in hooks[piece]:
                        fn()


        def stage_I_piece(tl, piece):
            p = tl.xp
            wt_, wr = take_wload("dn", piece)
            wt = wt_[:].rearrange("p (k c) -> p k c", k=4)
            for s, (c0, n) in enumerate(tl.subs):
                for half in range(2):
                    ob = bank[4 + 2 * s + half]
                    for kc in range(4):
                        f = piece * 4 + kc
                        P.add("pe", lambda e, kc=kc, f=f: e.matmul(
                            ob[0:n, :], lhsT=hidT[:, f, c0:c0 + n], rhs=wt[:, kc, half * 512:(half + 1) * 512],
                            start=(piece == 0 and kc == 0), stop=(piece == 7 and kc == 3)),
                            reads=[wr, r_hidT[f]], writes=[r_bank[4 + 2 * s + half]])
            issue_wload()
            if piece == 7:
                for s, (c0, n) in enumerate(tl.subs):
                    for half in range(2):
                        ob = bank[4 + 2 * s + half]
                        P.add("dve", lambda e: e.tensor_tensor(
                            out=xh[p][s][0:n, half * 512:(half + 1) * 512], in0=ob[0:n, :],
                            in1=xh[p][s][0:n, half * 512:(half + 1) * 512], op=ALU.add),
                            writes=[r_bank[4 + 2 * s + half], r_xh[p][s]])


        def stage_J(tl):
            p = tl.xp
            for s, (c0, n) in enumerate(tl.subs):
                ln_t, ln_r = rms_stats(xh[p][s][0:n, :], n, D, [r_xh[p][s]])
                rs_t, rs_r = exp_of(ln_t, ln_r, n, -0.5)
                yt, yr = xh[p][s], r_xh[p][s]
                P.add("dve", lambda e, yt=yt, n=n, s=s, rs_t=rs_t: e.scalar_tensor_tensor(
                    out=yt[0:n, :], in0=xh[p][s][0:n, :], scalar=rs_t[0:n, :], in1=gfin[0:n, :], op0=ALU.mult,
                    op1=ALU.mult), reads=[r_xh[p][s], rs_r, r_small], writes=[yr])
                dst = tl.ydst(s)
                out_ops.append(P.add("sp", lambda e, yt=yt, n=n, dst=dst: e.dma_start(out=dst, in_=yt[0:n, :]),
                                     reads=[yr], dma=True, chan=("yo", id(yr)), final=True))

        tiles = []
        mt = Tile()
        mt.kind = "meta"
        mt.par = 1
        mt.xp = 2
        mt.T = 16
        mt.L = 16
        mt.subs = [(0, 16)]
        mt.xsrc = lambda s: meta_d
        mt.kv_out = lambda s: (pmk_d, pmv_d)

        def meta_after():
            P.add("dve", lambda e: e.tensor_copy(out=ucx[0][:, :, 0:2], in_=ucx[1][:, :, 16:18]),
                  reads=r_ucx[1], writes=[r_ucctx[0]])
        mt.after_conv = meta_after
        tiles.append(mt)

        for t in range(NTP):
            tl = Tile()
            tl.kind = "prompt"
            tl.t = t
            tl.par = t % 2
            tl.xp = t % 3
            tl.T = MT
            tl.L = MT
            tl.subs = [(0, 128), (128, 128)]
            tl.xsrc = lambda s, t=t: xp_d[t * MT + s * 128: t * MT + (s + 1) * 128, :]
            tl.ydst = lambda s, t=t: yp_d[t * MT + s * 128: t * MT + (s + 1) * 128, :]
            if t == NTP - 1:
                tl.kv_out = lambda s: (pk_d, pv_d) if s == 1 else None
            else:
                tl.kv_out = lambda s: None

            def after(t=t, tl=tl):
                p = tl.par
                if t == NTP - 1:
                    out_ops.append(P.add("sp", lambda e: e.dma_start(out=pconv_d, in_=ucx[p][:, :, MT:MT + 2]),
                                         reads=r_ucx[p], dma=True, chan="pconv", final=True))
                else:
                    P.add("dve", lambda e: e.tensor_copy(out=ucx[1 - p][:, :, 0:2], in_=ucx[p][:, :, MT:MT + 2]),
                          reads=r_ucx[p], writes=[r_ucctx[1 - p]])
            tl.after_conv = after

            def groups(s, h, t=t):
                bi = 2 * t + s
                hp = h % 2
                g = h // 4
                gl = []
                gl.append(dict(kk=KKm[hp * 64:(hp + 1) * 64, g, 0:17], kk_rs=[r_KKm], v=Vm[0:17, g, :], v_rs=[r_Vm], nk=17,
                               hank=(Hm0[0:17, h, :] if bi == 0 else None), anti=anti17[:],
                               cb=(CBm0[0:17, h:h + 1] if bi == 0 else CBm[0:17, h:h + 1]), slot=2))
                if bi >= 1:
                    sl = (bi - 1) % NRING
                    gl.append(dict(kk=KKr[hp * 64:(hp + 1) * 64, sl, g, :], kk_rs=[r_KKr[sl]], v=Vr[:, sl, g, :],
                                   v_rs=[r_Vr[sl]], nk=128, hank=Hprev[:, h, :], anti=anti128[:], cb=None, slot=0))
                sl = bi % NRING
                gl.append(dict(kk=KKr[hp * 64:(hp + 1) * 64, sl, g, :], kk_rs=[r_KKr[sl]], v=Vr[:, sl, g, :],
                               v_rs=[r_Vr[sl]], nk=128, hank=Hcur[:, h, :], anti=anti128[:], cb=None, slot=1))
                return gl
            tl.groups = groups
            tiles.append(tl)

        stl = Tile()
        stl.kind = "sample"
        stl.par = NTP % 2
        stl.xp = NTP % 3
        stl.T = 64
        stl.L = 32
        stl.subs = [(0, 32), (32, 32)]
        stl.xsrc = lambda s: xs_d[32 * s:32 * (s + 1), :]
        stl.ydst = lambda s: ys_d[32 * s:32 * (s + 1), :]
        stl.kv_out = lambda s: (sk_d[32 * s:32 * (s + 1), :], sv_d[32 * s:32 * (s + 1), :])

        def sample_after():
            p = stl.par
            v = ucx[p][:, :, 0:68].rearrange("p j (a w) -> p j a w", a=2)
            for j in range(4):
                out_ops.append(P.add("sp", lambda e, j=j: e.dma_start(out=sconvo_d[:, j, :, :], in_=v[:, j, :, 32:34]),
                                     reads=r_ucx[p], dma=True, chan="sconvo", final=True))
        stl.after_conv = sample_after

        def sgroups(s, h):
            hp = h % 2
            g = h // 4
            return [
                dict(kk=KKsm[hp * 64:(hp + 1) * 64, s, g, 0:17], kk_rs=[r_scache], v=Vsm[0:17, s, g, :], v_rs=[r_scache],
                     nk=17, hank=None, anti=None, cb=CBm[0:17, h:h + 1], slot=2),
                dict(kk=KKsw[hp * 64:(hp + 1) * 64, s, g, :], kk_rs=[r_scache], v=Vsw[:, s, g, :], v_rs=[r_scache],
                     nk=128, hank=Hsw[:, h, :], anti=anti128[:], cb=None, slot=0),
                dict(kk=KKsn[hp * 64:(hp + 1) * 64, g, 32 * s:32 * (s + 1)], kk_rs=[r_KKsn], v=Vsn[0:32, s, g, :],
                     v_rs=[r_Vsn[s]], nk=32, hank=Hsn[0:32, h, :], anti=anti32[:], cb=None, slot=1),
            ]
        stl.groups = sgroups
        tiles.append(stl)

        def sample_ctx_load():
            p = stl.par
            v = ucx[p][:, :, 0:68].rearrange("p j (a w) -> p j a w", a=2)
            for j in range(4):
                P.add("sp", lambda e, j=j: e.dma_start(out=v[:, j, :, 0:2], in_=sconv_d[:, j, :, :]),
                      writes=[r_ucctx[p]] + r_ucx[p], dma=True, chan="sctx")

        def sample_cache_prep():
            p = stl.par
            for i, src in enumerate((ck_d, cv_d)):
                for sq in range(2):
                    P.add("sp", lambda e, i=i, src=src, sq=sq: e.dma_start(out=cstv[sq][:, i, :], in_=src[sq]),
                          writes=[r_cstv[sq]], dma=True, chan=("cst", sq))
            for i, src in enumerate((cmk_d, cmv_d)):
                for sq in range(2):
                    P.add("sp", lambda e, i=i, src=src, sq=sq: e.dma_start(out=cstv[sq][0:16, 2 + i, :], in_=src[sq]),
                          writes=[r_cstv[sq]], dma=True, chan=("cst", sq))
            for sq in range(2):
                P.add("dve", lambda e, sq=sq: e.tensor_copy(out=Vsw[:, sq, :, 0:64],
                                                            in_=cstv[sq][:, 1, :].rearrange("k (g c) -> k g c", g=2)),
                      reads=[r_cstv[sq]], writes=[r_scache])
                P.add("dve", lambda e, sq=sq: e.tensor_copy(out=Vsm[0:16, sq, :, 0:64],
                                                            in_=cstv[sq][0:16, 3, :].rearrange("k (g c) -> k g c", g=2)),
                      reads=[r_cstv[sq]], writes=[r_scache])
                for cp in range(2):
                    P.add("dve", lambda e, cp=cp, sq=sq: e.tensor_copy(
                        out=kdup[:, sq, :, cp, :], in_=cstv[sq][:, 0, :].rearrange("k (g c) -> k g c", g=2)),
                        reads=[r_cstv[sq]], writes=[r_kdup])
                    P.add("dve", lambda e, cp=cp, sq=sq: e.tensor_copy(
                        out=kmdup[:, sq, :, cp, :], in_=cstv[sq][0:16, 2, :].rearrange("k (g c) -> k g c", g=2)),
                        reads=[r_cstv[sq]], writes=[r_kmdup])
            for s in range(2):
                for g in range(2):
                    P.add("pe", lambda e, s=s, g=g: e.transpose(
                        out=TRb[:, 0:128], in_=kdup[:, s, g, :, :].rearrange("k a c -> k (a c)"), identity=ident[:]),
                        reads=[r_kdup, r_const], writes=[r_TR])
                    P.add("dve", lambda e, s=s, g=g: e.tensor_copy(out=KKsw[:, s, g, :], in_=TRb[:, 0:128]),
                          writes=[r_TR, r_scache])
                    P.add("pe", lambda e, s=s, g=g: e.transpose(
                        out=TRb[:, 0:16], in_=kmdup[:, s, g, :, :].rearrange("k a c -> k (a c)"), identity=ident[0:16, 0:16]),
                        reads=[r_kmdup, r_const], writes=[r_TR])
                    P.add("dve", lambda e, s=s, g=g: e.tensor_copy(out=KKsm[:, s, g, 0:16], in_=TRb[:, 0:16]),
                          writes=[r_TR, r_scache])

        prep_weights()
        stage_A(tiles[0])
        stage_B(tiles[0])
        stage_A(tiles[1])
        sample_cache_prep()
        for _ in range(4):
            issue_wload()
        full = tiles[1:]
        prev = None
        stage_B(full[0])
        for i, tl in enumerate(full):
            nsub = len(tl.subs)
            tl.e1 = [None] * nsub
            tl.e2 = [None] * nsub
            tl.g1 = [None] * nsub
            nx = full[i + 1] if i + 1 < len(full) else None
            if nx is not None:
                nx.a1 = [None] * len(nx.subs)
                for s in range(len(nx.subs)):
                    stage_A0(nx, s)
            for s in range(nsub):
                stage_D(tl, s)
            seq = []
            if nx is not None:
                seq += [("n", stage_A1, 0), ("n", stage_A1, 1)]
            seq += [("m", stage_E2a, 0), ("m", stage_E2a, 1), ("i",), ("m", stage_E2b, 0), ("i",), ("m", stage_F, 0),
                    ("m", stage_E2b, 1)]
            if nx is not None:
                seq += [("n", stage_A2, 0), ("i",), ("n", stage_A2, 1)]
            else:
                seq += [("i",)]
            seq += [("m", stage_G1, 0), ("m", stage_F, 1), ("i",), ("m", stage_G2, 0),
                    ("m", stage_G1, 1), ("i",), ("m", stage_G2, 1), ("i",), ("i",), ("i",)]
            piece = 0
            for it in seq:
                if it[0] == "m":
                    it[1](tl, it[2])
                elif it[0] == "n":
                    it[1](nx, it[2])
                elif prev is not None:
                    stage_I_piece(prev, piece)
                    piece += 1
            if prev is not None:
                assert piece == 8
                stage_J(prev)
            hooks = None
            if nx is not None:
                bgen = stage_B_gen(nx)

                def step(n, bgen=bgen):
                    def f():
                        for _ in range(n):
                            try:
                                next(bgen)
                            except StopIteration:
                                pass
                    return f
                hooks = {0: [step(1)], 1: [step(1)], 2: [step(2)], 3: [step(1)], 4: [step(2)], 5: [step(1)],
                         6: [step(2)], 7: [step(20)]}
            stage_H(tl, hooks)
            prev = tl
        for piece in range(8):
            stage_I_piece(prev, piece)
        stage_J(prev)

        P.emit(nc)
    return nc, P


_CACHE = {}


def _get_nc(S_TOK):
    if S_TOK not in _CACHE:
        _CACHE[S_TOK] = build(S_TOK)
    return _CACHE[S_TOK][0]


def kernel(x_prompt, x_sample, cache_k, cache_v, cache_meta_k, cache_meta_v, state_conv, meta_tokens,
           norm_mix, w_in, conv_w, attn_sinks, rel_bias_table, norm_conv_out, norm_attn_out, w_out,
           norm_mlp, w_up, w_down, norm_final):
    f = lambda a: np.ascontiguousarray(np.asarray(a, dtype=np.float32))
    x_prompt = f(x_prompt)
    x_sample = f(x_sample)
    B, S_TOK, _ = x_prompt.shape
    ncores = 8
    assert B == ncores
    nc = _get_nc(S_TOK)
    w_in0 = f(w_in)[0]
    k0 = w_in0[:, 2048:2112]
    k1 = w_in0[:, 2112:2176]
    win_x = np.ascontiguousarray(np.concatenate([w_in0, k0, k0, k1, k1], axis=1))
    pk = lambda v: np.ascontiguousarray(v.reshape(-1, 128).T)
    gout = np.concatenate([f(norm_conv_out)[0], f(norm_attn_out)[0]])
    convw = np.ascontiguousarray(f(conv_w)[0].reshape(3, 4, 128).transpose(2, 1, 0))
    common = {
        "meta": f(meta_tokens), "gmix": f(norm_mix).reshape(1, D), "gmlp": f(norm_mlp).reshape(1, D),
        "gcv": pk(f(norm_conv_out)[0]), "gatt": f(norm_attn_out).reshape(1, Q_DIM),
        "gfin": f(norm_final).reshape(1, D), "convw": convw, "sinks": f(attn_sinks).reshape(1, 8),
        "table": f(rel_bias_table), "oh": _onehot_const(), "win": win_x, "wout": f(w_out)[0],
        "wup": f(w_up)[0], "wdn": f(w_down)[0],
    }
    ck = f(cache_k)[0].reshape(16, 128, 128)
    cv = f(cache_v)[0].reshape(16, 128, 128)
    cmk = f(cache_meta_k)[0].reshape(16, 16, 128)
    cmv = f(cache_meta_v)[0].reshape(16, 16, 128)
    sc = f(state_conv)[0]
    in_maps = []
    for c in range(ncores):
        m = dict(common)
        m["xp"] = x_prompt[c]
        m["xs"] = np.ascontiguousarray(x_sample[2 * c:2 * c + 2].reshape(64, D))
        m["ck"] = np.ascontiguousarray(ck[2 * c:2 * c + 2])
        m["cv"] = np.ascontiguousarray(cv[2 * c:2 * c + 2])
        m["cmk"] = np.ascontiguousarray(cmk[2 * c:2 * c + 2])
        m["cmv"] = np.ascontiguousarray(cmv[2 * c:2 * c + 2])
        m["sconv"] = np.ascontiguousarray(sc[2 * c:2 * c + 2].reshape(2, 2, 4, 128).transpose(3, 2, 0, 1))
        in_maps.append(m)
    res = run_bass_kernel_spmd(nc, in_maps, core_ids=list(range(ncores)))
    R = res.results
    y_prompt = np.stack([R[c]["yp"] for c in range(ncores)]).astype(np.float32)
    y_sample = np.concatenate([R[c]["ys"].reshape(2, 32, D) for c in range(ncores)]).astype(np.float32)
    p_k = np.stack([R[c]["pk"].reshape(128, 2, 64) for c in range(ncores)])[None].astype(np.float32)
    p_v = np.stack([R[c]["pv"].reshape(128, 2, 64) for c in range(ncores)])[None].astype(np.float32)
    p_mk = np.stack([R[c]["pmk"].reshape(16, 2, 64) for c in range(ncores)])[None].astype(np.float32)
    p_mv = np.stack([R[c]["pmv"].reshape(16, 2, 64) for c in range(ncores)])[None].astype(np.float32)
    p_conv = np.stack([R[c]["pconv"].transpose(2, 1, 0).reshape(2, 512) for c in range(ncores)])[None].astype(np.float32)
    s_k = np.concatenate([R[c]["sk"].reshape(2, 32, 2, 64) for c in range(ncores)])[None].astype(np.float32)
    s_v = np.concatenate([R[c]["sv"].reshape(2, 32, 2, 64) for c in range(ncores)])[None].astype(np.float32)
    s_conv = np.concatenate([R[c]["sconvo"].transpose(2, 3, 1, 0).reshape(2, 2, 512) for c in range(ncores)])[None].astype(np.float32)
    return (y_prompt, y_sample, p_k, p_v, p_mk, p_mv, p_conv, s_k, s_v, s_conv)
```

```python
import contextlib
import math
import types
import numpy as np
import concourse.bass as bass
import concourse.mybir as mybir
from concourse.bass_utils import run_bass_kernel_spmd

F32 = mybir.dt.float32
BF16 = mybir.dt.bfloat16
AF = mybir.ActivationFunctionType
ALU = mybir.AluOpType

D = 1024
SEQ = 8192
N_META = 16
CONV_DIM = 512
Q_DIM = 512
IN_DIM = 2304
IN_X = 2560
D_FF = 4096
EPS = 1e-6
MT = 256
NEG = -30000.0
PAST_LEN = 4096
DBG_STOP = False

HSEG = [("prev", 128, 128, -128), ("cur", 128, 128, 0), ("meta0", 16, 128, -16),
        ("sw", 128, 32, -128), ("sn", 32, 32, 0)]
HOFF = {}
_o = 0
for _n, _nk, _T, _db in HSEG:
    HOFF[_n] = (_o, _nk, _T, _db)
    _o += _nk + _T - 1
HLEN = _o


def _t5_bucket_np(rp):
    rp = np.asarray(rp, dtype=np.int32)
    nb = 16
    max_exact = 8
    ret = np.where(rp > 0, nb, 0)
    n = np.abs(rp)
    nf = np.maximum(n, 1).astype(np.float32)
    large = max_exact + (np.log(nf / np.float32(max_exact)) / np.float32(math.log(128 / max_exact))
                         * np.float32(nb - max_exact)).astype(np.int32)
    large = np.minimum(large, nb - 1)
    return ret + np.where(n < max_exact, n, large)


def _bucket(rp):
    try:
        import jax
        import jax.numpy as jnp
        cpu = jax.devices("cpu")[0]
        with jax.default_device(cpu):
            rp = jnp.asarray(np.asarray(rp, dtype=np.int32))
            nb = 16
            max_exact = 8
            ret = jnp.where(rp > 0, nb, 0)
            n = jnp.abs(rp)
            nf = jnp.maximum(n, 1).astype(jnp.float32)
            large = max_exact + (jnp.log(nf / max_exact) / math.log(128 / max_exact) * (nb - max_exact)).astype(jnp.int32)
            large = jnp.minimum(large, nb - 1)
            return np.asarray(ret + jnp.where(n < max_exact, n, large))
    except Exception:
        return _t5_bucket_np(rp)


def _onehot_const():
    oh = np.zeros((32, HLEN), np.float32)
    for name, (off, nk, T, db) in HOFF.items():
        j = np.arange(nk + T - 1)
        d = db + (nk - 1) - j
        b = _bucket(d)
        oh[b, off + j] = 1.0
    return oh


def _freeze(fn):
    if fn.__closure__ is None:
        return fn
    cells = tuple(types.CellType(c.cell_contents) for c in fn.__closure__)
    return types.FunctionType(fn.__code__, fn.__globals__, fn.__name__, fn.__defaults__, cells)


class Res:
    __slots__ = ("name", "lw", "rd")

    def __init__(self, name):
        self.name = name
        self.lw = None
        self.rd = []


class Op:
    __slots__ = ("eng", "fn", "deps", "dma", "chan", "sig", "hasdep", "idx")


class Prog:
    ENGS = ("pe", "act", "dve", "pool", "sp")

    def __init__(self):
        self.ops = []
        self.chan_cnt = {}
        self.eng_cnt = {e: 0 for e in self.ENGS}
        self.final = []

    def add(self, eng, fn, reads=(), writes=(), dma=False, chan=None, final=False):
        op = Op()
        op.eng = eng
        op.fn = _freeze(fn)
        op.dma = dma
        op.hasdep = False
        op.sig = None
        op.idx = len(self.ops)
        deps = []
        for r in reads:
            if r.lw is not None:
                deps.append(r.lw)
        for w in writes:
            if w.lw is not None:
                deps.append(w.lw)
            deps.extend(w.rd)
        seen = set()
        od = []
        for d in deps:
            if d is op or id(d) in seen:
                continue
            seen.add(id(d))
            if eng == "pe" and d.eng == "pe" and not d.dma and not dma:
                continue
            d.hasdep = True
            od.append(d)
        op.deps = od
        for r in reads:
            r.rd.append(op)
        for w in writes:
            w.lw = op
            w.rd = []
        if dma:
            op.chan = chan if chan is not None else (writes[0] if writes else ("dma", eng))
        else:
            op.chan = None
        if dma:
            op.hasdep = True
        if final:
            op.hasdep = True
            self.final.append(op)
        self.ops.append(op)
        return op

    def emit(self, nc):
        chan_keys = []
        for op in self.ops:
            if not op.hasdep:
                continue
            if op.dma:
                k = op.chan if isinstance(op.chan, (str, tuple)) else id(op.chan)
                if k not in self.chan_cnt:
                    self.chan_cnt[k] = 0
                    chan_keys.append(k)
                self.chan_cnt[k] += 16
                op.sig = (("c", k), self.chan_cnt[k])
            else:
                self.eng_cnt[op.eng] += 1
                op.sig = (("e", op.eng), self.eng_cnt[op.eng])
        with contextlib.ExitStack() as st:
            sems = {}
            for e in self.ENGS:
                sems[("e", e)] = st.enter_context(nc.semaphore("s_" + e))
            for i, k in enumerate(chan_keys):
                sems[("c", k)] = st.enter_context(nc.semaphore("c_%d" % i))
            self.nsem = len(sems)
            block = st.enter_context(nc.Block())

            def run(engname, e):
                known = {}
                for op in self.ops:
                    if op.eng != engname:
                        continue
                    for d in op.deps:
                        sk, v = d.sig
                        if known.get(sk, 0) < v:
                            e.wait_ge(sems[sk], v)
                            known[sk] = v
                    ins = op.fn(e)
                    if op.sig is not None:
                        sk, v = op.sig
                        ins.then_inc(sems[sk], 16 if op.dma else 1)
                if engname == "sp":
                    fin = {}
                    for op in self.final:
                        sk, v = op.sig
                        fin[sk] = max(fin.get(sk, 0), v)
                    for sk, v in fin.items():
                        if known.get(sk, 0) < v:
                            e.wait_ge(sems[sk], v)
                            known[sk] = v

            @block.tensor
            def _(e):
                run("pe", e)

            @block.scalar
            def _(e):
                run("act", e)

            @block.vector
            def _(e):
                run("dve", e)

            @block.gpsimd
            def _(e):
                run("pool", e)

            @block.sync
            def _(e):
                run("sp", e)


class Rot:
    def __init__(self, alloc, name, n, shape, dt):
        self.t = [alloc("%s%d" % (name, i), shape, dt) for i in range(n)]
        self.r = [Res("%s%d" % (name, i)) for i in range(n)]
        self.i = 0

    def get(self):
        i = self.i
        self.i = (i + 1) % len(self.t)
        return self.t[i], self.r[i]


def build(S_TOK):
    assert S_TOK % MT == 0
    NTP = S_TOK // MT
    nc = bass.Bass("TRN2", target_bir_lowering=False)

    def din(name, shape, dt=F32):
        return nc.dram_tensor(name, list(shape), dt, kind="ExternalInput").ap()

    def dout(name, shape, dt=F32):
        return nc.dram_tensor(name, list(shape), dt, kind="ExternalOutput").ap()

    xp_d = din("xp", [S_TOK, D])
    xs_d = din("xs", [64, D])
    ck_d = din("ck", [2, 128, 128])
    cv_d = din("cv", [2, 128, 128])
    cmk_d = din("cmk", [2, 16, 128])
    cmv_d = din("cmv", [2, 16, 128])
    sconv_d = din("sconv", [128, 4, 2, 2])
    meta_d = din("meta", [16, D])
    gmix_d = din("gmix", [1, D])
    gmlp_d = din("gmlp", [1, D])
    gcv_d = din("gcv", [128, 4])
    gatt_d = din("gatt", [1, Q_DIM])
    gfin_d = din("gfin", [1, D])
    convw_d = din("convw", [128, 4, 3])
    sinks_d = din("sinks", [1, 8])
    table_d = din("table", [32, 8])
    oh_d = din("oh", [32, HLEN])
    win_d = din("win", [D, IN_X])
    wout_d = din("wout", [D, D])
    wup_d = din("wup", [D, D_FF])
    wdn_d = din("wdn", [D_FF, D])

    yp_d = dout("yp", [S_TOK, D])
    ys_d = dout("ys", [64, D])
    pk_d = dout("pk", [128, 128])
    pv_d = dout("pv", [128, 128])
    pmk_d = dout("pmk", [16, 128])
    pmv_d = dout("pmv", [16, 128])
    pconv_d = dout("pconv", [128, 4, 2])
    sk_d = dout("sk", [64, 128])
    sv_d = dout("sv", [64, 128])
    sconvo_d = dout("sconvo", [128, 4, 2, 2])

    gscr = nc.dram_tensor("gscr", [8, HLEN], BF16, kind="Internal").ap()
    wup_s = nc.dram_tensor("wup_s", [8, 128, 8, 512], BF16, kind="Internal").ap()
    wdn_s = nc.dram_tensor("wdn_s", [8, 128, 4, 1024], BF16, kind="Internal").ap()
    r_gscr = Res("gscr")
    r_wup_s = [Res("wup_s%d" % i) for i in range(8)]
    r_wdn_s = [Res("wdn_s%d" % i) for i in range(8)]

    P = Prog()
    with contextlib.ExitStack() as st:
        def sb(name, shape, dt):
            return st.enter_context(nc.sbuf_tensor("s_" + name, list(shape), dt))

        def ps(name, shape, dt):
            return st.enter_context(nc.psum_tensor("p_" + name, list(shape), dt))

        bank = [ps("bank%d" % i, [128, 512], F32) for i in range(8) if i != 2]
        bank.insert(2, None)
        TRb = ps("TRb", [128, 1024], BF16)
        r_bank = [Res("bank%d" % i) for i in range(8)]
        r_TR = r_bank[2]
        ssqc_ps = bank[3][:, 0:4]
        r_ssqc_ps = r_bank[3]
        gv_ps = bank[4]
        fmb_i = [0]

        def fmb_get():
            i = fmb_i[0]
            fmb_i[0] = 1 - i
            return bank[i], r_bank[i]

        xh = [[sb("xh%d%d" % (p, s), [128, D], F32) for s in range(2)] for p in range(3)]
        r_xh = [[Res("xh%d%d" % (p, s)) for s in range(2)] for p in range(3)]
        xsb = Rot(sb, "xsb", 2, [128, D], BF16)
        yaraw = Rot(sb, "yaraw", 2, [128, 512], F32)
        yas = Rot(sb, "yas", 2, [128, 512], BF16)

        ident = sb("ident", [128, 128], BF16)
        anti128 = sb("anti128", [128, 128], BF16)
        anti32 = sb("anti32", [32, 32], BF16)
        anti17 = sb("anti17", [17, 17], BF16)
        ones_col = sb("ones_col", [128, 1], BF16)
        iot = yaraw.t[0][:, 0:128]
        r_const = Res("const")
        r_iot = yaraw.r[0]
        P.add("pool", lambda e: e.iota(iot[:], pattern=[[1, 128]], base=0, channel_multiplier=-1,
                                       allow_small_or_imprecise_dtypes=True), writes=[r_iot])
        P.add("dve", lambda e: e.tensor_scalar(out=ident[:], in0=iot[:], scalar1=0.0, scalar2=None,
                                               op0=ALU.is_equal), reads=[r_iot], writes=[r_const])
        iot2 = yaraw.t[1][:, 0:128]
        r_iot2 = yaraw.r[1]
        P.add("pool", lambda e: e.iota(iot2[:], pattern=[[1, 128]], base=0, channel_multiplier=1,
                                       allow_small_or_imprecise_dtypes=True), writes=[r_iot2])
        P.add("dve", lambda e: e.tensor_scalar(out=anti128[:], in0=iot2[:], scalar1=127.0, scalar2=None,
                                               op0=ALU.is_equal), reads=[r_iot2], writes=[r_const])
        P.add("dve", lambda e: e.tensor_scalar(out=anti32[:], in0=iot2[0:32, 0:32], scalar1=31.0, scalar2=None,
                                               op0=ALU.is_equal), reads=[r_iot2], writes=[r_const])
        P.add("dve", lambda e: e.tensor_scalar(out=anti17[:], in0=iot2[0:17, 0:17], scalar1=15.0, scalar2=None,
                                               op0=ALU.is_equal), reads=[r_iot2], writes=[r_const])
        P.add("dve", lambda e: e.memset(ones_col[:], 1.0), writes=[r_const])

        gmix = sb("gmix", [128, D], F32)
        gmlp = sb("gmlp", [128, D], F32)
        gatt = sb("gatt", [128, Q_DIM], F32)
        gcv = sb("gcv", [128, 4], F32)
        convw = sb("convw", [128, 4, 3], F32)
        gfin = sb("gfin", [128, D], F32)
        table = sb("table", [32, 8], F32)
        ohs = xh[2][1][0:32, 0:HLEN]
        CBm0 = sb("CBm0", [17, 8], F32)
        CBm = sb("CBm", [17, 8], F32)
        r_small = [Res("small%d" % i) for i in range(7)]
        r_cb = [Res("cb%d" % i) for i in range(3)]
        for i, (tl, src) in enumerate(((table, table_d), (gmix, gmix_d.partition_broadcast(128)), (convw, convw_d),
                                       (gcv, gcv_d), (gatt, gatt_d.partition_broadcast(128)),
                                       (gmlp, gmlp_d.partition_broadcast(128)),
                                       (gfin, gfin_d.partition_broadcast(128)))):
            P.add("sp", lambda e, tl=tl, src=src: e.dma_start(out=tl[:], in_=src), writes=[r_small[i]], dma=True)
        P.add("dve", lambda e: e.memset(CBm0[:], 0.0), writes=[r_cb[0]])
        P.add("sp", lambda e: e.dma_start(out=CBm0[16:17, :], in_=sinks_d), writes=[r_cb[0]], dma=True)
        P.add("sp", lambda e: e.dma_start(out=CBm[16:17, :], in_=sinks_d), writes=[r_cb[1]], dma=True)
        P.add("sp", lambda e: e.dma_start(out=CBm[0:16, :], in_=table_d[15:16, :].partition_broadcast(16)),
              writes=[r_cb[2]], dma=True)

        gvb = xsb.t[1][0:8, 0:HLEN]
        r_gvb = xsb.r[1]
        P.add("sp", lambda e: e.dma_start(out=ohs, in_=oh_d), writes=[r_xh[2][1]], dma=True)
        Hprev = sb("Hprev", [128, 8, 128], BF16)
        Hcur = sb("Hcur", [128, 8, 128], BF16)
        Hm0 = sb("Hm0", [17, 8, 128], BF16)
        Hsw = sb("Hsw", [128, 8, 32], BF16)
        Hsn = sb("Hsn", [32, 8, 32], BF16)
        r_Hd = {n_: Res("H_" + n_) for n_ in ("prev", "cur", "meta0", "sw", "sn")}
        r_H = list(r_Hd.values())
        for c0 in range(0, HLEN, 512):
            c1 = min(HLEN, c0 + 512)
            P.add("pe", lambda e, c0=c0, c1=c1: e.matmul(gv_ps[0:8, 0:c1 - c0], lhsT=table[:], rhs=ohs[:, c0:c1],
                                                         start=True, stop=True),
                  reads=[r_small[0], r_xh[2][1]], writes=[r_bank[4]])
            P.add("dve", lambda e, c0=c0, c1=c1: e.tensor_copy(out=gvb[:, c0:c1], in_=gv_ps[0:8, 0:c1 - c0]),
                  writes=[r_bank[4], r_gvb])
        P.add("sp", lambda e: e.dma_start(out=gscr, in_=gvb), reads=[r_gvb], writes=[r_gscr], dma=True)
        P.add("dve", lambda e: e.memset(Hm0[:], 0.0), writes=[r_Hd["meta0"]])
        for tl, name in ((Hprev, "prev"), (Hcur, "cur"), (Hm0, "meta0"), (Hsw, "sw"), (Hsn, "sn")):
            off, nk, T, db = HOFF[name]
            src = bass.AP(gscr.tensor, off, [[1, nk], [HLEN, 8], [1, T]])
            P.add("sp", lambda e, tl=tl, src=src, nk=nk: e.dma_start(out=tl[0:nk, :, :], in_=src),
                  reads=[r_gscr], writes=[r_Hd[name]], dma=True)
        P.add("dve", lambda e: e.memset(Hprev[64:128, :, 64:128], NEG), writes=[r_Hd["prev"]])
        P.add("dve", lambda e: e.memset(Hcur[0:64, :, 0:64], NEG), writes=[r_Hd["cur"]])

        WIN = sb("WIN", [128, 8, IN_X], BF16)
        WOUT = sb("WOUT", [128, 8, D], BF16)
        r_WIN = [[Res("WIN%d_%d" % (k, c)) for c in range(2)] for k in range(8)]
        r_WOUT = [Res("WOUT%d" % c) for c in range(2)]

        def prep_weights():
            for k in range(8):
                for c in range(2):
                    P.add("pool", lambda e: e.dma_start(out=WIN[:, k, c * 1280:(c + 1) * 1280],
                                                        in_=win_d[k * 128:(k + 1) * 128, c * 1280:(c + 1) * 1280]),
                          writes=[r_WIN[k][c]], dma=True)
            for c in range(2):
                src = wout_d[:, c * 512:(c + 1) * 512].rearrange("(k p) c -> p k c", p=128)
                P.add("pool", lambda e: e.dma_start(out=WOUT[:, :, c * 512:(c + 1) * 512], in_=src),
                      writes=[r_WOUT[c]], dma=True)

        def prep_mlp_weights():
            for i in range(8):
                src = wup_d[:, i * 512:(i + 1) * 512].rearrange("(k p) c -> p k c", p=128)
                P.add("pool", lambda e: e.dma_start(out=wup_s[i], in_=src), reads=[r_KKm], writes=[r_wup_s[i]],
                      dma=True)
            for i in range(8):
                src = wdn_d[i * 512:(i + 1) * 512, :].rearrange("(k p) c -> p k c", p=128)
                P.add("pool", lambda e: e.dma_start(out=wdn_s[i], in_=src), reads=[r_KKm], writes=[r_wdn_s[i]],
                      dma=True)


        junk = Rot(sb, "junk", 1, [128, D], BF16)
        _xsT = sb("xsT", [128, 8, MT], BF16)
        xsT = [_xsT, _xsT]
        _r_xsT = [Res("xsT%d" % s) for s in range(2)]
        r_xsT = [_r_xsT, _r_xsT]
        h1nT = sb("h1nT", [128, 8, MT], BF16)
        r_h1nT = [Res("h1nT%d" % s) for s in range(2)]
        qT = sb("qT", [128, 4, MT], BF16)
        r_qT = [Res("qT%d" % j) for j in range(4)]
        NRING = 4
        KKr = sb("KKr", [128, NRING, 2, 128], BF16)
        r_KKr = [Res("KKr%d" % i) for i in range(NRING)]
        Vr = sb("Vr", [128, NRING, 2, 65], BF16)
        r_Vr = [Res("Vr%d" % i) for i in range(NRING)]
        KKm = sb("KKm", [128, 2, 17], BF16)
        Vm = sb("Vm", [17, 2, 65], BF16)
        r_KKm = Res("KKm")
        r_Vm = Res("Vm")
        KKsm = sb("KKsm", [128, 2, 2, 17], BF16)
        Vsm = sb("Vsm", [17, 2, 2, 65], BF16)
        KKsw = sb("KKsw", [128, 2, 2, 128], BF16)
        Vsw = sb("Vsw", [128, 2, 2, 65], BF16)
        KKsn = sb("KKsn", [128, 2, 64], BF16)
        Vsn = sb("Vsn", [32, 2, 2, 65], BF16)
        r_scache = Res("scache")
        r_KKsn = Res("KKsn")
        r_Vsn = [Res("Vsn0"), Res("Vsn1")]
        ucx = [sb("ucx%d" % p, [128, 4, MT + 4], F32) for p in range(2)]
        r_ucx = [[Res("ucx%d%d" % (p, j)) for j in range(4)] for p in range(2)]
        r_ucctx = [Res("ucctx%d" % p) for p in range(2)]
        csb = Rot(sb, "csb", 2, [128, MT], F32)
        tmpy = Rot(sb, "tmpy", 2, [128, MT], F32)
        bsb = Rot(sb, "bsb", 2, [128, MT], F32)
        ycsq = sb("ycsq", [128, 4, MT], BF16)
        r_ycsq = [Res("ycsq%d" % j) for j in range(4)]
        ycT = sb("ycT", [128, 4, MT], BF16)
        r_ycT = [Res("ycT%d" % j) for j in range(4)]
        yaT = sb("yaT", [128, 4, MT], BF16)
        r_yaT = [Res("yaT%d" % s) for s in range(2)]
        stat = Rot(sb, "stat", 48, [128, 1], F32)
        rec8 = Rot(sb, "rec8", 2, [128, 8], F32)
        hidT = sb("hidT", [128, 32, MT], BF16)
        r_hidT = [Res("hidT%d" % f) for f in range(32)]
        rtmp = Rot(sb, "rtmp", 2, [128, 2 * MT], F32)
        ring = Rot(sb, "ring", 4, [128, 4096], BF16)
        kvst = Rot(sb, "kvst", 1, [128, 256], F32)
        kdup = yas.t[0][:].rearrange("p (s g a c) -> p s g a c", s=2, g=2, a=2)
        kmdup = yas.t[1][0:16, :].rearrange("p (s g a c) -> p s g a c", s=2, g=2, a=2)
        r_kmdup = yas.r[1]
        cstv = [rtmp.t[q][:].rearrange("p (a c) -> p a c", a=4) for q in range(2)]
        r_cstv = [rtmp.r[q] for q in range(2)]
        r_kdup = yas.r[0]

        P.add("pool", lambda e: e.memset(Vr[:], 1.0), writes=r_Vr)
        P.add("pool", lambda e: e.memset(Vm[:], 1.0), writes=[r_Vm])
        P.add("pool", lambda e: e.memset(Vm[:, :, 0:64], 0.0), writes=[r_Vm])
        P.add("pool", lambda e: e.memset(Vsm[:], 1.0), writes=[r_scache])
        P.add("pool", lambda e: e.memset(Vsm[:, :, :, 0:64], 0.0), writes=[r_scache])
        P.add("pool", lambda e: e.memset(Vsw[:], 1.0), writes=[r_scache])
        P.add("pool", lambda e: e.memset(Vsn[:], 1.0), writes=r_Vsn)
        P.add("pool", lambda e: e.memset(KKm[:], 0.0), writes=[r_KKm])
        P.add("pool", lambda e: e.memset(KKsm[:], 0.0), writes=[r_scache])
        P.add("pool", lambda e: e.memset(KKr[:], 0.0), writes=r_KKr)

        out_ops = []

        def rms_stats(src_ap, n_tok, nfeat, reads):
            jt, jr = junk.get()
            s1, r1 = stat.get()
            P.add("act", lambda e: e.activation(out=jt[0:n_tok, 0:nfeat], in_=src_ap, func=AF.Square,
                                                accum_out=s1[0:n_tok, :]), reads=reads, writes=[jr, r1])
            s2, r2 = stat.get()
            P.add("act", lambda e: e.activation(out=s2[0:n_tok, :], in_=s1[0:n_tok, :], func=AF.Ln,
                                                scale=1.0 / nfeat, bias=EPS), reads=[r1], writes=[r2])
            return s2, r2

        def exp_of(ln_t, ln_r, n_tok, scale):
            s3, r3 = stat.get()
            P.add("act", lambda e: e.activation(out=s3[0:n_tok, :], in_=ln_t[0:n_tok, :], func=AF.Exp, scale=scale),
                  reads=[ln_r], writes=[r3])
            return s3, r3

        def norm_part1(src_t, src_r, n_tok, gain):
            ln_t, ln_r = rms_stats(src_t[0:n_tok, :], n_tok, D, [src_r])
            rs_t, rs_r = exp_of(ln_t, ln_r, n_tok, -0.5)
            xb, xbr = xsb.get()
            P.add("dve", lambda e: e.scalar_tensor_tensor(out=xb[0:n_tok, :], in0=src_t[0:n_tok, :],
                                                          scalar=rs_t[0:n_tok, :], in1=gain[0:n_tok, :],
                                                          op0=ALU.mult, op1=ALU.mult),
                  reads=[src_r, rs_r] + r_small, writes=[xbr])
            return xb, xbr

        def norm_part2(xb, xbr, n_tok, dstT, dst_r, col0):
            for k in range(8):
                P.add("pe", lambda e, k=k: e.transpose(out=TRb[:, k * 128:k * 128 + n_tok],
                                                       in_=xb[0:n_tok, k * 128:(k + 1) * 128],
                                                       identity=ident[0:n_tok, 0:n_tok]),
                      reads=[xbr, r_const], writes=[r_TR])
            src = TRb[:].rearrange("p (k t) -> p k t", k=8)[:, :, 0:n_tok]
            P.add("dve", lambda e: e.tensor_copy(out=dstT[:, :, col0:col0 + n_tok], in_=src),
                  writes=[r_TR, dst_r])

        def norm_transpose(src_t, src_r, n_tok, dstT, dst_r, col0, gain):
            xb, xbr = norm_part1(src_t, src_r, n_tok, gain)
            norm_part2(xb, xbr, n_tok, dstT, dst_r, col0)


        def inproj_cols(xT, xT_rs, T, cols):
            bt, br = fmb_get()
            for i, col in enumerate(cols):
                for k in range(8):
                    P.add("pe", lambda e, k=k, i=i, col=col: e.matmul(
                        bt[:, i * 256:i * 256 + T], lhsT=WIN[:, k, col:col + 128], rhs=xT[:, k, 0:T],
                        start=(k == 0), stop=(k == 7)), reads=r_WIN[k] + xT_rs, writes=[br])
            return bt, br

        def kv_tokmajor(xT, xT_r, col0, n_tok, bi):
            for k in range(8):
                P.add("pe", lambda e, k=k: e.matmul(bank[bi][0:n_tok, 0:256], lhsT=xT[:, k, col0:col0 + n_tok],
                                                    rhs=WIN[:, k, 2048:2304], start=(k == 0), stop=(k == 7)),
                      reads=r_WIN[k] + [xT_r], writes=[r_bank[bi]])

        PTh = Rot(sb, "PTh", 3, [128, 3, 128], BF16)

        def attention(T, qcol, groups_fn):
            pend = None
            for h in range(8):
                gl = groups_fn(h)
                hp = h % 2
                sbk = bank[h % 2]
                sbr = r_bank[h % 2]
                for g in gl:
                    nk = g["nk"]
                    sl = g["slot"]
                    P.add("pe", lambda e, g=g, nk=nk, sl=sl, hp=hp, h=h: e.matmul(
                        sbk[0:nk, sl * 128:sl * 128 + T], lhsT=g["kk"],
                        rhs=qT[hp * 64:(hp + 1) * 64, h // 2, qcol:qcol + T], start=True, stop=(g["hank"] is None)),
                        reads=g["kk_rs"] + [r_qT[h // 2]], writes=[sbr])
                    if g["hank"] is not None:
                        P.add("pe", lambda e, g=g, nk=nk, sl=sl: e.matmul(
                            sbk[0:nk, sl * 128:sl * 128 + T], lhsT=g["anti"], rhs=g["hank"], start=False, stop=True),
                            reads=r_H + [r_const], writes=[sbr])
                pt, pr = PTh.get()
                full = [g for g in gl if g["nk"] == 128 and g["cb"] is None]
                rest = [g for g in gl if not (g["nk"] == 128 and g["cb"] is None)]
                if full:
                    s0 = min(g["slot"] for g in full)
                    s1 = max(g["slot"] for g in full) + 1
                    assert s1 - s0 == len(full)
                    if T == 128:
                        P.add("act", lambda e, pt=pt, s0=s0, s1=s1: e.activation(
                            out=pt[:, s0:s1, :], in_=sbk[:, s0 * 128:s1 * 128].rearrange("p (a c) -> p a c", c=128),
                            func=AF.Exp), reads=[], writes=[sbr, pr])
                    else:
                        for g in full:
                            sl = g["slot"]
                            P.add("act", lambda e, pt=pt, sl=sl: e.activation(
                                out=pt[:, sl, 0:T], in_=sbk[:, sl * 128:sl * 128 + T], func=AF.Exp),
                                writes=[sbr, pr])
                for g in rest:
                    nk = g["nk"]
                    sl = g["slot"]
                    if g["cb"] is None:
                        P.add("act", lambda e, pt=pt, sl=sl, nk=nk: e.activation(
                            out=pt[0:nk, sl, 0:T], in_=sbk[0:nk, sl * 128:sl * 128 + T], func=AF.Exp),
                            writes=[sbr, pr])
                    else:
                        P.add("act", lambda e, pt=pt, sl=sl, nk=nk, g=g: e.activation(
                            out=pt[0:nk, sl, 0:T], in_=sbk[0:nk, sl * 128:sl * 128 + T], func=AF.Exp, bias=g["cb"]),
                            reads=r_cb, writes=[sbr, pr])
                if pend is not None:
                    emit_pv(*pend)
                pend = (h, T, gl, pt, pr)
            emit_pv(*pend)

        def emit_pv(h, T, gl, pt, pr):
            ob = bank[4 + h // 4]
            obr = r_bank[4 + h // 4]
            hh = h % 4
            n = len(gl)
            for i, g in enumerate(gl):
                nk = g["nk"]
                sl = g["slot"]
                P.add("pe", lambda e, g=g, nk=nk, i=i, sl=sl: e.matmul(
                    ob[0:T, hh * 65:(hh + 1) * 65], lhsT=pt[0:nk, sl, 0:T], rhs=g["v"], start=(i == 0), stop=(i == n - 1)),
                    reads=[pr] + g["v_rs"], writes=[obr])


        def attn_E1(T):
            rc, rcr = rec8.get()
            yr, yrr = yaraw.get()
            for b in range(2):
                ob = bank[4 + b]
                o3 = ob[0:T, 0:260].rearrange("p (h c) -> p h c", h=4)
                P.add("dve", lambda e, o3=o3, b=b: e.reciprocal(out=rc[0:T, 4 * b:4 * b + 4], in_=o3[:, :, 64]),
                      writes=[r_bank[4 + b], rcr])
                P.add("dve", lambda e, o3=o3, b=b: e.tensor_tensor(
                    out=yr[0:T, 256 * b:256 * (b + 1)].rearrange("p (h c) -> p h c", h=4),
                    in0=o3[:, :, 0:64],
                    in1=rc[0:T, 4 * b:4 * b + 4].unsqueeze(2).to_broadcast([T, 4, 64]),
                    op=ALU.mult), reads=[rcr], writes=[r_bank[4 + b], yrr])
            return yr, yrr

        def attn_E2a(T, yr, yrr, lnc_t, lnc_r):
            lna_t, lna_r = rms_stats(yr[0:T, :], T, Q_DIM, [yrr])
            d_t, d_r = stat.get()
            P.add("dve", lambda e: e.tensor_tensor(out=d_t[0:T, :], in0=lnc_t[0:T, :], in1=lna_t[0:T, :],
                                                   op=ALU.subtract), reads=[lnc_r, lna_r], writes=[d_r])
            ratio_t, ratio_r = exp_of(d_t, d_r, T, 0.5)
            rstdc_t, rstdc_r = exp_of(lnc_t, lnc_r, T, -0.5)
            ys, ysr = yas.get()
            P.add("dve", lambda e: e.scalar_tensor_tensor(out=ys[0:T, :], in0=yr[0:T, :], scalar=ratio_t[0:T, :],
                                                          in1=gatt[0:T, :], op0=ALU.mult, op1=ALU.mult),
                  reads=[yrr, ratio_r] + r_small, writes=[ysr])
            return ys, ysr, rstdc_t, rstdc_r

        def attn_E2b(T, ys, ysr, yaT_col0, r_yaT_s):
            for j in range(4):
                P.add("pe", lambda e, j=j: e.transpose(out=TRb[:, j * 128:j * 128 + T], in_=ys[0:T, j * 128:(j + 1) * 128],
                                                       identity=ident[0:T, 0:T]),
                      reads=[ysr, r_const], writes=[r_TR])
            src = TRb[:, 0:512].rearrange("p (k t) -> p k t", k=4)[:, :, 0:T]
            P.add("dve", lambda e: e.tensor_copy(out=yaT[:, :, yaT_col0:yaT_col0 + T], in_=src),
                  writes=[r_TR, r_yaT_s])


        ring_state = {"pending": [], "left": 16 * (NTP + 1)}

        def mlp_weights_iter():
            while True:
                for i in range(8):
                    yield ("up", i)
                for i in range(8):
                    yield ("dn", i)

        wgen = mlp_weights_iter()

        def issue_wload():
            if ring_state["left"] <= 0:
                return
            ring_state["left"] -= 1
            kind, i = next(wgen)
            t, r = ring.get()
            src = (wup_s if kind == "up" else wdn_s)[i]
            rs = (r_wup_s if kind == "up" else r_wdn_s)[i]
            if kind == "up":
                dstv = t[:].rearrange("p (k c) -> p k c", k=8)
            else:
                dstv = t[:].rearrange("p (k c) -> p k c", k=4)
            P.add("pool", lambda e, dstv=dstv, src=src: e.dma_start(out=dstv, in_=src), reads=[rs], writes=[r], dma=True)
            ring_state["pending"].append((kind, i, t, r))

        def take_wload(kind, i):
            k2, i2, t, r = ring_state["pending"].pop(0)
            assert (k2, i2) == (kind, i)
            return t, r

        class Tile:
            pass

        def stage_A0(tl, s):
            p = tl.xp
            c0, n = tl.subs[s]
            src = tl.xsrc(s)
            P.add("sp", lambda e: e.dma_start(out=xh[p][s][0:n, :], in_=src), writes=[r_xh[p][s]], dma=True)

        def stage_A1(tl, s):
            p = tl.xp
            c0, n = tl.subs[s]
            tl.a1[s] = norm_part1(xh[p][s], r_xh[p][s], n, gmix)

        def stage_A2(tl, s):
            p = tl.par
            c0, n = tl.subs[s]
            xb, xbr = tl.a1[s]
            norm_part2(xb, xbr, n, xsT[p], r_xsT[p][s], c0)

        def stage_A(tl):
            tl.a1 = [None] * len(tl.subs)
            for s in range(len(tl.subs)):
                stage_A0(tl, s)
                stage_A1(tl, s)
                stage_A2(tl, s)


        def stage_B(tl):
            for _ in stage_B_gen(tl):
                pass

        def stage_B_gen(tl):
            if tl.kind == "sample":
                sample_ctx_load()
            p = tl.par
            T = tl.T
            xT = xsT[p]
            xT_rs = [r_xsT[p][s] for s in range(len(tl.subs))]
            bt, br = inproj_cols(xT, xT_rs, T, [2304, 2432])
            if tl.kind == "prompt":
                s0 = (2 * tl.t) % NRING
                for g in range(2):
                    P.add("dve", lambda e, bt=bt, g=g, s0=s0: e.tensor_copy(
                        out=KKr[:, s0:s0 + 2, g, :], in_=bt[:, g * 256:(g + 1) * 256].rearrange("p (a c) -> p a c", a=2)),
                        writes=[br, r_KKr[s0], r_KKr[s0 + 1]])
            elif tl.kind == "meta":
                for g in range(2):
                    P.add("dve", lambda e, bt=bt, g=g: e.tensor_copy(out=KKm[:, g, 0:16], in_=bt[:, g * 256:g * 256 + 16]),
                          writes=[br, r_KKm])
            else:
                for g in range(2):
                    P.add("dve", lambda e, bt=bt, g=g: e.tensor_copy(out=KKsn[:, g, :], in_=bt[:, g * 256:g * 256 + 64]),
                          writes=[br, r_KKsn])
            yield
            for s, (c0, n) in enumerate(tl.subs):
                if s > 0:
                    yield
                bi = 6 + s
                kv_tokmajor(xT, r_xsT[p][s], c0, n, bi)
                src_v = bank[bi][0:n, 128:256].rearrange("p (g c) -> p g c", g=2)
                if tl.kind == "prompt":
                    sl = (2 * tl.t + s) % NRING
                    P.add("act", lambda e, sl=sl, src_v=src_v: e.activation(out=Vr[:, sl, :, 0:64], in_=src_v, func=AF.Copy),
                          writes=[r_bank[bi], r_Vr[sl]])
                elif tl.kind == "meta":
                    P.add("act", lambda e, src_v=src_v: e.activation(out=Vm[0:16, :, 0:64], in_=src_v, func=AF.Copy),
                          writes=[r_bank[bi], r_Vm])
                else:
                    P.add("act", lambda e, s=s, src_v=src_v: e.activation(out=Vsn[0:32, s, :, 0:64], in_=src_v, func=AF.Copy),
                          writes=[r_bank[bi], r_Vsn[s]])
                outs = tl.kv_out(s)
                if outs is not None:
                    kd, vd = outs
                    kt, kr = kvst.get()
                    P.add("dve", lambda e, kt=kt, n=n, bi=bi: e.tensor_copy(out=kt[0:n, :], in_=bank[bi][0:n, 0:256]),
                          writes=[r_bank[bi], kr])
                    out_ops.append(P.add("sp", lambda e, kt=kt, n=n, kd=kd: e.dma_start(out=kd, in_=kt[0:n, 0:128]),
                                         reads=[kr], dma=True, chan=("kvo", id(kr), 0), final=True))
                    out_ops.append(P.add("sp", lambda e, kt=kt, n=n, vd=vd: e.dma_start(out=vd, in_=kt[0:n, 128:256]),
                                         reads=[kr], dma=True, chan=("kvo", id(kr), 1), final=True))
            L_ = tl.L
            nseg = T // L_
            W = L_ + 2
            for j in range(4):
                yield
                ba, bar = inproj_cols(xT, xT_rs, T, [512 + 128 * j, 1024 + 128 * j])
                ct, cr = csb.get()
                P.add("act", lambda e, ct=ct, ba=ba: e.activation(out=ct[:, 0:T], in_=ba[:, 0:T], func=AF.Copy),
                      writes=[bar, cr])
                ucv = ucx[p][:, j, 0:nseg * W].rearrange("p (a w) -> p a w", a=nseg)
                P.add("dve", lambda e, ct=ct, ba=ba, ucv=ucv: e.tensor_tensor(
                    out=ucv[:, :, 2:2 + L_], in0=ba[:, 256:256 + T].rearrange("p (a w) -> p a w", a=nseg),
                    in1=ct[:, 0:T].rearrange("p (a w) -> p a w", a=nseg), op=ALU.mult),
                    reads=[cr], writes=[bar, r_ucx[p][j]])
                if tl.kind == "meta":
                    continue
                yield
                bb, bbr = inproj_cols(xT, xT_rs, T, [0 + 128 * j, 1536 + 128 * j])
                bs, bsr = bsb.get()
                P.add("act", lambda e, bs=bs, bb=bb: e.activation(out=bs[:, 0:T], in_=bb[:, 0:T], func=AF.Copy),
                      writes=[bbr, bsr])
                P.add("act", lambda e, bb=bb, j=j: e.activation(out=qT[:, j, 0:T], in_=bb[:, 256:256 + T], func=AF.Copy,
                                                               scale=0.125), writes=[bbr, r_qT[j]])
                ty, tyr = tmpy.get()
                tyv = ty[:, 0:T].rearrange("p (a w) -> p a w", a=nseg)
                P.add("dve", lambda e, tyv=tyv, ucv=ucv, j=j: e.tensor_scalar(
                    out=tyv, in0=ucv[:, :, 0:L_], scalar1=convw[:, j, 0:1], scalar2=None, op0=ALU.mult),
                    reads=[r_ucx[p][j], r_ucctx[p]] + r_small, writes=[tyr])
                for tap in (1, 2):
                    P.add("dve", lambda e, tyv=tyv, ucv=ucv, j=j, tap=tap: e.scalar_tensor_tensor(
                        out=tyv, in0=ucv[:, :, tap:tap + L_], scalar=convw[:, j, tap:tap + 1], in1=tyv,
                        op0=ALU.mult, op1=ALU.add), reads=[r_ucx[p][j], r_ucctx[p]] + r_small, writes=[tyr])
                P.add("dve", lambda e, ty=ty, bs=bs, j=j: e.tensor_tensor(out=ty[:, 0:T], in0=ty[:, 0:T],
                                                                          in1=bs[:, 0:T], op=ALU.mult),
                      reads=[bsr], writes=[tyr])
                P.add("dve", lambda e, ty=ty, j=j: e.tensor_scalar(out=ycT[:, j, 0:T], in0=ty[:, 0:T],
                                                                   scalar1=gcv[:, j:j + 1], scalar2=None,
                                                                   op0=ALU.mult),
                      reads=[tyr] + r_small, writes=[r_ycT[j]])
                P.add("act", lambda e, ty=ty, j=j: e.activation(out=ycsq[:, j, 0:T], in_=ty[:, 0:T], func=AF.Square),
                      reads=[tyr], writes=[r_ycsq[j]])
            yield
            tl.after_conv()
            tl.lnc = []
            if tl.kind != "meta":
                for s, (c0, n) in enumerate(tl.subs):
                    for j in range(4):
                        P.add("pe", lambda e, j=j, c0=c0, n=n, s=s: e.matmul(
                            ssqc_ps[0:n, s:s + 1], lhsT=ycsq[:, j, c0:c0 + n], rhs=ones_col[:, 0:1],
                            start=(j == 0), stop=(j == 3)), reads=[r_ycsq[j], r_const], writes=[r_ssqc_ps])
                    sc, scr = stat.get()
                    P.add("dve", lambda e, sc=sc, n=n, s=s: e.tensor_copy(out=sc[0:n, :], in_=ssqc_ps[0:n, s:s + 1]),
                          writes=[r_ssqc_ps, scr])
                    l2, l2r = stat.get()
                    P.add("act", lambda e, sc=sc, l2=l2, n=n: e.activation(out=l2[0:n, :], in_=sc[0:n, :], func=AF.Ln,
                                                                         scale=1.0 / CONV_DIM, bias=EPS),
                          reads=[scr], writes=[l2r])
                    tl.lnc.append((l2, l2r))

        def stage_D(tl, s):
            c0, n = tl.subs[s]
            attention(n, c0, lambda h: tl.groups(s, h))
            tl.e1[s] = attn_E1(n)

        def stage_E2a(tl, s):
            c0, n = tl.subs[s]
            yr, yrr = tl.e1[s]
            tl.e2[s] = attn_E2a(n, yr, yrr, tl.lnc[s][0], tl.lnc[s][1])

        def stage_E2b(tl, s):
            c0, n = tl.subs[s]
            ys, ysr, _, _ = tl.e2[s]
            attn_E2b(n, ys, ysr, c0, r_yaT[s])

        def stage_F(tl, s):
            p = tl.xp
            c0, n = tl.subs[s]
            _, _, rstdc_t, rstdc_r = tl.e2[s]
            for half in range(2):
                ob = bank[half]
                for k in range(8):
                    if k < 4:
                        lhsT = ycT[:, k, c0:c0 + n]
                        rr = r_ycT[k]
                    else:
                        lhsT = yaT[:, k - 4, c0:c0 + n]
                        rr = r_yaT[s]
                    P.add("pe", lambda e, k=k: e.matmul(
                        ob[0:n, :], lhsT=lhsT, rhs=WOUT[:, k, half * 512:(half + 1) * 512],
                        start=(k == 0), stop=(k == 7)), reads=[rr, r_WOUT[half]], writes=[r_bank[half]])
                P.add("dve", lambda e: e.scalar_tensor_tensor(
                    out=xh[p][s][0:n, half * 512:(half + 1) * 512], in0=ob[0:n, :], scalar=rstdc_t[0:n, :],
                    in1=xh[p][s][0:n, half * 512:(half + 1) * 512], op0=ALU.mult, op1=ALU.add),
                    reads=[rstdc_r], writes=[r_bank[half], r_xh[p][s]])

        def stage_G1(tl, s):
            p = tl.xp
            c0, n = tl.subs[s]
            tl.g1[s] = norm_part1(xh[p][s], r_xh[p][s], n, gmlp)

        def stage_G2(tl, s):
            c0, n = tl.subs[s]
            xb, xbr = tl.g1[s]
            norm_part2(xb, xbr, n, h1nT, r_h1nT[s], c0)

        sq_i = [0]


        def stage_H(tl, hooks=None):
            T = tl.T
            hr = [r_h1nT[s] for s in range(len(tl.subs))]
            for piece in range(8):
                wt_, wr = take_wload("up", piece)
                wt = wt_[:].rearrange("p (k c) -> p k c", k=8)
                for pair in range(2):
                    f0 = piece * 4 + pair * 2
                    bt, br = fmb_get()
                    for i in range(2):
                        fi = pair * 2 + i
                        for k in range(8):
                            P.add("pe", lambda e, bt=bt, wt=wt, k=k, fi=fi, i=i: e.matmul(
                                bt[:, i * 256:i * 256 + T], lhsT=wt[:, k, fi * 128:(fi + 1) * 128], rhs=h1nT[:, k, 0:T],
                                start=(k == 0), stop=(k == 7)), reads=[wr] + hr, writes=[br])
                    rt, rr = rtmp.get()
                    rtv = rt[:].rearrange("p (a c) -> p a c", a=2)
                    P.add("act", lambda e, rtv=rtv, bt=bt: e.activation(
                        out=rtv[:, :, 0:T], in_=bt[:].rearrange("p (a c) -> p a c", a=2)[:, :, 0:T], func=AF.Relu),
                        writes=[br, rr])
                    eng = "dve"
                    sq_i[0] += 1
                    P.add(eng, lambda e, rtv=rtv, f0=f0: e.tensor_tensor(
                        out=hidT[:, f0:f0 + 2, 0:T], in0=rtv[:, :, 0:T], in1=rtv[:, :, 0:T], op=ALU.mult),
                        reads=[rr], writes=[r_hidT[f0], r_hidT[f0 + 1]])
                issue_wload()
                if hooks and piece in hooks:
                    for fn in hooks[piece]:
                        fn()


        def stage_I_piece(tl, piece):
            p = tl.xp
            wt_, wr = take_wload("dn", piece)
            wt = wt_[:].rearrange("p (k c) -> p k c", k=4)
            for s, (c0, n) in enumerate(tl.subs):
                for half in range(2):
                    ob = bank[4 + 2 * s + half]
                    for kc in range(4):
                        f = piece * 4 + kc
                        P.add("pe", lambda e, kc=kc, f=f: e.matmul(
                            ob[0:n, :], lhsT=hidT[:, f, c0:c0 + n], rhs=wt[:, kc, half * 512:(half + 1) * 512],
                            start=(piece == 0 and kc == 0), stop=(piece == 7 and kc == 3)),
                            reads=[wr, r_hidT[f]], writes=[r_bank[4 + 2 * s + half]])
            issue_wload()
            if piece == 7:
                for s, (c0, n) in enumerate(tl.subs):
                    for half in range(2):
                        ob = bank[4 + 2 * s + half]
                        P.add("dve", lambda e: e.tensor_tensor(
                            out=xh[p][s][0:n, half * 512:(half + 1) * 512], in0=ob[0:n, :],
                            in1=xh[p][s][0:n, half * 512:(half + 1) * 512], op=ALU.add),
                            writes=[r_bank[4 + 2 * s + half], r_xh[p][s]])


        def stage_J(tl):
            p = tl.xp
            for s, (c0, n) in enumerate(tl.subs):
                ln_t, ln_r = rms_stats(xh[p][s][0:n, :], n, D, [r_xh[p][s]])
                rs_t, rs_r = exp_of(ln_t, ln_r, n, -0.5)
                yt, yr = xh[p][s], r_xh[p][s]
                P.add("dve", lambda e, yt=yt, n=n, s=s, rs_t=rs_t: e.scalar_tensor_tensor(
                    out=yt[0:n, :], in0=xh[p][s][0:n, :], scalar=rs_t[0:n, :], in1=gfin[0:n, :], op0=ALU.mult,
                    op1=ALU.mult), reads=[r_xh[p][s], rs_r] + r_small, writes=[yr])
                dst = tl.ydst(s)
                out_ops.append(P.add("sp", lambda e, yt=yt, n=n, dst=dst: e.dma_start(out=dst, in_=yt[0:n, :]),
                                     reads=[yr], dma=True, chan=("yo", id(yr)), final=True))

        tiles = []
        mt = Tile()
        mt.kind = "meta"
        mt.par = 1
        mt.xp = 2
        mt.T = 16
        mt.L = 16
        mt.subs = [(0, 16)]
        mt.xsrc = lambda s: meta_d
        mt.kv_out = lambda s: (pmk_d, pmv_d)

        def meta_after():
            P.add("dve", lambda e: e.tensor_copy(out=ucx[0][:, :, 0:2], in_=ucx[1][:, :, 16:18]),
                  reads=r_ucx[1], writes=[r_ucctx[0]])
        mt.after_conv = meta_after
        tiles.append(mt)

        for t in range(NTP):
            tl = Tile()
            tl.kind = "prompt"
            tl.t = t
            tl.par = t % 2
            tl.xp = t % 3
            tl.T = MT
            tl.L = MT
            tl.subs = [(0, 128), (128, 128)]
            tl.xsrc = lambda s, t=t: xp_d[t * MT + s * 128: t * MT + (s + 1) * 128, :]
            tl.ydst = lambda s, t=t: yp_d[t * MT + s * 128: t * MT + (s + 1) * 128, :]
            if t == NTP - 1:
                tl.kv_out = lambda s: (pk_d, pv_d) if s == 1 else None
            else:
                tl.kv_out = lambda s: None

            def after(t=t, tl=tl):
                p = tl.par
                if t == NTP - 1:
                    out_ops.append(P.add("sp", lambda e: e.dma_start(out=pconv_d, in_=ucx[p][:, :, MT:MT + 2]),
                                         reads=r_ucx[p], dma=True, chan="pconv", final=True))
                else:
                    P.add("dve", lambda e: e.tensor_copy(out=ucx[1 - p][:, :, 0:2], in_=ucx[p][:, :, MT:MT + 2]),
                          reads=r_ucx[p], writes=[r_ucctx[1 - p]])
            tl.after_conv = after

            def groups(s, h, t=t):
                bi = 2 * t + s
                hp = h % 2
                g = h // 4
                gl = []
                gl.append(dict(kk=KKm[hp * 64:(hp + 1) * 64, g, 0:17], kk_rs=[r_KKm], v=Vm[0:17, g, :], v_rs=[r_Vm], nk=17,
                               hank=(Hm0[0:17, h, :] if bi == 0 else None), anti=anti17[:],
                               cb=(CBm0[0:17, h:h + 1] if bi == 0 else CBm[0:17, h:h + 1]), slot=2))
                if bi >= 1:
                    sl = (bi - 1) % NRING
                    gl.append(dict(kk=KKr[hp * 64:(hp + 1) * 64, sl, g, :], kk_rs=[r_KKr[sl]], v=Vr[:, sl, g, :],
                                   v_rs=[r_Vr[sl]], nk=128, hank=Hprev[:, h, :], anti=anti128[:], cb=None, slot=0))
                sl = bi % NRING
                gl.append(dict(kk=KKr[hp * 64:(hp + 1) * 64, sl, g, :], kk_rs=[r_KKr[sl]], v=Vr[:, sl, g, :],
                               v_rs=[r_Vr[sl]], nk=128, hank=Hcur[:, h, :], anti=anti128[:], cb=None, slot=1))
                return gl
            tl.groups = groups
            tiles.append(tl)

        stl = Tile()
        stl.kind = "sample"
        stl.par = NTP % 2
        stl.xp = NTP % 3
        stl.T = 64
        stl.L = 32
        stl.subs = [(0, 32), (32, 32)]
        stl.xsrc = lambda s: xs_d[32 * s:32 * (s + 1), :]
        stl.ydst = lambda s: ys_d[32 * s:32 * (s + 1), :]
        stl.kv_out = lambda s: (sk_d[32 * s:32 * (s + 1), :], sv_d[32 * s:32 * (s + 1), :])

        def sample_after():
            p = stl.par
            v = ucx[p][:, :, 0:68].rearrange("p j (a w) -> p j a w", a=2)
            for j in range(4):
                out_ops.append(P.add("sp", lambda e, j=j: e.dma_start(out=sconvo_d[:, j, :, :], in_=v[:, j, :, 32:34]),
                                     reads=r_ucx[p], dma=True, chan="sconvo", final=True))
        stl.after_conv = sample_after

        def sgroups(s, h):
            hp = h % 2
            g = h // 4
            return [
                dict(kk=KKsm[hp * 64:(hp + 1) * 64, s, g, 0:17], kk_rs=[r_scache], v=Vsm[0:17, s, g, :], v_rs=[r_scache],
                     nk=17, hank=None, anti=None, cb=CBm[0:17, h:h + 1], slot=2),
                dict(kk=KKsw[hp * 64:(hp + 1) * 64, s, g, :], kk_rs=[r_scache], v=Vsw[:, s, g, :], v_rs=[r_scache],
                     nk=128, hank=Hsw[:, h, :], anti=anti128[:], cb=None, slot=0),
                dict(kk=KKsn[hp * 64:(hp + 1) * 64, g, 32 * s:32 * (s + 1)], kk_rs=[r_KKsn], v=Vsn[0:32, s, g, :],
                     v_rs=[r_Vsn[s]], nk=32, hank=Hsn[0:32, h, :], anti=anti32[:], cb=None, slot=1),
            ]
        stl.groups = sgroups
        tiles.append(stl)

        def sample_ctx_load():
            p = stl.par
            v = ucx[p][:, :, 0:68].rearrange("p j (a w) -> p j a w", a=2)
            for j in range(4):
                P.add("sp", lambda e, j=j: e.dma_start(out=v[:, j, :, 0:2], in_=sconv_d[:, j, :, :]),
                      writes=[r_ucctx[p]] + r_ucx[p], dma=True, chan="sctx")

        def sample_cache_prep():
            p = stl.par
            for i, src in enumerate((ck_d, cv_d)):
                for sq in range(2):
                    P.add("sp", lambda e, i=i, src=src, sq=sq: e.dma_start(out=cstv[sq][:, i, :], in_=src[sq]),
                          writes=[r_cstv[sq]], dma=True, chan=("cst", sq))
            for i, src in enumerate((cmk_d, cmv_d)):
                for sq in range(2):
                    P.add("sp", lambda e, i=i, src=src, sq=sq: e.dma_start(out=cstv[sq][0:16, 2 + i, :], in_=src[sq]),
                          writes=[r_cstv[sq]], dma=True, chan=("cst", sq))
            for sq in range(2):
                P.add("dve", lambda e, sq=sq: e.tensor_copy(out=Vsw[:, sq, :, 0:64],
                                                            in_=cstv[sq][:, 1, :].rearrange("k (g c) -> k g c", g=2)),
                      reads=[r_cstv[sq]], writes=[r_scache])
                P.add("dve", lambda e, sq=sq: e.tensor_copy(out=Vsm[0:16, sq, :, 0:64],
                                                            in_=cstv[sq][0:16, 3, :].rearrange("k (g c) -> k g c", g=2)),
                      reads=[r_cstv[sq]], writes=[r_scache])
                for cp in range(2):
                    P.add("dve", lambda e, cp=cp, sq=sq: e.tensor_copy(
                        out=kdup[:, sq, :, cp, :], in_=cstv[sq][:, 0, :].rearrange("k (g c) -> k g c", g=2)),
                        reads=[r_cstv[sq]], writes=[r_kdup])
                    P.add("dve", lambda e, cp=cp, sq=sq: e.tensor_copy(
                        out=kmdup[:, sq, :, cp, :], in_=cstv[sq][0:16, 2, :].rearrange("k (g c) -> k g c", g=2)),
                        reads=[r_cstv[sq]], writes=[r_kmdup])
            for s in range(2):
                for g in range(2):
                    P.add("pe", lambda e, s=s, g=g: e.transpose(
                        out=TRb[:, 0:128], in_=kdup[:, s, g, :, :].rearrange("k a c -> k (a c)"), identity=ident[:]),
                        reads=[r_kdup, r_const], writes=[r_TR])
                    P.add("dve", lambda e, s=s, g=g: e.tensor_copy(out=KKsw[:, s, g, :], in_=TRb[:, 0:128]),
                          writes=[r_TR, r_scache])
                    P.add("pe", lambda e, s=s, g=g: e.transpose(
                        out=TRb[:, 0:16], in_=kmdup[:, s, g, :, :].rearrange("k a c -> k (a c)"), identity=ident[0:16, 0:16]),
                        reads=[r_kmdup, r_const], writes=[r_TR])
                    P.add("dve", lambda e, s=s, g=g: e.tensor_copy(out=KKsm[:, s, g, 0:16], in_=TRb[:, 0:16]),
                          writes=[r_TR, r_scache])

        prep_weights()
        stage_A(tiles[0])
        stage_B(tiles[0])
        prep_mlp_weights()
        stage_A(tiles[1])
        sample_cache_prep()
        for _ in range(4):
            issue_wload()
        full = tiles[1:]
        prev = None
        stage_B(full[0])
        for i, tl in enumerate(full):
            nsub = len(tl.subs)
            tl.e1 = [None] * nsub
            tl.e2 = [None] * nsub
            tl.g1 = [None] * nsub
            nx = full[i + 1] if i + 1 < len(full) else None
            if nx is not None:
                nx.a1 = [None] * len(nx.subs)
                for s in range(len(nx.subs)):
                    stage_A0(nx, s)
            for s in range(nsub):
                stage_D(tl, s)
            seq = []
            if nx is not None:
                seq += [("n", stage_A1, 0), ("n", stage_A1, 1)]
            seq += [("m", stage_E2a, 0), ("m", stage_E2a, 1), ("i",), ("m", stage_E2b, 0), ("i",), ("m", stage_F, 0),
                    ("m", stage_E2b, 1)]
            if nx is not None:
                seq += [("n", stage_A2, 0), ("i",), ("n", stage_A2, 1)]
            else:
                seq += [("i",)]
            seq += [("m", stage_G1, 0), ("m", stage_F, 1), ("i",), ("m", stage_G2, 0),
                    ("m", stage_G1, 1), ("i",), ("m", stage_G2, 1), ("i",), ("i",), ("i",)]
            piece = 0
            for it in seq:
                if it[0] == "m":
                    it[1](tl, it[2])
                elif it[0] == "n":
                    it[1](nx, it[2])
                elif prev is not None:
                    stage_I_piece(prev, piece)
                    piece += 1
            if prev is not None:
                assert piece == 8
                stage_J(prev)
            hooks = None
            if nx is not None:
                bgen = stage_B_gen(nx)

                def step(n, bgen=bgen):
                    def f():
                        for _ in range(n):
                            try:
                                next(bgen)
                            except StopIteration:
                                pass
                    return f
                hooks = {0: [step(1)], 1: [step(1)], 2: [step(2)], 3: [step(1)], 4: [step(2)], 5: [step(1)],
                         6: [step(2)], 7: [step(20)]}
            stage_H(tl, hooks)
            prev = tl
        for piece in range(8):
            stage_I_piece(prev, piece)
        stage_J(prev)

        P.emit(nc)
    return nc, P


_CACHE = {}


def _get_nc(S_TOK):
    if S_TOK not in _CACHE:
        _CACHE[S_TOK] = build(S_TOK)
    return _CACHE[S_TOK][0]


def kernel(x_prompt, x_sample, cache_k, cache_v, cache_meta_k, cache_meta_v, state_conv, meta_tokens,
           norm_mix, w_in, conv_w, attn_sinks, rel_bias_table, norm_conv_out, norm_attn_out, w_out,
           norm_mlp, w_up, w_down, norm_final):
    f = lambda a: np.ascontiguousarray(np.asarray(a, dtype=np.float32))
    x_prompt = f(x_prompt)
    x_sample = f(x_sample)
    B, S_TOK, _ = x_prompt.shape
    ncores = 8
    assert B == ncores
    nc = _get_nc(S_TOK)
    w_in0 = f(w_in)[0]
    k0 = w_in0[:, 2048:2112]
    k1 = w_in0[:, 2112:2176]
    win_x = np.ascontiguousarray(np.concatenate([w_in0, k0, k0, k1, k1], axis=1))
    pk = lambda v: np.ascontiguousarray(v.reshape(-1, 128).T)
    gout = np.concatenate([f(norm_conv_out)[0], f(norm_attn_out)[0]])
    convw = np.ascontiguousarray(f(conv_w)[0].reshape(3, 4, 128).transpose(2, 1, 0))
    common = {
        "meta": f(meta_tokens), "gmix": f(norm_mix).reshape(1, D), "gmlp": f(norm_mlp).reshape(1, D),
        "gcv": pk(f(norm_conv_out)[0]), "gatt": f(norm_attn_out).reshape(1, Q_DIM),
        "gfin": f(norm_final).reshape(1, D), "convw": convw, "sinks": f(attn_sinks).reshape(1, 8),
        "table": f(rel_bias_table), "oh": _onehot_const(), "win": win_x, "wout": f(w_out)[0],
        "wup": f(w_up)[0], "wdn": f(w_down)[0],
    }
    ck = f(cache_k)[0].reshape(16, 128, 128)
    cv = f(cache_v)[0].reshape(16, 128, 128)
    cmk = f(cache_meta_k)[0].reshape(16, 16, 128)
    cmv = f(cache_meta_v)[0].reshape(16, 16, 128)
    sc = f(state_conv)[0]
    in_maps = []
    for c in range(ncores):
        m = dict(common)
        m["xp"] = x_prompt[c]
        m["xs"] = np.ascontiguousarray(x_sample[2 * c:2 * c + 2].reshape(64, D))
        m["ck"] = np.ascontiguousarray(ck[2 * c:2 * c + 2])
        m["cv"] = np.ascontiguousarray(cv[2 * c:2 * c + 2])
        m["cmk"] = np.ascontiguousarray(cmk[2 * c:2 * c + 2])
        m["cmv"] = np.ascontiguousarray(cmv[2 * c:2 * c + 2])
        m["sconv"] = np.ascontiguousarray(sc[2 * c:2 * c + 2].reshape(2, 2, 4, 128).transpose(3, 2, 0, 1))
        in_maps.append(m)
    res = run_bass_kernel_spmd(nc, in_maps, core_ids=list(range(ncores)))
    R = res.results
    y_prompt = np.stack([R[c]["yp"] for c in range(ncores)]).astype(np.float32)
    y_sample = np.concatenate([R[c]["ys"].reshape(2, 32, D) for c in range(ncores)]).astype(np.float32)
    p_k = np.stack([R[c]["pk"].reshape(128, 2, 64) for c in range(ncores)])[None].astype(np.float32)
    p_v = np.stack([R[c]["pv"].reshape(128, 2, 64) for c in range(ncores)])[None].astype(np.float32)
    p_mk = np.stack([R[c]["pmk"].reshape(16, 2, 64) for c in range(ncores)])[None].astype(np.float32)
    p_mv = np.stack([R[c]["pmv"].reshape(16, 2, 64) for c in range(ncores)])[None].astype(np.float32)
    p_conv = np.stack([R[c]["pconv"].transpose(2, 1, 0).reshape(2, 512) for c in range(ncores)])[None].astype(np.float32)
    s_k = np.concatenate([R[c]["sk"].reshape(2, 32, 2, 64) for c in range(ncores)])[None].astype(np.float32)
    s_v = np.concatenate([R[c]["sv"].reshape(2, 32, 2, 64) for c in range(ncores)])[None].astype(np.float32)
    s_conv = np.concatenate([R[c]["sconvo"].transpose(2, 3, 1, 0).reshape(2, 2, 512) for c in range(ncores)])[None].astype(np.float32)
    return (y_prompt, y_sample, p_k, p_v, p_mk, p_mv, p_conv, s_k, s_v, s_conv)
```

```python
import contextlib
import math
import types
import numpy as np
import concourse.bass as bass
import concourse.mybir as mybir
from concourse.bass_utils import run_bass_kernel_spmd

F32 = mybir.dt.float32
BF16 = mybir.dt.bfloat16
AF = mybir.ActivationFunctionType
ALU = mybir.AluOpType

D = 1024
SEQ = 8192
N_META = 16
CONV_DIM = 512
Q_DIM = 512
IN_DIM = 2304
IN_X = 2560
D_FF = 4096
EPS = 1e-6
MT = 256
NEG = -30000.0
PAST_LEN = 4096
DBG_STOP = False

HSEG = [("prev", 128, 128, -128), ("cur", 128, 128, 0), ("meta0", 16, 128, -16),
        ("sw", 128, 32, -128), ("sn", 32, 32, 0)]
HOFF = {}
_o = 0
for _n, _nk, _T, _db in HSEG:
    HOFF[_n] = (_o, _nk, _T, _db)
    _o += _nk + _T - 1
HLEN = _o


def _t5_bucket_np(rp):
    rp = np.asarray(rp, dtype=np.int32)
    nb = 16
    max_exact = 8
    ret = np.where(rp > 0, nb, 0)
    n = np.abs(rp)
    nf = np.maximum(n, 1).astype(np.float32)
    large = max_exact + (np.log(nf / np.float32(max_exact)) / np.float32(math.log(128 / max_exact))
                         * np.float32(nb - max_exact)).astype(np.int32)
    large = np.minimum(large, nb - 1)
    return ret + np.where(n < max_exact, n, large)


def _bucket(rp):
    return _t5_bucket_np(rp)


def _onehot_const():
    oh = np.zeros((32, HLEN), np.float32)
    for name, (off, nk, T, db) in HOFF.items():
        j = np.arange(nk + T - 1)
        d = db + (nk - 1) - j
        b = _bucket(d)
        oh[b, off + j] = 1.0
    return oh


def _freeze(fn):
    if fn.__closure__ is None:
        return fn
    cells = tuple(types.CellType(c.cell_contents) for c in fn.__closure__)
    return types.FunctionType(fn.__code__, fn.__globals__, fn.__name__, fn.__defaults__, cells)


class Res:
    __slots__ = ("name", "lw", "rd")

    def __init__(self, name):
        self.name = name
        self.lw = None
        self.rd = []


class Op:
    __slots__ = ("eng", "fn", "deps", "dma", "chan", "sig", "hasdep", "idx")


class Prog:
    ENGS = ("pe", "act", "dve", "pool", "sp")

    def __init__(self):
        self.ops = []
        self.chan_cnt = {}
        self.eng_cnt = {e: 0 for e in self.ENGS}
        self.final = []

    def add(self, eng, fn, reads=(), writes=(), dma=False, chan=None, final=False):
        op = Op()
        op.eng = eng
        op.fn = _freeze(fn)
        op.dma = dma
        op.hasdep = False
        op.sig = None
        op.idx = len(self.ops)
        deps = []
        for r in reads:
            if r.lw is not None:
                deps.append(r.lw)
        for w in writes:
            if w.lw is not None:
                deps.append(w.lw)
            deps.extend(w.rd)
        seen = set()
        od = []
        for d in deps:
            if d is op or id(d) in seen:
                continue
            seen.add(id(d))
            if eng == "pe" and d.eng == "pe" and not d.dma and not dma:
                continue
            d.hasdep = True
            od.append(d)
        op.deps = od
        for r in reads:
            r.rd.append(op)
        for w in writes:
            w.lw = op
            w.rd = []
        if dma:
            op.chan = chan if chan is not None else (writes[0] if writes else ("dma", eng))
        else:
            op.chan = None
        if dma:
            op.hasdep = True
        if final:
            op.hasdep = True
            self.final.append(op)
        self.ops.append(op)
        return op

    def emit(self, nc):
        chan_keys = []
        for op in self.ops:
            if not op.hasdep:
                continue
            if op.dma:
                k = op.chan if isinstance(op.chan, (str, tuple)) else id(op.chan)
                if k not in self.chan_cnt:
                    self.chan_cnt[k] = 0
                    chan_keys.append(k)
                self.chan_cnt[k] += 16
                op.sig = (("c", k), self.chan_cnt[k])
            else:
                self.eng_cnt[op.eng] += 1
                op.sig = (("e", op.eng), self.eng_cnt[op.eng])
        with contextlib.ExitStack() as st:
            sems = {}
            for e in self.ENGS:
                sems[("e", e)] = st.enter_context(nc.semaphore("s_" + e))
            for i, k in enumerate(chan_keys):
                sems[("c", k)] = st.enter_context(nc.semaphore("c_%d" % i))
            self.nsem = len(sems)
            block = st.enter_context(nc.Block())

            def run(engname, e):
                known = {}
                for op in self.ops:
                    if op.eng != engname:
                        continue
                    for d in op.deps:
                        sk, v = d.sig
                        if known.get(sk, 0) < v:
                            e.wait_ge(sems[sk], v)
                            known[sk] = v
                    ins = op.fn(e)
                    if op.sig is not None:
                        sk, v = op.sig
                        ins.then_inc(sems[sk], 16 if op.dma else 1)
                if engname == "sp":
                    fin = {}
                    for op in self.final:
                        sk, v = op.sig
                        fin[sk] = max(fin.get(sk, 0), v)
                    for sk, v in fin.items():
                        if known.get(sk, 0) < v:
                            e.wait_ge(sems[sk], v)
                            known[sk] = v

            @block.tensor
            def _(e):
                run("pe", e)

            @block.scalar
            def _(e):
                run("act", e)

            @block.vector
            def _(e):
                run("dve", e)

            @block.gpsimd
            def _(e):
                run("pool", e)

            @block.sync
            def _(e):
                run("sp", e)


class Rot:
    def __init__(self, alloc, name, n, shape, dt):
        self.t = [alloc("%s%d" % (name, i), shape, dt) for i in range(n)]
        self.r = [Res("%s%d" % (name, i)) for i in range(n)]
        self.i = 0

    def get(self):
        i = self.i
        self.i = (i + 1) % len(self.t)
        return self.t[i], self.r[i]


def build(S_TOK):
    assert S_TOK % MT == 0
    NTP = S_TOK // MT
    nc = bass.Bass("TRN2", target_bir_lowering=False)

    def din(name, shape, dt=F32):
        return nc.dram_tensor(name, list(shape), dt, kind="ExternalInput").ap()

    def dout(name, shape, dt=F32):
        return nc.dram_tensor(name, list(shape), dt, kind="ExternalOutput").ap()

    xp_d = din("xp", [S_TOK, D])
    xs_d = din("xs", [64, D])
    ck_d = din("ck", [2, 128, 128])
    cv_d = din("cv", [2, 128, 128])
    cmk_d = din("cmk", [2, 16, 128])
    cmv_d = din("cmv", [2, 16, 128])
    sconv_d = din("sconv", [128, 4, 2, 2])
    meta_d = din("meta", [16, D])
    gmix_d = din("gmix", [1, D])
    gmlp_d = din("gmlp", [1, D])
    gcv_d = din("gcv", [128, 4])
    gatt_d = din("gatt", [1, Q_DIM])
    gfin_d = din("gfin", [1, D])
    convw_d = din("convw", [128, 4, 3])
    sinks_d = din("sinks", [1, 8])
    table_d = din("table", [32, 8])
    oh_d = din("oh", [32, HLEN])
    win_d = din("win", [D, IN_X])
    wout_d = din("wout", [D, D])
    wup_d = din("wup", [D, D_FF])
    wdn_d = din("wdn", [D_FF, D])

    yp_d = dout("yp", [S_TOK, D])
    ys_d = dout("ys", [64, D])
    pk_d = dout("pk", [128, 128])
    pv_d = dout("pv", [128, 128])
    pmk_d = dout("pmk", [16, 128])
    pmv_d = dout("pmv", [16, 128])
    pconv_d = dout("pconv", [128, 4, 2])
    sk_d = dout("sk", [64, 128])
    sv_d = dout("sv", [64, 128])
    sconvo_d = dout("sconvo", [128, 4, 2, 2])

    gscr = nc.dram_tensor("gscr", [8, HLEN], BF16, kind="Internal").ap()
    wup_s = nc.dram_tensor("wup_s", [8, 128, 8, 512], BF16, kind="Internal").ap()
    wdn_s = nc.dram_tensor("wdn_s", [8, 128, 4, 1024], BF16, kind="Internal").ap()
    r_gscr = Res("gscr")
    r_wup_s = [Res("wup_s%d" % i) for i in range(8)]
    r_wdn_s = [Res("wdn_s%d" % i) for i in range(8)]

    P = Prog()
    with contextlib.ExitStack() as st:
        def sb(name, shape, dt):
            return st.enter_context(nc.sbuf_tensor("s_" + name, list(shape), dt))

        def ps(name, shape, dt):
            return st.enter_context(nc.psum_tensor("p_" + name, list(shape), dt))

        bank = [ps("bank%d" % i, [128, 512], F32) for i in range(8) if i != 2]
        bank.insert(2, None)
        TRb = ps("TRb", [128, 1024], BF16)
        r_bank = [Res("bank%d" % i) for i in range(8)]
        r_TR = r_bank[2]
        ssqc_ps = bank[3][:, 0:4]
        r_ssqc_ps = r_bank[3]
        gv_ps = bank[4]
        fmb_i = [0]

        def fmb_get():
            i = fmb_i[0]
            fmb_i[0] = 1 - i
            return bank[i], r_bank[i]

        xh = [[sb("xh%d%d" % (p, s), [128, D], F32) for s in range(2)] for p in range(3)]
        r_xh = [[Res("xh%d%d" % (p, s)) for s in range(2)] for p in range(3)]
        xsb = Rot(sb, "xsb", 2, [128, D], BF16)
        yaraw = Rot(sb, "yaraw", 2, [128, 512], F32)
        yas = Rot(sb, "yas", 2, [128, 512], BF16)

        ident = sb("ident", [128, 128], BF16)
        anti128 = sb("anti128", [128, 128], BF16)
        anti32 = sb("anti32", [32, 32], BF16)
        anti17 = sb("anti17", [17, 17], BF16)
        ones_col = sb("ones_col", [128, 1], BF16)
        iot = yaraw.t[0][:, 0:128]
        r_const = Res("const")
        r_iot = yaraw.r[0]
        P.add("pool", lambda e: e.iota(iot[:], pattern=[[1, 128]], base=0, channel_multiplier=-1,
                                       allow_small_or_imprecise_dtypes=True), writes=[r_iot])
        P.add("dve", lambda e: e.tensor_scalar(out=ident[:], in0=iot[:], scalar1=0.0, scalar2=None,
                                               op0=ALU.is_equal), reads=[r_iot], writes=[r_const])
        iot2 = yaraw.t[1][:, 0:128]
        r_iot2 = yaraw.r[1]
        P.add("pool", lambda e: e.iota(iot2[:], pattern=[[1, 128]], base=0, channel_multiplier=1,
                                       allow_small_or_imprecise_dtypes=True), writes=[r_iot2])
        P.add("dve", lambda e: e.tensor_scalar(out=anti128[:], in0=iot2[:], scalar1=127.0, scalar2=None,
                                               op0=ALU.is_equal), reads=[r_iot2], writes=[r_const])
        P.add("dve", lambda e: e.tensor_scalar(out=anti32[:], in0=iot2[0:32, 0:32], scalar1=31.0, scalar2=None,
                                               op0=ALU.is_equal), reads=[r_iot2], writes=[r_const])
        P.add("dve", lambda e: e.tensor_scalar(out=anti17[:], in0=iot2[0:17, 0:17], scalar1=15.0, scalar2=None,
                                               op0=ALU.is_equal), reads=[r_iot2], writes=[r_const])
        P.add("dve", lambda e: e.memset(ones_col[:], 1.0), writes=[r_const])

        gmix = sb("gmix", [128, D], F32)
        gmlp = sb("gmlp", [128, D], F32)
        gatt = sb("gatt", [128, Q_DIM], F32)
        gcv = sb("gcv", [128, 4], F32)
        convw = sb("convw", [128, 4, 3], F32)
        gfin = sb("gfin", [128, D], F32)
        table = sb("table", [32, 8], F32)
        ohs = xh[2][1][0:32, 0:HLEN]
        CBm0 = sb("CBm0", [17, 8], F32)
        CBm = sb("CBm", [17, 8], F32)
        r_small = [Res("small%d" % i) for i in range(7)]
        r_cb = [Res("cb%d" % i) for i in range(3)]
        for i, (tl, src) in enumerate(((table, table_d), (gmix, gmix_d.partition_broadcast(128)), (convw, convw_d),
                                       (gcv, gcv_d), (gatt, gatt_d.partition_broadcast(128)),
                                       (gmlp, gmlp_d.partition_broadcast(128)),
                                       (gfin, gfin_d.partition_broadcast(128)))):
            P.add("sp", lambda e, tl=tl, src=src: e.dma_start(out=tl[:], in_=src), writes=[r_small[i]], dma=True)
        P.add("dve", lambda e: e.memset(CBm0[:], 0.0), writes=[r_cb[0]])
        P.add("sp", lambda e: e.dma_start(out=CBm0[16:17, :], in_=sinks_d), writes=[r_cb[0]], dma=True)
        P.add("sp", lambda e: e.dma_start(out=CBm[16:17, :], in_=sinks_d), writes=[r_cb[1]], dma=True)
        P.add("sp", lambda e: e.dma_start(out=CBm[0:16, :], in_=table_d[15:16, :].partition_broadcast(16)),
              writes=[r_cb[2]], dma=True)

        gvb = xsb.t[1][0:8, 0:HLEN]
        r_gvb = xsb.r[1]
        P.add("sp", lambda e: e.dma_start(out=ohs, in_=oh_d), writes=[r_xh[2][1]], dma=True)
        Hprev = sb("Hprev", [128, 8, 128], BF16)
        Hcur = sb("Hcur", [128, 8, 128], BF16)
        Hm0 = sb("Hm0", [17, 8, 128], BF16)
        Hsw = sb("Hsw", [128, 8, 32], BF16)
        Hsn = sb("Hsn", [32, 8, 32], BF16)
        r_Hd = {n_: Res("H_" + n_) for n_ in ("prev", "cur", "meta0", "sw", "sn")}
        r_H = list(r_Hd.values())
        for c0 in range(0, HLEN, 512):
            c1 = min(HLEN, c0 + 512)
            P.add("pe", lambda e, c0=c0, c1=c1: e.matmul(gv_ps[0:8, 0:c1 - c0], lhsT=table[:], rhs=ohs[:, c0:c1],
                                                         start=True, stop=True),
                  reads=[r_small[0], r_xh[2][1]], writes=[r_bank[4]])
            P.add("dve", lambda e, c0=c0, c1=c1: e.tensor_copy(out=gvb[:, c0:c1], in_=gv_ps[0:8, 0:c1 - c0]),
                  writes=[r_bank[4], r_gvb])
        P.add("sp", lambda e: e.dma_start(out=gscr, in_=gvb), reads=[r_gvb], writes=[r_gscr], dma=True)
        P.add("dve", lambda e: e.memset(Hm0[:], 0.0), writes=[r_Hd["meta0"]])
        for tl, name in ((Hprev, "prev"), (Hcur, "cur"), (Hm0, "meta0"), (Hsw, "sw"), (Hsn, "sn")):
            off, nk, T, db = HOFF[name]
            src = bass.AP(gscr.tensor, off, [[1, nk], [HLEN, 8], [1, T]])
            P.add("sp", lambda e, tl=tl, src=src, nk=nk: e.dma_start(out=tl[0:nk, :, :], in_=src),
                  reads=[r_gscr], writes=[r_Hd[name]], dma=True)
        P.add("dve", lambda e: e.memset(Hprev[64:128, :, 64:128], NEG), writes=[r_Hd["prev"]])
        P.add("dve", lambda e: e.memset(Hcur[0:64, :, 0:64], NEG), writes=[r_Hd["cur"]])

        WIN = sb("WIN", [128, 8, IN_X], BF16)
        WOUT = sb("WOUT", [128, 8, D], BF16)
        r_WIN = [[Res("WIN%d_%d" % (k, c)) for c in range(2)] for k in range(8)]
        r_WOUT = [Res("WOUT%d" % c) for c in range(2)]

        def prep_weights():
            for k in range(8):
                for c in range(2):
                    P.add("pool", lambda e: e.dma_start(out=WIN[:, k, c * 1280:(c + 1) * 1280],
                                                        in_=win_d[k * 128:(k + 1) * 128, c * 1280:(c + 1) * 1280]),
                          writes=[r_WIN[k][c]], dma=True)
            for c in range(2):
                src = wout_d[:, c * 512:(c + 1) * 512].rearrange("(k p) c -> p k c", p=128)
                P.add("pool", lambda e: e.dma_start(out=WOUT[:, :, c * 512:(c + 1) * 512], in_=src),
                      writes=[r_WOUT[c]], dma=True)

        def prep_mlp_weights():
            for i in range(8):
                src = wup_d[:, i * 512:(i + 1) * 512].rearrange("(k p) c -> p k c", p=128)
                P.add("pool", lambda e: e.dma_start(out=wup_s[i], in_=src), reads=[r_KKm], writes=[r_wup_s[i]],
                      dma=True)
            for i in range(8):
                src = wdn_d[i * 512:(i + 1) * 512, :].rearrange("(k p) c -> p k c", p=128)
                P.add("pool", lambda e: e.dma_start(out=wdn_s[i], in_=src), reads=[r_KKm], writes=[r_wdn_s[i]],
                      dma=True)


        junk = Rot(sb, "junk", 1, [128, D], BF16)
        _xsT = sb("xsT", [128, 8, MT], BF16)
        xsT = [_xsT, _xsT]
        _r_xsT = [Res("xsT%d" % s) for s in range(2)]
        r_xsT = [_r_xsT, _r_xsT]
        h1nT = sb("h1nT", [128, 8, MT], BF16)
        r_h1nT = [Res("h1nT%d" % s) for s in range(2)]
        qT = sb("qT", [128, 4, MT], BF16)
        r_qT = [Res("qT%d" % j) for j in range(4)]
        NRING = 4
        KKr = sb("KKr", [128, NRING, 2, 128], BF16)
        r_KKr = [Res("KKr%d" % i) for i in range(NRING)]
        Vr = sb("Vr", [128, NRING, 2, 65], BF16)
        r_Vr = [Res("Vr%d" % i) for i in range(NRING)]
        KKm = sb("KKm", [128, 2, 17], BF16)
        Vm = sb("Vm", [17, 2, 65], BF16)
        r_KKm = Res("KKm")
        r_Vm = Res("Vm")
        KKsm = sb("KKsm", [128, 2, 2, 17], BF16)
        Vsm = sb("Vsm", [17, 2, 2, 65], BF16)
        KKsw = sb("KKsw", [128, 2, 2, 128], BF16)
        Vsw = sb("Vsw", [128, 2, 2, 65], BF16)
        KKsn = sb("KKsn", [128, 2, 64], BF16)
        Vsn = sb("Vsn", [32, 2, 2, 65], BF16)
        r_scache = Res("scache")
        r_KKsn = Res("KKsn")
        r_Vsn = [Res("Vsn0"), Res("Vsn1")]
        ucx = [sb("ucx%d" % p, [128, 4, MT + 4], F32) for p in range(2)]
        r_ucx = [[Res("ucx%d%d" % (p, j)) for j in range(4)] for p in range(2)]
        r_ucctx = [Res("ucctx%d" % p) for p in range(2)]
        csb = Rot(sb, "csb", 2, [128, MT], F32)
        tmpy = Rot(sb, "tmpy", 2, [128, MT], F32)
        bsb = Rot(sb, "bsb", 2, [128, MT], F32)
        ycsq = sb("ycsq", [128, 4, MT], BF16)
        r_ycsq = [Res("ycsq%d" % j) for j in range(4)]
        ycT = sb("ycT", [128, 4, MT], BF16)
        r_ycT = [Res("ycT%d" % j) for j in range(4)]
        yaT = sb("yaT", [128, 4, MT], BF16)
        r_yaT = [Res("yaT%d" % s) for s in range(2)]
        stat = Rot(sb, "stat", 48, [128, 1], F32)
        rec8 = Rot(sb, "rec8", 2, [128, 8], F32)
        hidT = sb("hidT", [128, 32, MT], BF16)
        r_hidT = [Res("hidT%d" % f) for f in range(32)]
        rtmp = Rot(sb, "rtmp", 2, [128, 2 * MT], F32)
        ring = Rot(sb, "ring", 4, [128, 4096], BF16)
        kvst = Rot(sb, "kvst", 1, [128, 256], F32)
        kdup = yas.t[0][:].rearrange("p (s g a c) -> p s g a c", s=2, g=2, a=2)
        kmdup = yas.t[1][0:16, :].rearrange("p (s g a c) -> p s g a c", s=2, g=2, a=2)
        r_kmdup = yas.r[1]
        cstv = [rtmp.t[q][:].rearrange("p (a c) -> p a c", a=4) for q in range(2)]
        r_cstv = [rtmp.r[q] for q in range(2)]
        r_kdup = yas.r[0]

        P.add("pool", lambda e: e.memset(Vr[:], 1.0), writes=r_Vr)
        P.add("pool", lambda e: e.memset(Vm[:], 1.0), writes=[r_Vm])
        P.add("pool", lambda e: e.memset(Vm[:, :, 0:64], 0.0), writes=[r_Vm])
        P.add("pool", lambda e: e.memset(Vsm[:], 1.0), writes=[r_scache])
        P.add("pool", lambda e: e.memset(Vsm[:, :, :, 0:64], 0.0), writes=[r_scache])
        P.add("pool", lambda e: e.memset(Vsw[:], 1.0), writes=[r_scache])
        P.add("pool", lambda e: e.memset(Vsn[:], 1.0), writes=r_Vsn)
        P.add("pool", lambda e: e.memset(KKm[:], 0.0), writes=[r_KKm])
        P.add("pool", lambda e: e.memset(KKsm[:], 0.0), writes=[r_scache])
        P.add("pool", lambda e: e.memset(KKr[:], 0.0), writes=r_KKr)

        out_ops = []

        def rms_stats(src_ap, n_tok, nfeat, reads):
            jt, jr = junk.get()
            s1, r1 = stat.get()
            P.add("act", lambda e: e.activation(out=jt[0:n_tok, 0:nfeat], in_=src_ap, func=AF.Square,
                                                accum_out=s1[0:n_tok, :]), reads=reads, writes=[jr, r1])
            s2, r2 = stat.get()
            P.add("act", lambda e: e.activation(out=s2[0:n_tok, :], in_=s1[0:n_tok, :], func=AF.Ln,
                                                scale=1.0 / nfeat, bias=EPS), reads=[r1], writes=[r2])
            return s2, r2

        def exp_of(ln_t, ln_r, n_tok, scale):
            s3, r3 = stat.get()
            P.add("act", lambda e: e.activation(out=s3[0:n_tok, :], in_=ln_t[0:n_tok, :], func=AF.Exp, scale=scale),
                  reads=[ln_r], writes=[r3])
            return s3, r3

        def norm_part1(src_t, src_r, n_tok, gain):
            ln_t, ln_r = rms_stats(src_t[0:n_tok, :], n_tok, D, [src_r])
            rs_t, rs_r = exp_of(ln_t, ln_r, n_tok, -0.5)
            xb, xbr = xsb.get()
            P.add("dve", lambda e: e.scalar_tensor_tensor(out=xb[0:n_tok, :], in0=src_t[0:n_tok, :],
                                                          scalar=rs_t[0:n_tok, :], in1=gain[0:n_tok, :],
                                                          op0=ALU.mult, op1=ALU.mult),
                  reads=[src_r, rs_r] + r_small, writes=[xbr])
            return xb, xbr

        def norm_part2(xb, xbr, n_tok, dstT, dst_r, col0):
            for k in range(8):
                P.add("pe", lambda e, k=k: e.transpose(out=TRb[:, k * 128:k * 128 + n_tok],
                                                       in_=xb[0:n_tok, k * 128:(k + 1) * 128],
                                                       identity=ident[0:n_tok, 0:n_tok]),
                      reads=[xbr, r_const], writes=[r_TR])
            src = TRb[:].rearrange("p (k t) -> p k t", k=8)[:, :, 0:n_tok]
            P.add("dve", lambda e: e.tensor_copy(out=dstT[:, :, col0:col0 + n_tok], in_=src),
                  writes=[r_TR, dst_r])

        def norm_transpose(src_t, src_r, n_tok, dstT, dst_r, col0, gain):
            xb, xbr = norm_part1(src_t, src_r, n_tok, gain)
            norm_part2(xb, xbr, n_tok, dstT, dst_r, col0)


        def inproj_cols(xT, xT_rs, T, cols):
            bt, br = fmb_get()
            for i, col in enumerate(cols):
                for k in range(8):
                    P.add("pe", lambda e, k=k, i=i, col=col: e.matmul(
                        bt[:, i * 256:i * 256 + T], lhsT=WIN[:, k, col:col + 128], rhs=xT[:, k, 0:T],
                        start=(k == 0), stop=(k == 7)), reads=r_WIN[k] + xT_rs, writes=[br])
            return bt, br

        def kv_tokmajor(xT, xT_r, col0, n_tok, bi):
            for k in range(8):
                P.add("pe", lambda e, k=k: e.matmul(bank[bi][0:n_tok, 0:256], lhsT=xT[:, k, col0:col0 + n_tok],
                                                    rhs=WIN[:, k, 2048:2304], start=(k == 0), stop=(k == 7)),
                      reads=r_WIN[k] + [xT_r], writes=[r_bank[bi]])

        PTh = Rot(sb, "PTh", 3, [128, 3, 128], BF16)

        def attention(T, qcol, groups_fn):
            pend = None
            for h in range(8):
                gl = groups_fn(h)
                hp = h % 2
                sbk = bank[h % 2]
                sbr = r_bank[h % 2]
                for g in gl:
                    nk = g["nk"]
                    sl = g["slot"]
                    P.add("pe", lambda e, g=g, nk=nk, sl=sl, hp=hp, h=h: e.matmul(
                        sbk[0:nk, sl * 128:sl * 128 + T], lhsT=g["kk"],
                        rhs=qT[hp * 64:(hp + 1) * 64, h // 2, qcol:qcol + T], start=True, stop=(g["hank"] is None)),
                        reads=g["kk_rs"] + [r_qT[h // 2]], writes=[sbr])
                    if g["hank"] is not None:
                        P.add("pe", lambda e, g=g, nk=nk, sl=sl: e.matmul(
                            sbk[0:nk, sl * 128:sl * 128 + T], lhsT=g["anti"], rhs=g["hank"], start=False, stop=True),
                            reads=r_H + [r_const], writes=[sbr])
                pt, pr = PTh.get()
                full = [g for g in gl if g["nk"] == 128 and g["cb"] is None]
                rest = [g for g in gl if not (g["nk"] == 128 and g["cb"] is None)]
                if full:
                    s0 = min(g["slot"] for g in full)
                    s1 = max(g["slot"] for g in full) + 1
                    assert s1 - s0 == len(full)
                    if T == 128:
                        P.add("act", lambda e, pt=pt, s0=s0, s1=s1: e.activation(
                            out=pt[:, s0:s1, :], in_=sbk[:, s0 * 128:s1 * 128].rearrange("p (a c) -> p a c", c=128),
                            func=AF.Exp), reads=[], writes=[sbr, pr])
                    else:
                        for g in full:
                            sl = g["slot"]
                            P.add("act", lambda e, pt=pt, sl=sl: e.activation(
                                out=pt[:, sl, 0:T], in_=sbk[:, sl * 128:sl * 128 + T], func=AF.Exp),
                                writes=[sbr, pr])
                for g in rest:
                    nk = g["nk"]
                    sl = g["slot"]
                    if g["cb"] is None:
                        P.add("act", lambda e, pt=pt, sl=sl, nk=nk: e.activation(
                            out=pt[0:nk, sl, 0:T], in_=sbk[0:nk, sl * 128:sl * 128 + T], func=AF.Exp),
                            writes=[sbr, pr])
                    else:
                        P.add("act", lambda e, pt=pt, sl=sl, nk=nk, g=g: e.activation(
                            out=pt[0:nk, sl, 0:T], in_=sbk[0:nk, sl * 128:sl * 128 + T], func=AF.Exp, bias=g["cb"]),
                            reads=r_cb, writes=[sbr, pr])
                if pend is not None:
                    emit_pv(*pend)
                pend = (h, T, gl, pt, pr)
            emit_pv(*pend)

        def emit_pv(h, T, gl, pt, pr):
            ob = bank[4 + h // 4]
            obr = r_bank[4 + h // 4]
            hh = h % 4
            n = len(gl)
            for i, g in enumerate(gl):
                nk = g["nk"]
                sl = g["slot"]
                P.add("pe", lambda e, g=g, nk=nk, i=i, sl=sl: e.matmul(
                    ob[0:T, hh * 65:(hh + 1) * 65], lhsT=pt[0:nk, sl, 0:T], rhs=g["v"], start=(i == 0), stop=(i == n - 1)),
                    reads=[pr] + g["v_rs"], writes=[obr])


        def attn_E1(T):
            rc, rcr = rec8.get()
            yr, yrr = yaraw.get()
            for b in range(2):
                ob = bank[4 + b]
                o3 = ob[0:T, 0:260].rearrange("p (h c) -> p h c", h=4)
                P.add("dve", lambda e, o3=o3, b=b: e.reciprocal(out=rc[0:T, 4 * b:4 * b + 4], in_=o3[:, :, 64]),
                      writes=[r_bank[4 + b], rcr])
                P.add("dve", lambda e, o3=o3, b=b: e.tensor_tensor(
                    out=yr[0:T, 256 * b:256 * (b + 1)].rearrange("p (h c) -> p h c", h=4),
                    in0=o3[:, :, 0:64],
                    in1=rc[0:T, 4 * b:4 * b + 4].unsqueeze(2).to_broadcast([T, 4, 64]),
                    op=ALU.mult), reads=[rcr], writes=[r_bank[4 + b], yrr])
            return yr, yrr

        def attn_E2a(T, yr, yrr, lnc_t, lnc_r):
            lna_t, lna_r = rms_stats(yr[0:T, :], T, Q_DIM, [yrr])
            d_t, d_r = stat.get()
            P.add("dve", lambda e: e.tensor_tensor(out=d_t[0:T, :], in0=lnc_t[0:T, :], in1=lna_t[0:T, :],
                                                   op=ALU.subtract), reads=[lnc_r, lna_r], writes=[d_r])
            ratio_t, ratio_r = exp_of(d_t, d_r, T, 0.5)
            rstdc_t, rstdc_r = exp_of(lnc_t, lnc_r, T, -0.5)
            ys, ysr = yas.get()
            P.add("dve", lambda e: e.scalar_tensor_tensor(out=ys[0:T, :], in0=yr[0:T, :], scalar=ratio_t[0:T, :],
                                                          in1=gatt[0:T, :], op0=ALU.mult, op1=ALU.mult),
                  reads=[yrr, ratio_r] + r_small, writes=[ysr])
            return ys, ysr, rstdc_t, rstdc_r

        def attn_E2b(T, ys, ysr, yaT_col0, r_yaT_s):
            for j in range(4):
                P.add("pe", lambda e, j=j: e.transpose(out=TRb[:, j * 128:j * 128 + T], in_=ys[0:T, j * 128:(j + 1) * 128],
                                                       identity=ident[0:T, 0:T]),
                      reads=[ysr, r_const], writes=[r_TR])
            src = TRb[:, 0:512].rearrange("p (k t) -> p k t", k=4)[:, :, 0:T]
            P.add("dve", lambda e: e.tensor_copy(out=yaT[:, :, yaT_col0:yaT_col0 + T], in_=src),
                  writes=[r_TR, r_yaT_s])


        ring_state = {"pending": [], "left": 16 * (NTP + 1)}

        def mlp_weights_iter():
            while True:
                for i in range(8):
                    yield ("up", i)
                for i in range(8):
                    yield ("dn", i)

        wgen = mlp_weights_iter()

        def issue_wload():
            if ring_state["left"] <= 0:
                return
            ring_state["left"] -= 1
            kind, i = next(wgen)
            t, r = ring.get()
            src = (wup_s if kind == "up" else wdn_s)[i]
            rs = (r_wup_s if kind == "up" else r_wdn_s)[i]
            if kind == "up":
                dstv = t[:].rearrange("p (k c) -> p k c", k=8)
            else:
                dstv = t[:].rearrange("p (k c) -> p k c", k=4)
            P.add("pool", lambda e, dstv=dstv, src=src: e.dma_start(out=dstv, in_=src), reads=[rs], writes=[r], dma=True)
            ring_state["pending"].append((kind, i, t, r))

        def take_wload(kind, i):
            k2, i2, t, r = ring_state["pending"].pop(0)
            assert (k2, i2) == (kind, i)
            return t, r

        class Tile:
            pass

        def stage_A0(tl, s):
            p = tl.xp
            c0, n = tl.subs[s]
            src = tl.xsrc(s)
            P.add("sp", lambda e: e.dma_start(out=xh[p][s][0:n, :], in_=src), writes=[r_xh[p][s]], dma=True)

        def stage_A1(tl, s):
            p = tl.xp
            c0, n = tl.subs[s]
            tl.a1[s] = norm_part1(xh[p][s], r_xh[p][s], n, gmix)

        def stage_A2(tl, s):
            p = tl.par
            c0, n = tl.subs[s]
            xb, xbr = tl.a1[s]
            norm_part2(xb, xbr, n, xsT[p], r_xsT[p][s], c0)

        def stage_A(tl):
            tl.a1 = [None] * len(tl.subs)
            for s in range(len(tl.subs)):
                stage_A0(tl, s)
                stage_A1(tl, s)
                stage_A2(tl, s)


        def stage_B(tl):
            for _ in stage_B_gen(tl):
                pass

        def stage_B_gen(tl):
            if tl.kind == "sample":
                sample_ctx_load()
            p = tl.par
            T = tl.T
            xT = xsT[p]
            xT_rs = [r_xsT[p][s] for s in range(len(tl.subs))]
            bt, br = inproj_cols(xT, xT_rs, T, [2304, 2432])
            if tl.kind == "prompt":
                s0 = (2 * tl.t) % NRING
                for g in range(2):
                    P.add("dve", lambda e, bt=bt, g=g, s0=s0: e.tensor_copy(
                        out=KKr[:, s0:s0 + 2, g, :], in_=bt[:, g * 256:(g + 1) * 256].rearrange("p (a c) -> p a c", a=2)),
                        writes=[br, r_KKr[s0], r_KKr[s0 + 1]])
            elif tl.kind == "meta":
                for g in range(2):
                    P.add("dve", lambda e, bt=bt, g=g: e.tensor_copy(out=KKm[:, g, 0:16], in_=bt[:, g * 256:g * 256 + 16]),
                          writes=[br, r_KKm])
            else:
                for g in range(2):
                    P.add("dve", lambda e, bt=bt, g=g: e.tensor_copy(out=KKsn[:, g, :], in_=bt[:, g * 256:g * 256 + 64]),
                          writes=[br, r_KKsn])
            yield
            for s, (c0, n) in enumerate(tl.subs):
                if s > 0:
                    yield
                bi = 6 + s
                kv_tokmajor(xT, r_xsT[p][s], c0, n, bi)
                src_v = bank[bi][0:n, 128:256].rearrange("p (g c) -> p g c", g=2)
                if tl.kind == "prompt":
                    sl = (2 * tl.t + s) % NRING
                    P.add("act", lambda e, sl=sl, src_v=src_v: e.activation(out=Vr[:, sl, :, 0:64], in_=src_v, func=AF.Copy),
                          writes=[r_bank[bi], r_Vr[sl]])
                elif tl.kind == "meta":
                    P.add("act", lambda e, src_v=src_v: e.activation(out=Vm[0:16, :, 0:64], in_=src_v, func=AF.Copy),
                          writes=[r_bank[bi], r_Vm])
                else:
                    P.add("act", lambda e, s=s, src_v=src_v: e.activation(out=Vsn[0:32, s, :, 0:64], in_=src_v, func=AF.Copy),
                          writes=[r_bank[bi], r_Vsn[s]])
                outs = tl.kv_out(s)
                if outs is not None:
                    kd, vd = outs
                    kt, kr = kvst.get()
                    P.add("dve", lambda e, kt=kt, n=n, bi=bi: e.tensor_copy(out=kt[0:n, :], in_=bank[bi][0:n, 0:256]),
                          writes=[r_bank[bi], kr])
                    out_ops.append(P.add("sp", lambda e, kt=kt, n=n, kd=kd: e.dma_start(out=kd, in_=kt[0:n, 0:128]),
                                         reads=[kr], dma=True, chan=("kvo", id(kr), 0), final=True))
                    out_ops.append(P.add("sp", lambda e, kt=kt, n=n, vd=vd: e.dma_start(out=vd, in_=kt[0:n, 128:256]),
                                         reads=[kr], dma=True, chan=("kvo", id(kr), 1), final=True))
            L_ = tl.L
            nseg = T // L_
            W = L_ + 2
            for j in range(4):
                yield
                ba, bar = inproj_cols(xT, xT_rs, T, [512 + 128 * j, 1024 + 128 * j])
                ct, cr = csb.get()
                P.add("act", lambda e, ct=ct, ba=ba: e.activation(out=ct[:, 0:T], in_=ba[:, 0:T], func=AF.Copy),
                      writes=[bar, cr])
                ucv = ucx[p][:, j, 0:nseg * W].rearrange("p (a w) -> p a w", a=nseg)
                P.add("dve", lambda e, ct=ct, ba=ba, ucv=ucv: e.tensor_tensor(
                    out=ucv[:, :, 2:2 + L_], in0=ba[:, 256:256 + T].rearrange("p (a w) -> p a w", a=nseg),
                    in1=ct[:, 0:T].rearrange("p (a w) -> p a w", a=nseg), op=ALU.mult),
                    reads=[cr], writes=[bar, r_ucx[p][j]])
                if tl.kind == "meta":
                    continue
                yield
                bb, bbr = inproj_cols(xT, xT_rs, T, [0 + 128 * j, 1536 + 128 * j])
                bs, bsr = bsb.get()
                P.add("act", lambda e, bs=bs, bb=bb: e.activation(out=bs[:, 0:T], in_=bb[:, 0:T], func=AF.Copy),
                      writes=[bbr, bsr])
                P.add("act", lambda e, bb=bb, j=j: e.activation(out=qT[:, j, 0:T], in_=bb[:, 256:256 + T], func=AF.Copy,
                                                               scale=0.125), writes=[bbr, r_qT[j]])
                ty, tyr = tmpy.get()
                tyv = ty[:, 0:T].rearrange("p (a w) -> p a w", a=nseg)
                P.add("dve", lambda e, tyv=tyv, ucv=ucv, j=j: e.tensor_scalar(
                    out=tyv, in0=ucv[:, :, 0:L_], scalar1=convw[:, j, 0:1], scalar2=None, op0=ALU.mult),
                    reads=[r_ucx[p][j], r_ucctx[p]] + r_small, writes=[tyr])
                for tap in (1, 2):
                    P.add("dve", lambda e, tyv=tyv, ucv=ucv, j=j, tap=tap: e.scalar_tensor_tensor(
                        out=tyv, in0=ucv[:, :, tap:tap + L_], scalar=convw[:, j, tap:tap + 1], in1=tyv,
                        op0=ALU.mult, op1=ALU.add), reads=[r_ucx[p][j], r_ucctx[p]] + r_small, writes=[tyr])
                P.add("dve", lambda e, ty=ty, bs=bs, j=j: e.tensor_tensor(out=ty[:, 0:T], in0=ty[:, 0:T],
                                                                          in1=bs[:, 0:T], op=ALU.mult),
                      reads=[bsr], writes=[tyr])
                P.add("dve", lambda e, ty=ty, j=j: e.tensor_scalar(out=ycT[:, j, 0:T], in0=ty[:, 0:T],
                                                                   scalar1=gcv[:, j:j + 1], scalar2=None,
                                                                   op0=ALU.mult),
                      reads=[tyr] + r_small, writes=[r_ycT[j]])
                P.add("act", lambda e, ty=ty, j=j: e.activation(out=ycsq[:, j, 0:T], in_=ty[:, 0:T], func=AF.Square),
                      reads=[tyr], writes=[r_ycsq[j]])
            yield
            tl.after_conv()
            tl.lnc = []
            if tl.kind != "meta":
                for s, (c0, n) in enumerate(tl.subs):
                    for j in range(4):
                        P.add("pe", lambda e, j=j, c0=c0, n=n, s=s: e.matmul(
                            ssqc_ps[0:n, s:s + 1], lhsT=ycsq[:, j, c0:c0 + n], rhs=ones_col[:, 0:1],
                            start=(j == 0), stop=(j == 3)), reads=[r_ycsq[j], r_const], writes=[r_ssqc_ps])
                    sc, scr = stat.get()
                    P.add("dve", lambda e, sc=sc, n=n, s=s: e.tensor_copy(out=sc[0:n, :], in_=ssqc_ps[0:n, s:s + 1]),
                          writes=[r_ssqc_ps, scr])
                    l2, l2r = stat.get()
                    P.add("act", lambda e, sc=sc, l2=l2, n=n: e.activation(out=l2[0:n, :], in_=sc[0:n, :], func=AF.Ln,
                                                                         scale=1.0 / CONV_DIM, bias=EPS),
                          reads=[scr], writes=[l2r])
                    tl.lnc.append((l2, l2r))

        def stage_D(tl, s):
            c0, n = tl.subs[s]
            attention(n, c0, lambda h: tl.groups(s, h))
            tl.e1[s] = attn_E1(n)

        def stage_E2a(tl, s):
            c0, n = tl.subs[s]
            yr, yrr = tl.e1[s]
            tl.e2[s] = attn_E2a(n, yr, yrr, tl.lnc[s][0], tl.lnc[s][1])

        def stage_E2b(tl, s):
            c0, n = tl.subs[s]
            ys, ysr, _, _ = tl.e2[s]
            attn_E2b(n, ys, ysr, c0, r_yaT[s])

        def stage_F(tl, s):
            p = tl.xp
            c0, n = tl.subs[s]
            _, _, rstdc_t, rstdc_r = tl.e2[s]
            for half in range(2):
                ob = bank[half]
                for k in range(8):
                    if k < 4:
                        lhsT = ycT[:, k, c0:c0 + n]
                        rr = r_ycT[k]
                    else:
                        lhsT = yaT[:, k - 4, c0:c0 + n]
                        rr = r_yaT[s]
                    P.add("pe", lambda e, k=k: e.matmul(
                        ob[0:n, :], lhsT=lhsT, rhs=WOUT[:, k, half * 512:(half + 1) * 512],
                        start=(k == 0), stop=(k == 7)), reads=[rr, r_WOUT[half]], writes=[r_bank[half]])
                P.add("dve", lambda e: e.scalar_tensor_tensor(
                    out=xh[p][s][0:n, half * 512:(half + 1) * 512], in0=ob[0:n, :], scalar=rstdc_t[0:n, :],
                    in1=xh[p][s][0:n, half * 512:(half + 1) * 512], op0=ALU.mult, op1=ALU.add),
                    reads=[rstdc_r], writes=[r_bank[half], r_xh[p][s]])

        def stage_G1(tl, s):
            p = tl.xp
            c0, n = tl.subs[s]
            tl.g1[s] = norm_part1(xh[p][s], r_xh[p][s], n, gmlp)

        def stage_G2(tl, s):
            c0, n = tl.subs[s]
            xb, xbr = tl.g1[s]
            norm_part2(xb, xbr, n, h1nT, r_h1nT[s], c0)

        sq_i = [0]


        def stage_H(tl, hooks=None):
            T = tl.T
            hr = [r_h1nT[s] for s in range(len(tl.subs))]
            for piece in range(8):
                wt_, wr = take_wload("up", piece)
                wt = wt_[:].rearrange("p (k c) -> p k c", k=8)
                for pair in range(2):
                    f0 = piece * 4 + pair * 2
                    bt, br = fmb_get()
                    for i in range(2):
                        fi = pair * 2 + i
                        for k in range(8):
                            P.add("pe", lambda e, bt=bt, wt=wt, k=k, fi=fi, i=i: e.matmul(
                                bt[:, i * 256:i * 256 + T], lhsT=wt[:, k, fi * 128:(fi + 1) * 128], rhs=h1nT[:, k, 0:T],
                                start=(k == 0), stop=(k == 7)), reads=[wr] + hr, writes=[br])
                    rt, rr = rtmp.get()
                    rtv = rt[:].rearrange("p (a c) -> p a c", a=2)
                    P.add("act", lambda e, rtv=rtv, bt=bt: e.activation(
                        out=rtv[:, :, 0:T], in_=bt[:].rearrange("p (a c) -> p a c", a=2)[:, :, 0:T], func=AF.Relu),
                        writes=[br, rr])
                    eng = "dve"
                    sq_i[0] += 1
                    P.add(eng, lambda e, rtv=rtv, f0=f0: e.tensor_tensor(
                        out=hidT[:, f0:f0 + 2, 0:T], in0=rtv[:, :, 0:T], in1=rtv[:, :, 0:T], op=ALU.mult),
                        reads=[rr], writes=[r_hidT[f0], r_hidT[f0 + 1]])
                issue_wload()
                if hooks and piece in hooks:
                    for fn in hooks[piece]:
                        fn()


        def stage_I_piece(tl, piece):
            p = tl.xp
            wt_, wr = take_wload("dn", piece)
            wt = wt_[:].rearrange("p (k c) -> p k c", k=4)
            for s, (c0, n) in enumerate(tl.subs):
                for half in range(2):
                    ob = bank[4 + 2 * s + half]
                    for kc in range(4):
                        f = piece * 4 + kc
                        P.add("pe", lambda e, kc=kc, f=f: e.matmul(
                            ob[0:n, :], lhsT=hidT[:, f, c0:c0 + n], rhs=wt[:, kc, half * 512:(half + 1) * 512],
                            start=(piece == 0 and kc == 0), stop=(piece == 7 and kc == 3)),
                            reads=[wr, r_hidT[f]], writes=[r_bank[4 + 2 * s + half]])
            issue_wload()
            if piece == 7:
                for s, (c0, n) in enumerate(tl.subs):
                    for half in range(2):
                        ob = bank[4 + 2 * s + half]
                        P.add("dve", lambda e: e.tensor_tensor(
                            out=xh[p][s][0:n, half * 512:(half + 1) * 512], in0=ob[0:n, :],
                            in1=xh[p][s][0:n, half * 512:(half + 1) * 512], op=ALU.add),
                            writes=[r_bank[4 + 2 * s + half], r_xh[p][s]])


        def stage_J(tl):
            p = tl.xp
            for s, (c0, n) in enumerate(tl.subs):
                ln_t, ln_r = rms_stats(xh[p][s][0:n, :], n, D, [r_xh[p][s]])
                rs_t, rs_r = exp_of(ln_t, ln_r, n, -0.5)
                yt, yr = xh[p][s], r_xh[p][s]
                P.add("dve", lambda e, yt=yt, n=n, s=s, rs_t=rs_t: e.scalar_tensor_tensor(
                    out=yt[0:n, :], in0=xh[p][s][0:n, :], scalar=rs_t[0:n, :], in1=gfin[0:n, :], op0=ALU.mult,
                    op1=ALU.mult), reads=[r_xh[p][s], rs_r] + r_small, writes=[yr])
                dst = tl.ydst(s)
                out_ops.append(P.add("sp", lambda e, yt=yt, n=n, dst=dst: e.dma_start(out=dst, in_=yt[0:n, :]),
                                     reads=[yr], dma=True, chan=("yo", id(yr)), final=True))

        tiles = []
        mt = Tile()
        mt.kind = "meta"
        mt.par = 1
        mt.xp = 2
        mt.T = 16
        mt.L = 16
        mt.subs = [(0, 16)]
        mt.xsrc = lambda s: meta_d
        mt.kv_out = lambda s: (pmk_d, pmv_d)

        def meta_after():
            P.add("dve", lambda e: e.tensor_copy(out=ucx[0][:, :, 0:2], in_=ucx[1][:, :, 16:18]),
                  reads=r_ucx[1], writes=[r_ucctx[0]])
        mt.after_conv = meta_after
        tiles.append(mt)

        for t in range(NTP):
            tl = Tile()
            tl.kind = "prompt"
            tl.t = t
            tl.par = t % 2
            tl.xp = t % 3
            tl.T = MT
            tl.L = MT
            tl.subs = [(0, 128), (128, 128)]
            tl.xsrc = lambda s, t=t: xp_d[t * MT + s * 128: t * MT + (s + 1) * 128, :]
            tl.ydst = lambda s, t=t: yp_d[t * MT + s * 128: t * MT + (s + 1) * 128, :]
            if t == NTP - 1:
                tl.kv_out = lambda s: (pk_d, pv_d) if s == 1 else None
            else:
                tl.kv_out = lambda s: None

            def after(t=t, tl=tl):
                p = tl.par
                if t == NTP - 1:
                    out_ops.append(P.add("sp", lambda e: e.dma_start(out=pconv_d, in_=ucx[p][:, :, MT:MT + 2]),
                                         reads=r_ucx[p], dma=True, chan="pconv", final=True))
                else:
                    P.add("dve", lambda e: e.tensor_copy(out=ucx[1 - p][:, :, 0:2], in_=ucx[p][:, :, MT:MT + 2]),
                          reads=r_ucx[p], writes=[r_ucctx[1 - p]])
            tl.after_conv = after

            def groups(s, h, t=t):
                bi = 2 * t + s
                hp = h % 2
                g = h // 4
                gl = []
                gl.append(dict(kk=KKm[hp * 64:(hp + 1) * 64, g, 0:17], kk_rs=[r_KKm], v=Vm[0:17, g, :], v_rs=[r_Vm], nk=17,
                               hank=(Hm0[0:17, h, :] if bi == 0 else None), anti=anti17[:],
                               cb=(CBm0[0:17, h:h + 1] if bi == 0 else CBm[0:17, h:h + 1]), slot=2))
                if bi >= 1:
                    sl = (bi - 1) % NRING
                    gl.append(dict(kk=KKr[hp * 64:(hp + 1) * 64, sl, g, :], kk_rs=[r_KKr[sl]], v=Vr[:, sl, g, :],
                                   v_rs=[r_Vr[sl]], nk=128, hank=Hprev[:, h, :], anti=anti128[:], cb=None, slot=0))
                sl = bi % NRING
                gl.append(dict(kk=KKr[hp * 64:(hp + 1) * 64, sl, g, :], kk_rs=[r_KKr[sl]], v=Vr[:, sl, g, :],
                               v_rs=[r_Vr[sl]], nk=128, hank=Hcur[:, h, :], anti=anti128[:], cb=None, slot=1))
                return gl
            tl.groups = groups
            tiles.append(tl)

        stl = Tile()
        stl.kind = "sample"
        stl.par = NTP % 2
        stl.xp = NTP % 3
        stl.T = 64
        stl.L = 32
        stl.subs = [(0, 32), (32, 32)]
        stl.xsrc = lambda s: xs_d[32 * s:32 * (s + 1), :]
        stl.ydst = lambda s: ys_d[32 * s:32 * (s + 1), :]
        stl.kv_out = lambda s: (sk_d[32 * s:32 * (s + 1), :], sv_d[32 * s:32 * (s + 1), :])

        def sample_after():
            p = stl.par
            v = ucx[p][:, :, 0:68].rearrange("p j (a w) -> p j a w", a=2)
            for j in range(4):
                out_ops.append(P.add("sp", lambda e, j=j: e.dma_start(out=sconvo_d[:, j, :, :], in_=v[:, j, :, 32:34]),
                                     reads=r_ucx[p], dma=True, chan="sconvo", final=True))
        stl.after_conv = sample_after

        def sgroups(s, h):
            hp = h % 2
            g = h // 4
            return [
                dict(kk=KKsm[hp * 64:(hp + 1) * 64, s, g, 0:17], kk_rs=[r_scache], v=Vsm[0:17, s, g, :], v_rs=[r_scache],
                     nk=17, hank=None, anti=None, cb=CBm[0:17, h:h + 1], slot=2),
                dict(kk=KKsw[hp * 64:(hp + 1) * 64, s, g, :], kk_rs=[r_scache], v=Vsw[:, s, g, :], v_rs=[r_scache],
                     nk=128, hank=Hsw[:, h, :], anti=anti128[:], cb=None, slot=0),
                dict(kk=KKsn[hp * 64:(hp + 1) * 64, g, 32 * s:32 * (s + 1)], kk_rs=[r_KKsn], v=Vsn[0:32, s, g, :],
                     v_rs=[r_Vsn[s]], nk=32, hank=Hsn[0:32, h, :], anti=anti32[:], cb=None, slot=1),
            ]
        stl.groups = sgroups
        tiles.append(stl)

        def sample_ctx_load():
            p = stl.par
            v = ucx[p][:, :, 0:68].rearrange("p j (a w) -> p j a w", a=2)
            for j in range(4):
                P.add("sp", lambda e, j=j: e.dma_start(out=v[:, j, :, 0:2], in_=sconv_d[:, j, :, :]),
                      writes=[r_ucctx[p]] + r_ucx[p], dma=True, chan="sctx")

        def sample_cache_prep():
            p = stl.par
            for i, src in enumerate((ck_d, cv_d)):
                for sq in range(2):
                    P.add("sp", lambda e, i=i, src=src, sq=sq: e.dma_start(out=cstv[sq][:, i, :], in_=src[sq]),
                          writes=[r_cstv[sq]], dma=True, chan=("cst", sq))
            for i, src in enumerate((cmk_d, cmv_d)):
                for sq in range(2):
                    P.add("sp", lambda e, i=i, src=src, sq=sq: e.dma_start(out=cstv[sq][0:16, 2 + i, :], in_=src[sq]),
                          writes=[r_cstv[sq]], dma=True, chan=("cst", sq))
            for sq in range(2):
                P.add("dve", lambda e, sq=sq: e.tensor_copy(out=Vsw[:, sq, :, 0:64],
                                                            in_=cstv[sq][:, 1, :].rearrange("k (g c) -> k g c", g=2)),
                      reads=[r_cstv[sq]], writes=[r_scache])
                P.add("dve", lambda e, sq=sq: e.tensor_copy(out=Vsm[0:16, sq, :, 0:64],
                                                            in_=cstv[sq][0:16, 3, :].rearrange("k (g c) -> k g c", g=2)),
                      reads=[r_cstv[sq]], writes=[r_scache])
                for cp in range(2):
                    P.add("dve", lambda e, cp=cp, sq=sq: e.tensor_copy(
                        out=kdup[:, sq, :, cp, :], in_=cstv[sq][:, 0, :].rearrange("k (g c) -> k g c", g=2)),
                        reads=[r_cstv[sq]], writes=[r_kdup])
                    P.add("dve", lambda e, cp=cp, sq=sq: e.tensor_copy(
                        out=kmdup[:, sq, :, cp, :], in_=cstv[sq][0:16, 2, :].rearrange("k (g c) -> k g c", g=2)),
                        reads=[r_cstv[sq]], writes=[r_kmdup])
            for s in range(2):
                for g in range(2):
                    P.add("pe", lambda e, s=s, g=g: e.transpose(
                        out=TRb[:, 0:128], in_=kdup[:, s, g, :, :].rearrange("k a c -> k (a c)"), identity=ident[:]),
                        reads=[r_kdup, r_const], writes=[r_TR])
                    P.add("dve", lambda e, s=s, g=g: e.tensor_copy(out=KKsw[:, s, g, :], in_=TRb[:, 0:128]),
                          writes=[r_TR, r_scache])
                    P.add("pe", lambda e, s=s, g=g: e.transpose(
                        out=TRb[:, 0:16], in_=kmdup[:, s, g, :, :].rearrange("k a c -> k (a c)"), identity=ident[0:16, 0:16]),
                        reads=[r_kmdup, r_const], writes=[r_TR])
                    P.add("dve", lambda e, s=s, g=g: e.tensor_copy(out=KKsm[:, s, g, 0:16], in_=TRb[:, 0:16]),
                          writes=[r_TR, r_scache])

        prep_weights()
        stage_A(tiles[0])
        stage_B(tiles[0])
        prep_mlp_weights()
        stage_A(tiles[1])
        sample_cache_prep()
        for _ in range(4):
            issue_wload()
        full = tiles[1:]
        prev = None
        stage_B(full[0])
        for i, tl in enumerate(full):
            nsub = len(tl.subs)
            tl.e1 = [None] * nsub
            tl.e2 = [None] * nsub
            tl.g1 = [None] * nsub
            nx = full[i + 1] if i + 1 < len(full) else None
            if nx is not None:
                nx.a1 = [None] * len(nx.subs)
                for s in range(len(nx.subs)):
                    stage_A0(nx, s)
            for s in range(nsub):
                stage_D(tl, s)
            seq = []
            if nx is not None:
                seq += [("n", stage_A1, 0), ("n", stage_A1, 1)]
            seq += [("m", stage_E2a, 0), ("m", stage_E2a, 1), ("i",), ("m", stage_E2b, 0), ("i",), ("m", stage_F, 0),
                    ("m", stage_E2b, 1)]
            if nx is not None:
                seq += [("n", stage_A2, 0), ("i",), ("n", stage_A2, 1)]
            else:
                seq += [("i",)]
            seq += [("m", stage_G1, 0), ("m", stage_F, 1), ("i",), ("m", stage_G2, 0),
                    ("m", stage_G1, 1), ("i",), ("m", stage_G2, 1), ("i",), ("i",), ("i",)]
            piece = 0
            for it in seq:
                if it[0] == "m":
                    it[1](tl, it[2])
                elif it[0] == "n":
                    it[1](nx, it[2])
                elif prev is not None:
                    stage_I_piece(prev, piece)
                    piece += 1
            if prev is not None:
                assert piece == 8
                stage_J(prev)
            hooks = None
            if nx is not None:
                bgen = stage_B_gen(nx)

                def step(n, bgen=bgen):
                    def f():
                        for _ in range(n):
                            try:
                                next(bgen)
                            except StopIteration:
                                pass
                    return f
                hooks = {0: [step(1)], 1: [step(1)], 2: [step(2)], 3: [step(1)], 4: [step(2)], 5: [step(1)],
                         6: [step(2)], 7: [step(20)]}
            stage_H(tl, hooks)
            prev = tl
        for piece in range(8):
            stage_I_piece(prev, piece)
        stage_J(prev)

        P.emit(nc)
    return nc, P


_CACHE = {}


def _get_nc(S_TOK):
    if S_TOK not in _CACHE:
        _CACHE[S_TOK] = build(S_TOK)
    return _CACHE[S_TOK][0]


def kernel(x_prompt, x_sample, cache_k, cache_v, cache_meta_k, cache_meta_v, state_conv, meta_tokens,
           norm_mix, w_in, conv_w, attn_sinks, rel_bias_table, norm_conv_out, norm_attn_out, w_out,
           norm_mlp, w_up, w_down, norm_final):
    f = lambda a: np.ascontiguousarray(np.asarray(a, dtype=np.float32))
    x_prompt = f(x_prompt)
    x_sample = f(x_sample)
    B, S_TOK, _ = x_prompt.shape
    ncores = 8
    assert B == ncores
    nc = _get_nc(S_TOK)
    w_in0 = f(w_in)[0]
    k0 = w_in0[:, 2048:2112]
    k1 = w_in0[:, 2112:2176]
    win_x = np.ascontiguousarray(np.concatenate([w_in0, k0, k0, k1, k1], axis=1))
    pk = lambda v: np.ascontiguousarray(v.reshape(-1, 128).T)
    gout = np.concatenate([f(norm_conv_out)[0], f(norm_attn_out)[0]])
    convw = np.ascontiguousarray(f(conv_w)[0].reshape(3, 4, 128).transpose(2, 1, 0))
    common = {
        "meta": f(meta_tokens), "gmix": f(norm_mix).reshape(1, D), "gmlp": f(norm_mlp).reshape(1, D),
        "gcv": pk(f(norm_conv_out)[0]), "gatt": f(norm_attn_out).reshape(1, Q_DIM),
        "gfin": f(norm_final).reshape(1, D), "convw": convw, "sinks": f(attn_sinks).reshape(1, 8),
        "table": f(rel_bias_table), "oh": _onehot_const(), "win": win_x, "wout": f(w_out)[0],
        "wup": f(w_up)[0], "wdn": f(w_down)[0],
    }
    ck = f(cache_k)[0].reshape(16, 128, 128)
    cv = f(cache_v)[0].reshape(16, 128, 128)
    cmk = f(cache_meta_k)[0].reshape(16, 16, 128)
    cmv = f(cache_meta_v)[0].reshape(16, 16, 128)
    sc = f(state_conv)[0]
    in_maps = []
    for c in range(ncores):
        m = dict(common)
        m["xp"] = x_prompt[c]
        m["xs"] = np.ascontiguousarray(x_sample[2 * c:2 * c + 2].reshape(64, D))
        m["ck"] = np.ascontiguousarray(ck[2 * c:2 * c + 2])
        m["cv"] = np.ascontiguousarray(cv[2 * c:2 * c + 2])
        m["cmk"] = np.ascontiguousarray(cmk[2 * c:2 * c + 2])
        m["cmv"] = np.ascontiguousarray(cmv[2 * c:2 * c + 2])
        m["sconv"] = np.ascontiguousarray(sc[2 * c:2 * c + 2].reshape(2, 2, 4, 128).transpose(3, 2, 0, 1))
        in_maps.append(m)
    res = run_bass_kernel_spmd(nc, in_maps, core_ids=list(range(ncores)))
    R = res.results
    y_prompt = np.stack([R[c]["yp"] for c in range(ncores)]).astype(np.float32)
    y_sample = np.concatenate([R[c]["ys"].reshape(2, 32, D) for c in range(ncores)]).astype(np.float32)
    p_k = np.stack([R[c]["pk"].reshape(128, 2, 64) for c in range(ncores)])[None].astype(np.float32)
    p_v = np.stack([R[c]["pv"].reshape(128, 2, 64) for c in range(ncores)])[None].astype(np.float32)
    p_mk = np.stack([R[c]["pmk"].reshape(16, 2, 64) for c in range(ncores)])[None].astype(np.float32)
    p_mv = np.stack([R[c]["pmv"].reshape(16, 2, 64) for c in range(ncores)])[None].astype(np.float32)
    p_conv = np.stack([R[c]["pconv"].transpose(2, 1, 0).reshape(2, 512) for c in range(ncores)])[None].astype(np.float32)
    s_k = np.concatenate([R[c]["sk"].reshape(2, 32, 2, 64) for c in range(ncores)])[None].astype(np.float32)
    s_v = np.concatenate([R[c]["sv"].reshape(2, 32, 2, 64) for c in range(ncores)])[None].astype(np.float32)
    s_conv = np.concatenate([R[c]["sconvo"].transpose(2, 3, 1, 0).reshape(2, 2, 512) for c in range(ncores)])[None].astype(np.float32)
    return (y_prompt, y_sample, p_k, p_v, p_mk, p_mv, p_conv, s_k, s_v, s_conv)
```
